# Optimizing a Trainium2 kernel written in Bass

```python
import math
import jax, jax.numpy as jnp
from jax import lax
import numpy as np

D_MODEL = 4096
BATCH = 4
SEQ = 2048
DEPTH = 4
DEC_BATCH = 8
DEC_SEQ = 8
PAST_LEN = 8192
PAGE_SIZE = 128

N_EVEN = (DEPTH + 1) // 2
N_ODD = DEPTH // 2
D_A = D_MODEL // 2
S5_GRP = 16
G_A = D_A // S5_GRP
S5_P = 64
D_B = D_MODEL // 2
POOL_WINDOWS = (2, 4, 8, 16)
G_B = len(POOL_WINDOWS)
C_B = D_B // G_B
POOL_BUF = max(POOL_WINDOWS) - 1
HEAD_DIM = 128
N_HEADS_C = (D_MODEL // 2) // HEAD_DIM
D_C = N_HEADS_C * HEAD_DIM
BRANCHES = ((128, 1), (512, 4), (2048, 16))
MAX_WINDOW = 2048
ATT_BLOCK = 128
ATT_SCALE = HEAD_DIM ** -0.5
D_D = D_MODEL // 2
G_D = 8
C_D = D_D // G_D
CHUNK = 128
D_FF = ((8 * D_MODEL + 3 * 256 - 1) // (3 * 256)) * 256
RMS_EPS = 1e-6
LN_EPS = 1e-5
NEG_INF = -1e30

kernel_name = 'hybrid_s5_pool_dilattn_sgu_decode_step'


def _rmsnorm(x, g):
    xf = x.astype(jnp.float32)
    y = xf * lax.rsqrt(jnp.mean(xf * xf, axis=-1, keepdims=True) + RMS_EPS)
    return (y * g.astype(jnp.float32)).astype(x.dtype)


def _layernorm(x, g, b):
    xf = x.astype(jnp.float32)
    mu = jnp.mean(xf, axis=-1, keepdims=True)
    xc = xf - mu
    y = xc * lax.rsqrt(jnp.mean(xc * xc, axis=-1, keepdims=True) + LN_EPS)
    return y * g.astype(jnp.float32) + b.astype(jnp.float32)


def _linrec_combine(e1, e2):
    a1, b1 = e1
    a2, b2 = e2
    return a1 * a2, a2 * b1 + b2


def _s5(u, h0_re, h0_im, lam_re, lam_im, log_dt, b_re, b_im, c_re, c_im, d_skip, w_glu, b_glu):
    f32 = jnp.float32
    bsz, L, _ = u.shape
    uf = u.astype(f32)
    lam = lax.complex(lam_re.astype(f32), lam_im.astype(f32))
    dt = jnp.exp(log_dt.astype(f32))[:, None]
    lam_bar = jnp.exp(lam * dt)
    b_bar = ((lam_bar - 1.0) / lam)[:, :, None] * lax.complex(b_re.astype(f32), b_im.astype(f32))
    bu = jnp.einsum('blgh,gph->blgp', uf.reshape(bsz, L, G_A, S5_GRP).astype(jnp.complex64), b_bar)
    h0 = lax.complex(h0_re.astype(f32), h0_im.astype(f32))
    bu = bu.at[:, 0].add(lam_bar * h0)
    a = jnp.broadcast_to(lam_bar, (1, L, G_A, S5_P))
    _, h = lax.associative_scan(_linrec_combine, (a, bu), axis=1)
    c = lax.complex(c_re.astype(f32), c_im.astype(f32))
    y = jnp.einsum('blgp,ghp->blgh', h, c).real.reshape(bsz, L, D_A) + d_skip.astype(f32) * uf
    y = jax.nn.gelu(y, approximate=False)
    y = y * jax.nn.sigmoid(y @ w_glu.astype(f32) + b_glu.astype(f32))
    h_last = h[:, -1]
    return y.astype(u.dtype), h_last.real.astype(u.dtype), h_last.imag.astype(u.dtype)


def _pool(u, buf, start_pos, w_lin, scale):
    f32 = jnp.float32
    bsz, L, _ = u.shape
    ext = jnp.concatenate([buf.astype(f32), u.astype(f32)], axis=1)
    cs = jnp.concatenate([jnp.zeros((bsz, 1, D_B), f32), jnp.cumsum(ext, axis=1)], axis=1)
    pos = start_pos + jnp.arange(L, dtype=jnp.int32)
    hi = cs[:, POOL_BUF + 1:]
    groups = []
    for g, w in enumerate(POOL_WINDOWS):
        sl = slice(g * C_B, (g + 1) * C_B)
        lo = cs[:, POOL_BUF + 1 - w:POOL_BUF + 1 - w + L, sl]
        cnt = jnp.minimum(pos + 1, w).astype(f32)[None, :, None]
        groups.append((hi[:, :, sl] - lo) / cnt - ext[:, POOL_BUF:, sl])
    z = jnp.stack(groups, axis=2)
    y = jnp.einsum('blgc,gcd->blgd', z, w_lin.astype(f32)).reshape(bsz, L, D_B) * scale.astype(f32)
    return y.astype(u.dtype), ext[:, -POOL_BUF:].astype(u.dtype)


def _branch_prompt(q, k, v, window, dilation):
    bsz, L, H, E = q.shape
    n_prev = window // dilation
    nback = -(-n_prev // ATT_BLOCK)
    span = dilation * ATT_BLOCK
    Lp = -(-L // span) * span
    nb = Lp // span

    def to_sub(t):
        t = jnp.pad(t, ((0, 0), (0, Lp - L), (0, 0), (0, 0))).reshape(bsz, Lp // dilation, dilation, H, E)
        return t.transpose(0, 2, 1, 3, 4).reshape(bsz, dilation, nb, ATT_BLOCK, H, E)

    def band(t):
        tp = jnp.pad(t, ((0, 0), (0, 0), (nback, 0), (0, 0), (0, 0), (0, 0)))
        return jnp.concatenate([tp[:, :, j:j + nb] for j in range(nback + 1)], axis=3)

    qs = to_sub(q)
    kb = band(to_sub(k))
    vb = band(to_sub(v))
    s = jnp.einsum('brnqhe,brnkhe->brnhqk', qs, kb) * ATT_SCALE
    kj = jnp.arange((nback + 1) * ATT_BLOCK)
    dist = nback * ATT_BLOCK + jnp.arange(ATT_BLOCK)[:, None] - kj[None, :]
    kidx = (jnp.arange(nb)[:, None] - nback) * ATT_BLOCK + kj[None, :]
    valid = ((dist >= 0) & (dist <= n_prev))[None] & (kidx >= 0)[:, None, :]
    s = jnp.where(valid[None, None, :, None], s, NEG_INF)
    m = jnp.max(s, axis=-1, keepdims=True)
    p = jnp.exp(s - m)
    l = jnp.sum(p, axis=-1, keepdims=True)
    o = jnp.einsum('brnhqk,brnkhe->brnqhe', p / l, vb)
    o = o.reshape(bsz, dilation, Lp // dilation, H, E).transpose(0, 2, 1, 3, 4).reshape(bsz, Lp, H, E)[:, :L]
    lse = jnp.moveaxis((m + jnp.log(l))[..., 0], 3, 4)
    lse = lse.reshape(bsz, dilation, Lp // dilation, H).transpose(0, 2, 1, 3).reshape(bsz, Lp, H)[:, :L]
    return o, lse


def _branch_sample(q, k_ext, v_ext, n_buf, window, dilation):
    S = q.shape[1]
    n_prev = window // dilation
    idx = n_buf + jnp.arange(S)[:, None] - dilation * jnp.arange(n_prev + 1)[None, :]
    valid = idx >= 0
    idx = jnp.maximum(idx, 0)
    kg = k_ext[:, idx]
    vg = v_ext[:, idx]
    s = jnp.einsum('bshe,bskhe->bhsk', q, kg) * ATT_SCALE
    s = jnp.where(valid[None, None], s, NEG_INF)
    m = jnp.max(s, axis=-1, keepdims=True)
    p = jnp.exp(s - m)
    l = jnp.sum(p, axis=-1, keepdims=True)
    o = jnp.einsum('bhsk,bskhe->bshe', p / l, vg)
    lse = jnp.moveaxis((m + jnp.log(l))[..., 0], 1, 2)
    return o, lse


def _combine_branches(outs, lses):
    wts = jax.nn.softmax(jnp.stack(lses, axis=0), axis=0)
    return jnp.sum(wts[..., None] * jnp.stack(outs, axis=0), axis=0)


def _sgu(gu, gv, ln_g, ln_b, w_s, b_s):
    f32 = jnp.float32
    bsz, L, _ = gu.shape
    T = min(L, CHUNK)
    vn = _layernorm(gv, ln_g, ln_b)
    mask = jnp.tril(jnp.ones((T, T), dtype=bool))
    w = jnp.where(mask[None], w_s[:, :T, :T].astype(f32), 0.0)
    vr = vn.reshape(bsz, L // T, T, G_D, C_D)
    mixed = jnp.einsum('gts,bnsgc->bntgc', w, vr) + b_s[:, :T].astype(f32).T[None, None, :, :, None]
    out = gu.astype(f32) * mixed.reshape(bsz, L, D_D)
    return out.astype(gu.dtype), vn.astype(gu.dtype)


def _even_layer(x, h0_re, h0_im, pool_buf, start_pos, norm_g, w_in, w_out, s5_params, pool_w, pool_scale):
    z = _rmsnorm(x, norm_g) @ w_in
    ya, hr, hi = _s5(z[..., :D_A], h0_re, h0_im, *s5_params)
    yb, buf = _pool(z[..., D_A:], pool_buf, start_pos, pool_w, pool_scale)
    return x + jnp.concatenate([ya, yb], axis=-1) @ w_out, hr, hi, buf


def _odd_layer(x, k_buf, v_buf, norm_g, w_in, w_out, qn, kn, ln_g, ln_b, w_s, b_s):
    f32 = jnp.float32
    bsz, L, _ = x.shape
    z = _rmsnorm(x, norm_g) @ w_in
    q, k, v, gu, gv = jnp.split(z, [D_C, 2 * D_C, 3 * D_C, 3 * D_C + D_D], axis=-1)
    q = _rmsnorm(q.reshape(bsz, L, N_HEADS_C, HEAD_DIM), qn).astype(f32)
    k = _rmsnorm(k.reshape(bsz, L, N_HEADS_C, HEAD_DIM), kn).astype(f32)
    v = v.reshape(bsz, L, N_HEADS_C, HEAD_DIM).astype(f32)
    outs, lses = [], []
    if k_buf is None:
        for w, d in BRANCHES:
            o, lse = _branch_prompt(q, k, v, w, d)
            outs.append(o)
            lses.append(lse)
        n_keep = min(MAX_WINDOW, L)
        new_k, new_v = k[:, L - n_keep:], v[:, L - n_keep:]
    else:
        n_buf = k_buf.shape[1]
        k_ext = jnp.concatenate([k_buf.astype(f32), k], axis=1)
        v_ext = jnp.concatenate([v_buf.astype(f32), v], axis=1)
        for w, d in BRANCHES:
            o, lse = _branch_sample(q, k_ext, v_ext, n_buf, w, d)
            outs.append(o)
            lses.append(lse)
        new_k, new_v = k, v
    att = _combine_branches(outs, lses).reshape(bsz, L, D_C).astype(x.dtype)
    sg, v_rows = _sgu(jax.nn.gelu(gu, approximate=False), jax.nn.gelu(gv, approximate=False), ln_g, ln_b, w_s, b_s)
    y = jnp.concatenate([att, sg], axis=-1) @ w_out
    return x + y, new_k.astype(x.dtype), new_v.astype(x.dtype), v_rows


def _swiglu(x, g, w1, w3, w2):
    h = _rmsnorm(x, g)
    return x + (jax.nn.silu(h @ w1) * (h @ w3)) @ w2


def setup_inputs(seed: int = 0) -> dict:
    key = jax.random.key(seed)
    keys = iter(jax.random.split(key, 48))
    f32 = jnp.float32

    def nrm(shape, scale):
        return jax.random.normal(next(keys), shape, f32) * scale

    w_buf = min(MAX_WINDOW, PAST_LEN)
    lam_im0 = jnp.pi * jnp.arange(S5_P, dtype=f32)
    return {
        'x_prompt': nrm((BATCH, SEQ, D_MODEL), 1.0),
        'x_sample': nrm((DEC_BATCH, DEC_SEQ, D_MODEL), 1.0),
        'state_s5_re': nrm((N_EVEN, DEC_BATCH, G_A, S5_P), 0.1),
        'state_s5_im': nrm((N_EVEN, DEC_BATCH, G_A, S5_P), 0.1),
        'state_pool': nrm((N_EVEN, DEC_BATCH, POOL_BUF, D_B), 1.0),
        'cache_k': nrm((N_ODD, DEC_BATCH, w_buf, N_HEADS_C, HEAD_DIM), 1.0),
        'cache_v': nrm((N_ODD, DEC_BATCH, w_buf, N_HEADS_C, HEAD_DIM), 1.0),
        'norm_mix': 1.0 + nrm((DEPTH, D_MODEL), 0.02),
        'norm_ffn': 1.0 + nrm((DEPTH, D_MODEL), 0.02),
        'ev_w_in': nrm((N_EVEN, D_MODEL, D_A + D_B), D_MODEL ** -0.5),
        'ev_w_out': nrm((N_EVEN, D_A + D_B, D_MODEL), (D_A + D_B) ** -0.5),
        's5_lambda_re': -0.5 + nrm((N_EVEN, G_A, S5_P), 0.01),
        's5_lambda_im': jnp.broadcast_to(lam_im0, (N_EVEN, G_A, S5_P)) + nrm((N_EVEN, G_A, S5_P), 0.01),
        's5_log_dt': jax.random.uniform(next(keys), (N_EVEN, G_A), f32, minval=math.log(1e-3), maxval=math.log(1e-1)),
        's5_b_re': nrm((N_EVEN, G_A, S5_P, S5_GRP), (2 * S5_GRP) ** -0.5),
        's5_b_im': nrm((N_EVEN, G_A, S5_P, S5_GRP), (2 * S5_GRP) ** -0.5),
        's5_c_re': nrm((N_EVEN, G_A, S5_GRP, S5_P), 1.0),
        's5_c_im': nrm((N_EVEN, G_A, S5_GRP, S5_P), 1.0),
        's5_d': nrm((N_EVEN, D_A), 1.0),
        's5_w_glu': nrm((N_EVEN, D_A, D_A), D_A ** -0.5),
        's5_b_glu': nrm((N_EVEN, D_A), 0.02),
        'pool_w': nrm((N_EVEN, G_B, C_B, C_B), C_B ** -0.5),
        'pool_scale': 1.0 + nrm((N_EVEN, D_B), 0.02),
        'od_w_in': nrm((N_ODD, D_MODEL, 3 * D_C + 2 * D_D), D_MODEL ** -0.5),
        'od_w_out': nrm((N_ODD, D_C + D_D, D_MODEL), (D_C + D_D) ** -0.5),
        'q_norm': 1.0 + nrm((N_ODD, HEAD_DIM), 0.02),
        'k_norm': 1.0 + nrm((N_ODD, HEAD_DIM), 0.02),
        'sgu_ln_g': 1.0 + nrm((N_ODD, D_D), 0.02),
        'sgu_ln_b': nrm((N_ODD, D_D), 0.02),
        'sgu_w': nrm((N_ODD, G_D, CHUNK, CHUNK), CHUNK ** -0.5),
        'sgu_b': 1.0 + nrm((N_ODD, G_D, CHUNK), 0.02),
        'ffn_w1': nrm((DEPTH, D_MODEL, D_FF), D_MODEL ** -0.5),
        'ffn_w3': nrm((DEPTH, D_MODEL, D_FF), D_MODEL ** -0.5),
        'ffn_w2': nrm((DEPTH, D_FF, D_MODEL), D_FF ** -0.5),
    }


def reference(x_prompt, x_sample, state_s5_re, state_s5_im, state_pool, cache_k, cache_v,
              norm_mix, norm_ffn, ev_w_in, ev_w_out, s5_lambda_re, s5_lambda_im, s5_log_dt,
              s5_b_re, s5_b_im, s5_c_re, s5_c_im, s5_d, s5_w_glu, s5_b_glu, pool_w, pool_scale,
              od_w_in, od_w_out, q_norm, k_norm, sgu_ln_g, sgu_ln_b, sgu_w, sgu_b,
              ffn_w1, ffn_w3, ffn_w2):
    xp, xs = x_prompt, x_sample
    bp = xp.shape[0]
    s5r_p, s5i_p, pool_p, k_p, v_p = [], [], [], [], []
    s5r_s, s5i_s, pool_s, k_s, v_s, sgu_s = [], [], [], [], [], []
    for l in range(DEPTH):
        i = l // 2
        if l % 2 == 0:
            s5_params = (s5_lambda_re[i], s5_lambda_im[i], s5_log_dt[i], s5_b_re[i], s5_b_im[i],
                         s5_c_re[i], s5_c_im[i], s5_d[i], s5_w_glu[i], s5_b_glu[i])
            zero_h = jnp.zeros((bp, G_A, S5_P), jnp.float32)
            zero_buf = jnp.zeros((bp, POOL_BUF, D_B), xp.dtype)
            xp, hr, hi, buf = _even_layer(xp, zero_h, zero_h, zero_buf, 0, norm_mix[l], ev_w_in[i], ev_w_out[i],
                                          s5_params, pool_w[i], pool_scale[i])
            s5r_p.append(hr)
            s5i_p.append(hi)
            pool_p.append(buf)
            xs, hr, hi, buf = _even_layer(xs, state_s5_re[i], state_s5_im[i], state_pool[i], PAST_LEN,
                                          norm_mix[l], ev_w_in[i], ev_w_out[i], s5_params, pool_w[i], pool_scale[i])
            s5r_s.append(hr)
            s5i_s.append(hi)
            pool_s.append(buf)
        else:
            xp, nk, nv, _ = _odd_layer(xp, None, None, norm_mix[l], od_w_in[i], od_w_out[i], q_norm[i], k_norm[i],
                                       sgu_ln_g[i], sgu_ln_b[i], sgu_w[i], sgu_b[i])
            k_p.append(nk)
            v_p.append(nv)
            xs, nk, nv, vrows = _odd_layer(xs, cache_k[i], cache_v[i], norm_mix[l], od_w_in[i], od_w_out[i],
                                           q_norm[i], k_norm[i], sgu_ln_g[i], sgu_ln_b[i], sgu_w[i], sgu_b[i])
            k_s.append(nk)
            v_s.append(nv)
            sgu_s.append(vrows)
        xp = _swiglu(xp, norm_ffn[l], ffn_w1[l], ffn_w3[l], ffn_w2[l])
        xs = _swiglu(xs, norm_ffn[l], ffn_w1[l], ffn_w3[l], ffn_w2[l])
    return (xp, xs,
            jnp.stack(s5r_p), jnp.stack(s5i_p), jnp.stack(pool_p), jnp.stack(k_p), jnp.stack(v_p),
            jnp.stack(s5r_s), jnp.stack(s5i_s), jnp.stack(pool_s), jnp.stack(k_s), jnp.stack(v_s),
            jnp.stack(sgu_s))
```

```python
import math
import numpy as np
import concourse.bass as bass
import concourse.mybir as mybir
from concourse.bass_utils import run_bass_kernel_spmd

F32 = mybir.dt.float32
BF16 = mybir.dt.bfloat16
ALU = mybir.AluOpType
AF = mybir.ActivationFunctionType
AX = mybir.AxisListType
SEM_LIMIT = 30000
NSLOTS = 6
SB_BASE = 16512
SB_TOP = 229344

D = 4096
L = 2048
S = 8
T = L + S
DFF = 11008
NJ = DFF // 128
SUBS = [(0, 512), (512, 1024), (1024, 1536), (1536, 2048), (2048, 2056)]
ATT_SCALE = 128 ** -0.5
EXP_SHIFT = -8.0
ATT_HEADS = 16
ATT_Q = "sp"
ATT_T4 = 4
ATT_PARTS = ('norm', 'tr', 'trdma', 'trs', 'prompt', 'sample')
PI = math.pi


class Eng:
    def __init__(self, kb, name):
        self.kb = kb
        self.name = name
        self.epoch = 0
        self.sems = [kb.nc.alloc_semaphore(f"s_{name}_0")]
        self.n = 0
        self.waited = {}

    def bump(self, inc):
        if self.n + inc > SEM_LIMIT:
            self.epoch += 1
            self.sems.append(self.kb.nc.alloc_semaphore(f"s_{self.name}_{self.epoch}"))
            self.n = 0
        self.n += inc
        return (self, self.epoch, self.n)


class Buf:
    __slots__ = ("w", "r")

    def __init__(self):
        self.w = None
        self.r = {}


class Tile:
    def __init__(self, t):
        self.t = t
        self.b = Buf()

    def __getitem__(self, k):
        return self.t[k]


class KB:
    def __init__(self, nc):
        self.nc = nc
        self.ops = {k: [] for k in ("pe", "act", "dve", "pool", "sp")}
        self.engs = {k: Eng(self, k) for k in ("pe", "act", "dve", "pool", "sp")}
        self.dslots = {}
        self.dslot_rr = {}
        self.sb_off = SB_BASE
        self.sb_cnt = 0
        self.bank_rr = 0

    def tile(self, shape, dtype, name="t"):
        esz = 2 if dtype == BF16 else 4
        per_part = int(np.prod(shape[1:])) * esz
        off = (self.sb_off + 63) // 64 * 64
        self.sb_off = off + per_part
        assert self.sb_off <= SB_TOP, f"sbuf overflow {self.sb_off} ({name})"
        self.sb_cnt += 1
        return Tile(self.nc.alloc_sbuf_tensor_at(f"{name}_{self.sb_cnt}", list(shape), dtype, offset=off))

    def sb_mark(self):
        return self.sb_off

    def sb_reset(self, mark=SB_BASE):
        self.sb_off = mark

    def _waits(self, E, reads, writes):
        deps = {}

        def add(d):
            if d is None:
                return
            F, ep, c = d
            if deps.get((F, ep), 0) < c:
                deps[(F, ep)] = c

        for b in reads:
            add(b.w)
        for b in writes:
            add(b.w)
            for d in b.r.values():
                add(d)
        waits = []
        for (F, ep), c in deps.items():
            if F is E and E.name == "pe":
                continue
            done = False
            for (F2, ep2), c2 in E.waited.items():
                if F2 is F and (ep2 > ep or (ep2 == ep and c2 >= c)):
                    done = True
                    break
            if done:
                continue
            E.waited[(F, ep)] = c
            waits.append((F.sems[ep], c))
        return waits

    def _mark(self, tok, reads, writes):
        E = tok[0]
        for b in reads:
            b.r[E] = tok
        for b in writes:
            b.w = tok
            b.r = {}

    def op(self, ename, fn, reads=(), writes=()):
        reads = [x.b if isinstance(x, Tile) else x for x in reads]
        writes = [x.b if isinstance(x, Tile) else x for x in writes]
        E = self.engs[ename]
        waits = self._waits(E, reads, writes)
        tok = E.bump(1)
        sem = E.sems[tok[1]]

        def rec(e, waits=waits, fn=fn, sem=sem):
            for s, v in waits:
                e.wait_ge(s, v)
            fn(e).then_inc(sem, 1)

        self.ops[ename].append(rec)
        self._mark(tok, reads, writes)
        return tok

    def dma(self, qname, out, in_, reads=(), writes=(), nslots=None, **kw):
        nslots = nslots or NSLOTS
        reads = [x.b if isinstance(x, Tile) else x for x in reads]
        writes = [x.b if isinstance(x, Tile) else x for x in writes]
        Q = self.engs[qname]
        rr = self.dslot_rr.get(qname, 0)
        self.dslot_rr[qname] = rr + 1
        key = (qname, rr % nslots)
        if key not in self.dslots:
            self.dslots[key] = Eng(self, f"d{qname}{rr % nslots}")
        Dm = self.dslots[key]
        waits = self._waits(Q, reads, writes)
        if Dm.n > 0:
            k = (Dm, Dm.epoch)
            if Q.waited.get(k, 0) < Dm.n:
                Q.waited[k] = Dm.n
                waits.append((Dm.sems[Dm.epoch], Dm.n))
        tok = Dm.bump(16)
        sem = Dm.sems[tok[1]]

        def rec(e, waits=waits, sem=sem, out=out, in_=in_, kw=kw):
            for s, v in waits:
                e.wait_ge(s, v)
            e.dma_start(out=out, in_=in_, **kw).then_inc(sem, 16)

        self.ops[qname].append(rec)
        self._mark(tok, reads, writes)
        return tok

    def barrier(self):
        allE = list(self.engs.values()) + list(self.dslots.values())
        for ename, E in self.engs.items():
            waits = []
            for F in allE:
                if F is E or (F.n == 0 and F.epoch == 0):
                    continue
                k = (F, F.epoch)
                if E.waited.get(k, 0) < F.n:
                    E.waited[k] = F.n
                    waits.append((F.sems[F.epoch], F.n))
            if waits:
                def rec(e, waits=waits):
                    for s, v in waits:
                        e.wait_ge(s, v)
                self.ops[ename].append(rec)

    def phase_end(self):
        self.barrier()
        self.sb_reset(self.persist_mark)

    def emit(self):
        self.barrier()
        with self.nc.Block() as block:
            @block.tensor
            def _(e):
                for f in self.ops["pe"]:
                    f(e)

            @block.scalar
            def _(e):
                for f in self.ops["act"]:
                    f(e)

            @block.vector
            def _(e):
                for f in self.ops["dve"]:
                    f(e)

            @block.gpsimd
            def _(e):
                for f in self.ops["pool"]:
                    f(e)

            @block.sync
            def _(e):
                for f in self.ops["sp"]:
                    f(e)


C_ID = 0
C_WM = 128
C_WS = C_WM + 2560
C_WSN = C_WS + 128
C_TRIL = C_WSN + 8
C_RM = C_TRIL + 128
C_FIX = C_RM + 8
NCONST = C_FIX + 60


def _mult(d):
    d = np.asarray(d)
    m = ((d >= 0) & (d <= 128)).astype(np.float32)
    m += ((d >= 0) & (d <= 512) & (d % 4 == 0)).astype(np.float32)
    m += ((d >= 0) & (d <= 2048) & (d % 16 == 0)).astype(np.float32)
    return m


def make_consts():
    c = np.zeros((128, NCONST), np.float32)
    c[:, C_ID:C_ID + 128] = np.eye(128, dtype=np.float32)
    i = np.arange(128)[:, None]
    x = np.arange(2560)[None, :]
    c[:, C_WM:C_WM + 2560] = _mult(x - 384 - i)
    kt = (np.arange(128) // 8)[None, :]
    s = (np.arange(128) % 8)[None, :]
    c[:, C_WS:C_WS + 128] = _mult(2048 + s - kt * 128 - i)
    sp = np.arange(8)[:, None]
    sq = np.arange(8)[None, :]
    c[0:8, C_WSN:C_WSN + 8] = _mult(sq - sp)
    c[:, C_TRIL:C_TRIL + 128] = (np.arange(128)[None, :] <= i).astype(np.float32)
    c[:, C_RM:C_RM + 8] = ((i // 16) == np.arange(8)[None, :]).astype(np.float32)
    for g, w in enumerate((2, 4, 8, 16)):
        t = np.arange(15)
        c[:, C_FIX + g * 15:C_FIX + (g + 1) * 15] = (1.0 / np.minimum(t + 1, w) - 1.0 / w)[None, :]
    return c


class _Lazy(dict):
    def __init__(self, mk):
        super().__init__()
        self.mk = mk
        self.shapes = {}

    def __missing__(self, k):
        v = self.mk(k, self.shapes[k])
        self[k] = v
        return v


class Prog:
    def __init__(self, depth=4, dbg=False):
        self.depth = depth
        self.dbg = dbg
        nc = self.nc = bass.Bass("TRN2", target_bir_lowering=False)
        kb = self.kb = KB(nc)
        ne, no = 2, 2

        def din(name, shape):
            return nc.dram_tensor(name, list(shape), F32, kind="ExternalInput").ap()

        def dout(name, shape):
            return nc.dram_tensor(name, list(shape), F32, kind="ExternalOutput").ap()

        self.i = _Lazy(din)
        self.i.shapes = dict(
            x_prompt=(L, D), x_sample=(S, D),
            state_s5_re=(ne, 64, 128), state_s5_im=(ne, 64, 128),
            state_pool=(ne, 15, 2048),
            cache_k=(no, 2048, 16, 128), cache_v=(no, 2048, 16, 128),
            norm_mix=(4, D), norm_ffn=(4, D),
            ev_w_in=(ne, D, D), ev_w_out=(ne, D, D),
            s5_lambda_re=(ne, 64, 128), s5_lambda_im=(ne, 64, 128),
            s5_log_dt=(ne, 64, 2),
            s5_b_re=(ne, 64, 128, 16), s5_b_im=(ne, 64, 128, 16),
            s5_c_re=(ne, 2048, 64), s5_c_im=(ne, 2048, 64),
            s5_d=(ne, 2048), s5_w_glu=(ne, 2048, 2048), s5_b_glu=(ne, 2048),
            pool_w=(ne, 4, 512, 512), pool_scale=(ne, 2048),
            od_w_in=(no, D, 10240), od_w_out=(no, D, D),
            q_norm=(no, 128), k_norm=(no, 128),
            sgu_ln_g=(no, 2048), sgu_ln_b=(no, 2048),
            sgu_w=(no, 8, 128, 128), sgu_b=(no, 8, 128),
            ffn_w1=(4, D, DFF), ffn_w3=(4, D, DFF), ffn_w2=(4, DFF, D),
            consts=(128, NCONST),
        )
        self.o = dict(
            y_prompt=dout("y_prompt", (L, D)), y_sample=dout("y_sample", (S, D)),
            s5_re_prompt=dout("s5_re_prompt", (ne, 64, 128)), s5_im_prompt=dout("s5_im_prompt", (ne, 64, 128)),
            pool_prompt=dout("pool_prompt", (ne, 15, 2048)),
            k_prompt=dout("k_prompt", (no, 2048, 16, 128)), v_prompt=dout("v_prompt", (no, 2048, 16, 128)),
            s5_re_sample=dout("s5_re_sample", (ne, 64, 128)), s5_im_sample=dout("s5_im_sample", (ne, 64, 128)),
            pool_sample=dout("pool_sample", (ne, 15, 2048)),
            k_sample=dout("k_sample", (no, 8, 16, 128)), v_sample=dout("v_sample", (no, 8, 16, 128)),
            sgu_v_sample=dout("sgu_v_sample", (no, 8, 2048)),
        )
        kind = "ExternalOutput" if dbg else "Internal"
        self.X = nc.dram_tensor("Xs", [D, T], F32, kind=kind).ap()
        self.Z = nc.dram_tensor("Zs", [10240, T], F32, kind=kind).ap()
        self.MIX = nc.dram_tensor("MIXs", [D, T], BF16, kind="Internal").ap()
        self.ACTB = nc.dram_tensor("ACTs", [DFF, T], BF16, kind="Internal").ap()
        self.YG = nc.dram_tensor("YGs", [2048, T], BF16, kind="Internal").ap()
        self.psum = [nc.alloc_psum_tensor(f"ps{i}", [128, 512], F32) for i in range(8)]
        self.psb = [Buf() for _ in range(8)]

        self.cst = kb.tile([128, NCONST], F32, "cst")
        kb.dma("sp", self.cst[:], self.i["consts"], writes=[self.cst])
        self.ident = self.cst.t[:, C_ID:C_ID + 128]
        self.identb = kb.tile([128, 128], BF16, "identb")
        kb.op("dve", lambda e: e.tensor_copy(out=self.identb[:], in_=self.ident), reads=[self.cst], writes=[self.identb])
        self.ones_b = kb.tile([128, 128], BF16, "ones_b")
        kb.op("dve", lambda e: e.memset(self.ones_b[:], 1.0), writes=[self.ones_b])
        self.ones_f = kb.tile([128, 128], F32, "ones_f")
        kb.op("dve", lambda e: e.memset(self.ones_f[:], 1.0), writes=[self.ones_f])
        kb.persist_mark = kb.sb_mark()
        kb.barrier()

    def bank(self):
        b = self.kb.bank_rr % 8
        self.kb.bank_rr += 1
        return b

    def load_cols(self, dram_vec, n, name="col"):
        kb = self.kb
        if n == 1:
            out = kb.tile([128, 1], F32, name)
            kb.dma("sp", out[:], dram_vec.rearrange("(p o) -> p o", o=1), writes=[out])
            return out
        raw = kb.tile([n, 128], F32, name + "r")
        kb.dma("sp", raw[:], dram_vec.rearrange("(k p) -> k p", p=128), writes=[raw])
        out = kb.tile([128, n], F32, name)
        bk = self.bank()
        ps = self.psum[bk]
        kb.op("pe", lambda e: e.transpose(out=ps[:, 0:n], in_=raw[:], identity=self.ident[0:n, 0:n]),
              reads=[raw, self.cst], writes=[self.psb[bk]])
        kb.op("dve", lambda e: e.tensor_copy(out=out[:], in_=ps[:, 0:n]), reads=[self.psb[bk]], writes=[out])
        return out

    def gemm(self, wview, KT, n_tiles, rhs_fn, rhs_bufs, epilogue, subs=SUBS, nw=3, wname="w"):
        kb = self.kb
        wts = [kb.tile([128, KT, 128], BF16, wname) for _ in range(nw)]
        for n in range(n_tiles):
            wt = wts[n % nw]
            kb.dma("pool", wt[:], wview(n), writes=[wt])
            for si, (c0, c1) in enumerate(subs):
                bk = self.bank()
                ps = self.psum[bk]

                def mm(e, wt=wt, c0=c0, c1=c1, ps=ps):
                    for kt in range(KT):
                        ins = e.matmul(out=ps[:, 0:c1 - c0], lhsT=wt[:, kt, :], rhs=rhs_fn(kt, c0, c1),
                                       start=(kt == 0), stop=(kt == KT - 1))
                    return ins

                kb.op("pe", mm, reads=[wt] + list(rhs_bufs), writes=[self.psb[bk]])
                epilogue(n, si, c0, c1, ps[:, 0:c1 - c0], self.psb[bk])

    def phase_in(self):
        kb = self.kb
        Xv = self.X.rearrange("(kt p) t -> p kt t", p=128)
        xin = [kb.tile([128, D], F32, "xin") for _ in range(2)]
        xo = [kb.tile([128, 32, 128], F32, "xo") for _ in range(2)]
        for tt in range(16):
            xi = xin[tt % 2]
            xt = xo[tt % 2]
            kb.dma("sp", xi[:], self.i["x_prompt"][tt * 128:(tt + 1) * 128, :], writes=[xi])
            for k4 in range(8):
                bk = self.bank()
                ps = self.psum[bk]

                def tr(e, xi=xi, ps=ps, k4=k4):
                    for q in range(4):
                        kt = k4 * 4 + q
                        ins = e.transpose(out=ps[:, q * 128:(q + 1) * 128], in_=xi[:, kt * 128:(kt + 1) * 128], identity=self.ident)
                    return ins
                kb.op("pe", tr, reads=[xi, self.cst], writes=[self.psb[bk]])
                eng = "dve" if k4 % 2 == 0 else "act"
                if eng == "dve":
                    kb.op("dve", lambda e, xt=xt, ps=ps, k4=k4: e.tensor_copy(out=xt[:, k4 * 4:(k4 + 1) * 4, :], in_=ps[:].rearrange("p (a b) -> p a b", b=128)),
                          reads=[self.psb[bk]], writes=[xt])
                else:
                    kb.op("act", lambda e, xt=xt, ps=ps, k4=k4: e.copy(out=xt[:, k4 * 4:(k4 + 1) * 4, :], in_=ps[:].rearrange("p (a b) -> p a b", b=128)),
                          reads=[self.psb[bk]], writes=[xt])
            kb.dma("sp", Xv[:, :, tt * 128:(tt + 1) * 128], xt[:], reads=[xt])
        xs = kb.tile([S, D], F32, "xs")
        kb.dma("sp", xs[:], self.i["x_sample"], writes=[xs])
        xso = kb.tile([128, 32, S], F32, "xso")
        bk = self.bank()
        ps = self.psum[bk]

        def trs(e):
            for kt in range(32):
                ins = e.transpose(out=ps[:, kt * S:(kt + 1) * S], in_=xs[:, kt * 128:(kt + 1) * 128], identity=self.ident[0:S, 0:S])
            return ins
        kb.op("pe", trs, reads=[xs, self.cst], writes=[self.psb[bk]])
        kb.op("dve", lambda e: e.tensor_copy(out=xso[:], in_=ps[:, 0:32 * S].rearrange("p (a b) -> p a b", b=S)), reads=[self.psb[bk]], writes=[xso])
        kb.dma("sp", Xv[:, :, L:T], xso[:], reads=[xso])
        kb.phase_end()

    def phase_out(self):
        kb = self.kb
        Xv = self.X.rearrange("(kt p) t -> p kt t", p=128)
        xin = [kb.tile([128, 32, 128], F32, "yin") for _ in range(2)]
        yo = [kb.tile([128, D], F32, "yo") for _ in range(2)]
        for tt in range(16):
            xi = xin[tt % 2]
            yt = yo[tt % 2]
            kb.dma("sp", xi[:], Xv[:, :, tt * 128:(tt + 1) * 128], writes=[xi])
            for k4 in range(8):
                bk = self.bank()
                ps = self.psum[bk]

                def tr(e, xi=xi, ps=ps, k4=k4):
                    for q in range(4):
                        ins = e.transpose(out=ps[:, q * 128:(q + 1) * 128], in_=xi[:, k4 * 4 + q, :], identity=self.ident)
                    return ins
                kb.op("pe", tr, reads=[xi, self.cst], writes=[self.psb[bk]])
                if k4 % 2 == 0:
                    kb.op("dve", lambda e, yt=yt, ps=ps, k4=k4: e.tensor_copy(out=yt[:, k4 * 512:(k4 + 1) * 512], in_=ps[:]), reads=[self.psb[bk]], writes=[yt])
                else:
                    kb.op("act", lambda e, yt=yt, ps=ps, k4=k4: e.copy(out=yt[:, k4 * 512:(k4 + 1) * 512], in_=ps[:]), reads=[self.psb[bk]], writes=[yt])
            kb.dma("sp", self.o["y_prompt"][tt * 128:(tt + 1) * 128, :], yt[:], reads=[yt])
        xs = kb.tile([128, 32, S], F32, "ysin")
        kb.dma("sp", xs[:], Xv[:, :, L:T], writes=[xs])
        ys = kb.tile([S, D], F32, "ys")
        for k4 in range(8):
            bk = self.bank()
            ps = self.psum[bk]

            def trs(e, ps=ps, k4=k4):
                for q in range(4):
                    ins = e.transpose(out=ps[0:S, q * 128:(q + 1) * 128], in_=xs[:, k4 * 4 + q, :], identity=self.ident)
                return ins
            kb.op("pe", trs, reads=[xs, self.cst], writes=[self.psb[bk]])
            kb.op("dve", lambda e, ps=ps, k4=k4: e.tensor_copy(out=ys[:, k4 * 512:(k4 + 1) * 512], in_=ps[0:S, :]), reads=[self.psb[bk]], writes=[ys])
        kb.dma("sp", self.o["y_sample"], ys[:], reads=[ys])
        kb.phase_end()

    def norm_prologue(self, gamma_row):
        kb = self.kb
        Xv = self.X.rearrange("(kt p) t -> p kt t", p=128)
        g_t = self.load_cols(gamma_row, 32, "gam")
        xg = kb.tile([128, 32, T], BF16, "xg")
        rstd = kb.tile([128, T], F32, "rstd")
        pm_ = kb.sb_mark()
        xf = [kb.tile([128, T], F32, "xf") for _ in range(2)]
        sq = [kb.tile([128, T], BF16, "sq") for _ in range(2)]
        banks = [0, 1, 2, 3, 4]
        for kt in range(32):
            x_ = xf[kt % 2]
            s_ = sq[kt % 2]
            kb.dma("sp", x_[:], Xv[:, kt, :], writes=[x_])
            kb.op("act", lambda e, x_=x_, s_=s_: e.activation(out=s_[:], in_=x_[:], func=AF.Square), reads=[x_], writes=[s_])
            kb.op("dve", lambda e, x_=x_, kt=kt: e.tensor_scalar(out=xg[:, kt, :], in0=x_[:], scalar1=g_t[:, kt:kt + 1], scalar2=None, op0=ALU.mult),
                  reads=[x_, g_t], writes=[xg])

            def mm(e, s_=s_, kt=kt):
                for si, (c0, c1) in enumerate(SUBS):
                    ins = e.matmul(out=self.psum[banks[si]][:, 0:c1 - c0], lhsT=self.ones_b[:], rhs=s_[:, c0:c1], start=(kt == 0), stop=(kt == 31))
                return ins
            kb.op("pe", mm, reads=[s_, self.ones_b], writes=[self.psb[b] for b in banks])
        for si, (c0, c1) in enumerate(SUBS):
            b = banks[si]
            kb.op("act", lambda e, b=b, c0=c0, c1=c1: e.activation(out=rstd[:, c0:c1], in_=self.psum[b][:, 0:c1 - c0], func=AF.Sqrt, bias=1e-6, scale=1.0 / D),
                  reads=[self.psb[b]], writes=[rstd])
        kb.op("dve", lambda e: e.reciprocal(out=rstd[:], in_=rstd[:]), reads=[rstd], writes=[rstd])
        self.kb.bank_rr = 5
        kb.barrier()
        kb.sb_reset(pm_)
        return xg, rstd

    def phase_win(self, gamma_row, w, n_out):
        kb = self.kb
        xg, rstd = self.norm_prologue(gamma_row)
        wv = w.rearrange("(kt p) n -> p kt n", p=128)
        zo = [kb.tile([128, T], F32, "zo") for _ in range(2)]

        def epi(n, si, c0, c1, ps, pb):
            z_ = zo[n % 2]
            eng = "dve"
            kb.op(eng, lambda e: e.tensor_tensor(out=z_[:, c0:c1], in0=ps, in1=rstd[:, c0:c1], op=ALU.mult), reads=[pb, rstd], writes=[z_])
            if si == len(SUBS) - 1:
                kb.dma("sp", self.Z[n * 128:(n + 1) * 128, :], z_[:], reads=[z_])

        self.gemm(lambda n: wv[:, :, n * 128:(n + 1) * 128], 32, n_out // 128, lambda kt, c0, c1: xg[:, kt, c0:c1], [xg], epi)
        kb.phase_end()

    def phase_wout(self, w):
        kb = self.kb
        mx = kb.tile([128, 32, T], BF16, "mx")
        Mv = self.MIX.rearrange("(kt p) t -> p kt t", p=128)
        for q in range(4):
            kb.dma("sp", mx[:, q * 8:(q + 1) * 8, :], Mv[:, q * 8:(q + 1) * 8, :], writes=[mx])
        wv = w.rearrange("(kt p) n -> p kt n", p=128)
        self.resid_gemm(wv, 32, mx, SUBS)
        kb.phase_end()

    def resid_gemm(self, wv, KT, rhs_tile, subs, col_lo=0, col_hi=T, nw=3):
        kb = self.kb
        ncols = col_hi - col_lo
        xr = [kb.tile([128, ncols], F32, "xr") for _ in range(3)]
        xbufs = [Buf() for _ in range(32)]

        def wview(n):
            return wv[:, :, n * 128:(n + 1) * 128]

        def epi(n, si, c0, c1, ps, pb):
            x_ = xr[n % 3]
            if si == 0:
                kb.dma("sp", x_[:], self.X[n * 128:(n + 1) * 128, col_lo:col_hi], reads=[xbufs[n]], writes=[x_])
            kb.op("dve", lambda e: e.tensor_tensor(out=x_[:, c0:c1], in0=ps, in1=x_[:, c0:c1], op=ALU.add), reads=[pb, x_], writes=[x_])
            if si == len(subs) - 1:
                kb.dma("sp", self.X[n * 128:(n + 1) * 128, col_lo:col_hi], x_[:], reads=[x_], writes=[xbufs[n]])

        self.gemm(wview, KT, 32, lambda kt, c0, c1: rhs_tile[:, kt, c0:c1], [rhs_tile], epi, subs=subs, nw=nw)

    def phase_ffn(self, l):
        kb = self.kb
        xg, rstd = self.norm_prologue(self.i["norm_ffn"][l])
        w1v = self.i["ffn_w1"][l].rearrange("(kt p) n -> p kt n", p=128)
        w3v = self.i["ffn_w3"][l].rearrange("(kt p) n -> p kt n", p=128)
        sa = [kb.tile([128, T], F32, "sa") for _ in range(2)]
        ao = [kb.tile([128, T], BF16, "ao") for _ in range(2)]
        tb = [kb.tile([128, 512], F32, "tb") for _ in range(2)]
        cnt = [0]

        def epi(n, si, c0, c1, ps, pb):
            j, which = n // 2, n % 2
            s_ = sa[j % 2]
            a_ = ao[j % 2]
            if which == 0:
                kb.op("dve", lambda e: e.tensor_tensor(out=s_[:, c0:c1], in0=ps, in1=rstd[:, c0:c1], op=ALU.mult), reads=[pb, rstd], writes=[s_])
                kb.op("act", lambda e: e.activation(out=s_[:, c0:c1], in_=s_[:, c0:c1], func=AF.Silu), reads=[s_], writes=[s_])
            else:
                t_ = tb[cnt[0] % 2]
                cnt[0] += 1
                kb.op("dve", lambda e: e.tensor_tensor(out=t_[:, 0:c1 - c0], in0=ps, in1=rstd[:, c0:c1], op=ALU.mult), reads=[pb, rstd], writes=[t_])
                kb.op("pool", lambda e: e.tensor_tensor(out=a_[:, c0:c1], in0=t_[:, 0:c1 - c0], in1=s_[:, c0:c1], op=ALU.mult), reads=[t_, s_], writes=[a_])
                if si == len(SUBS) - 1:
                    kb.dma("sp", self.ACTB[j * 128:(j + 1) * 128, :], a_[:], reads=[a_])

        def wview(n):
            j, which = n // 2, n % 2
            return (w1v if which == 0 else w3v)[:, :, j * 128:(j + 1) * 128]

        self.gemm(wview, 32, 2 * NJ, lambda kt, c0, c1: xg[:, kt, c0:c1], [xg], epi)
        kb.phase_end()
        w2v = self.i["ffn_w2"][l].rearrange("(kt p) n -> p kt n", p=128)
        Av = self.ACTB.rearrange("(kt p) t -> p kt t", p=128)
        for (lo, hi) in [(0, 512), (512, 1024), (1024, 1536), (1536, T)]:
            at = kb.tile([128, NJ, hi - lo], BF16, "at")
            for q in range(0, NJ, 22):
                q1 = min(NJ, q + 22)
                kb.dma("sp", at[:, q:q1, :], Av[:, q:q1, lo:hi], writes=[at])
            subs = [(0, 512)] if hi - lo == 512 else [(0, 512), (512, 520)]
            self.resid_gemm(w2v, NJ, at, subs, lo, hi, nw=2)
            kb.phase_end()


def _stt(e, out, in0, scalar, in1, op0=ALU.mult, op1=ALU.mult):
    return e.scalar_tensor_tensor(out=out, in0=in0, scalar=scalar, in1=in1, op0=op0, op1=op1)


def phase_attn(self, i):
    kb = self.kb
    P = self.psum
    PB = self.psb
    qn = self.load_cols(self.i["q_norm"][i], 1, "qn")
    kn = self.load_cols(self.i["k_norm"][i], 1, "kn")
    qns = kb.tile([128, 1], F32, "qns")
    kb.op("dve", lambda e: e.tensor_scalar(out=qns[:], in0=qn[:], scalar1=ATT_SCALE, scalar2=None, op0=ALU.mult), reads=[qn], writes=[qns])
    wm = kb.tile([128, 2560], BF16, "wm")
    kb.op("dve", lambda e: e.tensor_copy(out=wm[:], in_=self.cst.t[:, C_WM:C_WM + 2560]), reads=[self.cst], writes=[wm])
    ws = kb.tile([128, 136], BF16, "ws")
    kb.op("dve", lambda e: e.tensor_copy(out=ws[:], in_=self.cst.t[:, C_WS:C_WS + 136]), reads=[self.cst], writes=[ws])
    kpv = self.o["k_prompt"][i].rearrange("(tt p) h d -> p tt h d", p=128)
    vpv = self.o["v_prompt"][i].rearrange("(tt p) h d -> p tt h d", p=128)
    ckv = self.i["cache_k"][i].rearrange("(tt p) h d -> p tt h d", p=128)
    cvv = self.i["cache_v"][i].rearrange("(tt p) h d -> p tt h d", p=128)
    qf = kb.tile([128, T], F32, "qf")
    kf = kb.tile([128, T], F32, "kf")
    vf = kb.tile([128, T], F32, "vf")
    sqb = kb.tile([128, T], BF16, "sqb")
    rq = kb.tile([128, T], F32, "rq")
    rk = kb.tile([128, T], F32, "rk")
    qb = kb.tile([128, T], BF16, "qb")
    knf = kb.tile([128, T], F32, "knf")
    kbt = kb.tile([128, T], BF16, "kbt")
    kout = kb.tile([128, 16, 128], F32, "kout")
    vout = kb.tile([128, 16, 128], F32, "vout")
    vtok = kb.tile([128, 16, 128], BF16, "vtok")
    ks = kb.tile([S, 128], F32, "ks")
    vs = kb.tile([S, 128], F32, "vs")
    vsb = kb.tile([S, 128], BF16, "vsb")
    pe_ = [kb.tile([128, 512], BF16, "pe") for _ in range(3)]
    pm_ = [kb.tile([128, 512], BF16, "pm") for _ in range(3)]
    rl = kb.tile([128, 512], F32, "rl")
    ao = [kb.tile([128, T], BF16, "ao") for _ in range(2)]
    ck = kb.tile([128, 16, 128], F32, "ck")
    cv = kb.tile([128, 16, 128], F32, "cv")
    vcb = kb.tile([128, 16, 128], BF16, "vcb")
    kcT = kb.tile([128, 2048], BF16, "kcT")
    srr = [0]

    def sbank():
        b = srr[0] % 4
        srr[0] += 1
        return b

    for hd in range(ATT_HEADS):
        a_ = ao[hd % 2]
        kb.dma("sp", qf[:], self.Z[hd * 128:(hd + 1) * 128, :], writes=[qf])
        kb.dma("sp", kf[:], self.Z[2048 + hd * 128:2048 + (hd + 1) * 128, :], writes=[kf])
        kb.dma("sp", vf[:], self.Z[4096 + hd * 128:4096 + (hd + 1) * 128, :], writes=[vf])
        for hh in range(2):
            kb.dma("sp", ck[:, hh * 8:(hh + 1) * 8, :], ckv[:, hh * 8:(hh + 1) * 8, hd, :], writes=[ck])
            kb.dma("sp", cv[:, hh * 8:(hh + 1) * 8, :], cvv[:, hh * 8:(hh + 1) * 8, hd, :], writes=[cv])
        if 'early' in ATT_PARTS:
            for hh in range(2):
                kb.dma(ATT_Q, kpv[:, hh * 8:(hh + 1) * 8, hd, :], ck[:, hh * 8:(hh + 1) * 8, :], reads=[ck])
        if 'xload' in ATT_PARTS:
            for hh in range(2):
                kb.dma(ATT_Q, vcb[:, hh * 8:(hh + 1) * 8, :].bitcast(F32)[:, :, 0:64] if False else kout[:, hh * 8:(hh + 1) * 8, :], ckv[:, hh * 8:(hh + 1) * 8, hd, :], writes=[kout])
        for (src, dst) in (((qf, rq), (kf, rk)) if 'norm' in ATT_PARTS else []):
            kb.op("act", lambda e, src=src: e.activation(out=sqb[:], in_=src[:], func=AF.Square), reads=[src], writes=[sqb])
            for (c0, c1) in SUBS:
                b = sbank()
                kb.op("pe", lambda e, b=b, c0=c0, c1=c1: e.matmul(out=P[b][:, 0:c1 - c0], lhsT=self.ones_b[:], rhs=sqb[:, c0:c1], start=True, stop=True),
                      reads=[sqb, self.ones_b], writes=[PB[b]])
                kb.op("act", lambda e, b=b, c0=c0, c1=c1, dst=dst: e.activation(out=dst[:, c0:c1], in_=P[b][:, 0:c1 - c0], func=AF.Sqrt, bias=1e-6, scale=1.0 / 128),
                      reads=[PB[b]], writes=[dst])
            kb.op("dve", lambda e, dst=dst: e.reciprocal(out=dst[:], in_=dst[:]), reads=[dst], writes=[dst])
        kb.op("dve", lambda e: _stt(e, qb[:], qf[:], qns[:, 0:1], rq[:]), reads=[qf, qns, rq], writes=[qb])
        kb.op("dve", lambda e: _stt(e, knf[:], kf[:], kn[:, 0:1], rk[:]), reads=[kf, kn, rk], writes=[knf])
        kb.op("act", lambda e: e.copy(out=kbt[:], in_=knf[:]), reads=[knf], writes=[kbt])
        if 'tr' not in ATT_PARTS:
            continue
        for (src, dstf, dstb) in (((knf, kout, None), (vf, vout, vtok)) if 'trk' not in ATT_PARTS else ((knf, kout, None),)):
            for t4 in range(int(ATT_T4)):
                b = sbank()

                def tr(e, b=b, src=src, t4=t4):
                    for q in range(4):
                        tt = t4 * 4 + q
                        ins = e.transpose(out=P[b][:, q * 128:(q + 1) * 128], in_=src[:, tt * 128:(tt + 1) * 128], identity=self.ident)
                    return ins
                kb.op("pe", tr, reads=[src, self.cst], writes=[PB[b]])
                if 'trnoevac' in ATT_PARTS:
                    continue
                kb.op("dve", lambda e, b=b, dstf=dstf, t4=t4: e.tensor_copy(out=dstf[:, t4 * 4:(t4 + 1) * 4, :], in_=P[b][:].rearrange("p (a c) -> p a c", c=128)),
                      reads=[PB[b]], writes=[dstf])
                if dstb is not None:
                    kb.op("act", lambda e, dstf=dstf, dstb=dstb, t4=t4: e.copy(out=dstb[:, t4 * 4:(t4 + 1) * 4, :], in_=dstf[:, t4 * 4:(t4 + 1) * 4, :]),
                          reads=[dstf], writes=[dstb])
        if 'trdma' in ATT_PARTS:
            for hh in range(2):
                if 'nok' not in ATT_PARTS:
                    src_t = ck if 'srcck' in ATT_PARTS else kout
                    dst_ap = self.Z[0:128, hh * 1024:(hh + 1) * 1024].rearrange("p (a b) -> p a b", b=128) if 'dstz' in ATT_PARTS else kpv[:, hh * 8:(hh + 1) * 8, hd, :]
                    kb.dma(ATT_Q, dst_ap, src_t[:, hh * 8:(hh + 1) * 8, :], reads=[src_t])
                if 'nov' not in ATT_PARTS:
                    kb.dma("sp", vpv[:, hh * 8:(hh + 1) * 8, hd, :], vout[:, hh * 8:(hh + 1) * 8, :], reads=[vout])
        if 'trs' not in ATT_PARTS:
            continue
        b = sbank()

        def trs(e, b=b):
            e.transpose(out=P[b][0:S, 0:128], in_=knf[:, L:T], identity=self.ident)
            return e.transpose(out=P[b][0:S, 128:256], in_=vf[:, L:T], identity=self.ident)
        kb.op("pe", trs, reads=[knf, vf, self.cst], writes=[PB[b]])
        kb.op("dve", lambda e, b=b: e.tensor_copy(out=ks[:], in_=P[b][0:S, 0:128]), reads=[PB[b]], writes=[ks])
        kb.op("dve", lambda e, b=b: e.tensor_copy(out=vs[:], in_=P[b][0:S, 128:256]), reads=[PB[b]], writes=[vs])
        kb.op("act", lambda e: e.copy(out=vsb[:], in_=vs[:]), reads=[vs], writes=[vsb])
        kb.dma("sp", self.o["k_sample"][i][:, hd, :], ks[:], reads=[ks])
        kb.dma("sp", self.o["v_sample"][i][:, hd, :], vs[:], reads=[vs])
        cnt = 0
        for qs in (range(4) if 'prompt' in ATT_PARTS else []):
            ob, lb = (4, 5) if qs % 2 == 0 else (6, 7)
            nk = 4 * qs + 4
            for kt in range(nk):
                b = sbank()
                p1 = pe_[cnt % 3]
                p2 = pm_[cnt % 3]
                cnt += 1
                off = 384 + qs * 512 - kt * 128
                kb.op("pe", lambda e, b=b, kt=kt, qs=qs: e.matmul(out=P[b][:], lhsT=kbt[:, kt * 128:(kt + 1) * 128], rhs=qb[:, qs * 512:(qs + 1) * 512], start=True, stop=True),
                      reads=[kbt, qb], writes=[PB[b]])
                kb.op("act", lambda e, b=b, p1=p1: e.activation(out=p1[:], in_=P[b][:], func=AF.Exp, bias=EXP_SHIFT, scale=1.0), reads=[PB[b]], writes=[p1])
                eng = "dve" if cnt % 3 != 0 else "pool"
                kb.op(eng, lambda e, p1=p1, p2=p2, off=off: e.tensor_tensor(out=p2[:], in0=p1[:], in1=wm[:, off:off + 512], op=ALU.mult), reads=[p1, wm], writes=[p2])

                def pv(e, p2=p2, kt=kt, nk=nk, ob=ob, lb=lb):
                    e.matmul(out=P[ob][:], lhsT=vtok[:, kt, :], rhs=p2[:], start=(kt == 0), stop=(kt == nk - 1))
                    return e.matmul(out=P[lb][:], lhsT=self.ones_b[:], rhs=p2[:], start=(kt == 0), stop=(kt == nk - 1))
                kb.op("pe", pv, reads=[p2, vtok, self.ones_b], writes=[PB[ob], PB[lb]])
            kb.op("dve", lambda e, lb=lb: e.reciprocal(out=rl[:], in_=P[lb][:]), reads=[PB[lb]], writes=[rl])
            kb.op("dve", lambda e, ob=ob, qs=qs, a_=a_: e.tensor_tensor(out=a_[:, qs * 512:(qs + 1) * 512], in0=P[ob][:], in1=rl[:], op=ALU.mult), reads=[PB[ob], rl], writes=[a_])
        if 'sample' not in ATT_PARTS:
            continue
        kb.op("pool", lambda e: e.tensor_copy(out=vcb[:], in_=cv[:]), reads=[cv], writes=[vcb])
        for t4 in range(4):
            b = sbank()

            def trc(e, b=b, t4=t4):
                for q in range(4):
                    ins = e.transpose(out=P[b][:, q * 128:(q + 1) * 128], in_=ck[:, t4 * 4 + q, :], identity=self.ident)
                return ins
            kb.op("pe", trc, reads=[ck, self.cst], writes=[PB[b]])
            kb.op("act", lambda e, b=b, t4=t4: e.copy(out=kcT[:, t4 * 512:(t4 + 1) * 512], in_=P[b][:]), reads=[PB[b]], writes=[kcT])
        b = sbank()

        def sc(e, b=b):
            for kt in range(16):
                e.matmul(out=P[b][:, kt * S:(kt + 1) * S], lhsT=kcT[:, kt * 128:(kt + 1) * 128], rhs=qb[:, L:T], start=True, stop=True)
            return e.matmul(out=P[b][0:S, 128:128 + S], lhsT=kbt[:, L:T], rhs=qb[:, L:T], start=True, stop=True)
        kb.op("pe", sc, reads=[kcT, kbt, qb], writes=[PB[b]])
        p1 = pe_[cnt % 3]
        p2 = pm_[cnt % 3]
        cnt += 1
        kb.op("act", lambda e, b=b, p1=p1: e.activation(out=p1[:, 0:128], in_=P[b][:, 0:128], func=AF.Exp, bias=EXP_SHIFT, scale=1.0), reads=[PB[b]], writes=[p1])
        kb.op("act", lambda e, b=b, p1=p1: e.activation(out=p1[0:S, 128:136], in_=P[b][0:S, 128:136], func=AF.Exp, bias=EXP_SHIFT, scale=1.0), reads=[PB[b]], writes=[p1])
        kb.op("dve", lambda e, p1=p1, p2=p2: e.tensor_tensor(out=p2[:, 0:128], in0=p1[:, 0:128], in1=ws[:, 0:128], op=ALU.mult), reads=[p1, ws], writes=[p2])
        kb.op("dve", lambda e, p1=p1, p2=p2: e.tensor_tensor(out=p2[0:S, 128:136], in0=p1[0:S, 128:136], in1=ws[0:S, 128:136], op=ALU.mult), reads=[p1, ws], writes=[p2])
        ob, lb = 4, 5

        def pvs(e, p2=p2):
            for kt in range(16):
                e.matmul(out=P[ob][:, 0:S], lhsT=vcb[:, kt, :], rhs=p2[:, kt * S:(kt + 1) * S], start=(kt == 0), stop=False)
            e.matmul(out=P[ob][:, 0:S], lhsT=vsb[:], rhs=p2[0:S, 128:136], start=False, stop=True)
            for kt in range(16):
                e.matmul(out=P[lb][:, 0:S], lhsT=self.ones_b[:], rhs=p2[:, kt * S:(kt + 1) * S], start=(kt == 0), stop=False)
            return e.matmul(out=P[lb][:, 0:S], lhsT=self.ones_b[0:S, :], rhs=p2[0:S, 128:136], start=False, stop=True)
        kb.op("pe", pvs, reads=[p2, vcb, vsb, self.ones_b], writes=[PB[ob], PB[lb]])
        kb.op("dve", lambda e: e.reciprocal(out=rl[:, 0:S], in_=P[lb][:, 0:S]), reads=[PB[lb]], writes=[rl])
        kb.op("dve", lambda e, a_=a_: e.tensor_tensor(out=a_[:, L:T], in0=P[ob][:, 0:S], in1=rl[:, 0:S], op=ALU.mult), reads=[PB[ob], rl], writes=[a_])
        kb.dma("sp", self.MIX[hd * 128:(hd + 1) * 128, :], a_[:], reads=[a_])
    kb.bank_rr = 0
    kb.phase_end()


def phase_sgu(self, i):
    kb = self.kb
    P = self.psum
    PB = self.psb
    lg = self.load_cols(self.i["sgu_ln_g"][i], 16, "lg")
    lbv = self.load_cols(self.i["sgu_ln_b"][i], 16, "lb")
    tril = self.cst.t[:, C_TRIL:C_TRIL + 128]
    wst = kb.tile([128, 8, 128], BF16, "wst")
    wraw = [kb.tile([128, 128], F32, "wraw") for _ in range(2)]
    for g in range(8):
        w_ = wraw[g % 2]
        kb.dma("sp", w_[:], self.i["sgu_w"][i, g], writes=[w_])
        kb.op("dve", lambda e, w_=w_: e.tensor_tensor(out=w_[:], in0=w_[:], in1=tril, op=ALU.mult), reads=[w_, self.cst], writes=[w_])
        b = g % 4
        kb.op("pe", lambda e, b=b, w_=w_: e.transpose(out=P[b][:, 0:128], in_=w_[:], identity=self.ident), reads=[w_, self.cst], writes=[PB[b]])
        kb.op("act", lambda e, b=b, g=g: e.copy(out=wst[:, g, :], in_=P[b][:, 0:128]), reads=[PB[b]], writes=[wst])
    sbr = kb.tile([1, 1024], F32, "sbr")
    kb.dma("sp", sbr[:], self.i["sgu_b"][i].rearrange("g t -> (g t)").rearrange("(o n) -> o n", o=1), writes=[sbr])
    sbb = kb.tile([1, 1024], BF16, "sbb")
    kb.op("dve", lambda e: e.tensor_copy(out=sbb[:], in_=sbr[:]), reads=[sbr], writes=[sbb])
    GUG = kb.tile([128, 16, T], BF16, "GUG")
    VN = kb.tile([128, 16, T], BF16, "VN")
    gsf = kb.tile([128, 16, S], F32, "gsf")
    vns = kb.tile([128, 16, S], F32, "vns")
    sgu_mark = kb.sb_mark()
    gvf = [kb.tile([128, T], F32, "gvf") for _ in range(2)]
    sqf = [kb.tile([128, L], F32, "sqf") for _ in range(2)]
    guf = [kb.tile([128, T], F32, "guf") for _ in range(2)]
    for ft in range(16):
        g_ = gvf[ft % 2]
        s_ = sqf[ft % 2]
        u_ = guf[ft % 2]
        kb.dma("sp", g_[:], self.Z[8192 + ft * 128:8192 + (ft + 1) * 128, :], writes=[g_])
        kb.dma("sp", u_[:], self.Z[6144 + ft * 128:6144 + (ft + 1) * 128, :], writes=[u_])
        kb.op("act", lambda e, g_=g_: e.activation(out=g_[:], in_=g_[:], func=AF.Gelu), reads=[g_], writes=[g_])
        kb.op("dve", lambda e, g_=g_, ft=ft: e.tensor_copy(out=VN[:, ft, :], in_=g_[:]), reads=[g_], writes=[VN])
        kb.op("dve", lambda e, g_=g_, ft=ft: e.tensor_copy(out=gsf[:, ft, :], in_=g_[:, L:T]), reads=[g_], writes=[gsf])
        kb.op("pool", lambda e, g_=g_, s_=s_: e.tensor_tensor(out=s_[:], in0=g_[:, 0:L], in1=g_[:, 0:L], op=ALU.mult), reads=[g_], writes=[s_])

        def st(e, g_=g_, s_=s_, ft=ft):
            for si in range(4):
                c0, c1 = SUBS[si]
                e.matmul(out=P[si][:], lhsT=self.ones_f[:], rhs=g_[:, c0:c1], start=(ft == 0), stop=(ft == 15))
                ins = e.matmul(out=P[4 + si][:], lhsT=self.ones_f[:], rhs=s_[:, c0:c1], start=(ft == 0), stop=(ft == 15))
            return ins
        kb.op("pe", st, reads=[g_, s_, self.ones_f], writes=PB)
        kb.op("act", lambda e, u_=u_, ft=ft: e.activation(out=GUG[:, ft, :], in_=u_[:], func=AF.Gelu), reads=[u_], writes=[GUG])
    kb.barrier()
    kb.sb_reset(sgu_mark)
    mu = kb.tile([128, L], F32, "mu")
    rs = kb.tile([128, L], F32, "rs")
    nmr = kb.tile([128, L], F32, "nmr")
    for si in range(4):
        c0, c1 = SUBS[si]
        kb.op("act", lambda e, si=si, c0=c0, c1=c1: e.activation(out=mu[:, c0:c1], in_=P[si][:], func=AF.Copy, scale=1.0 / 2048), reads=[PB[si]], writes=[mu])
        kb.op("dve", lambda e, c0=c0, c1=c1: e.tensor_tensor(out=nmr[:, c0:c1], in0=mu[:, c0:c1], in1=mu[:, c0:c1], op=ALU.mult), reads=[mu], writes=[nmr])
        kb.op("dve", lambda e, si=si, c0=c0, c1=c1: _stt(e, rs[:, c0:c1], P[4 + si][:], 1.0 / 2048, nmr[:, c0:c1], ALU.mult, ALU.subtract), reads=[PB[4 + si], nmr], writes=[rs])
    kb.op("act", lambda e: e.activation(out=rs[:], in_=rs[:], func=AF.Sqrt, bias=1e-5, scale=1.0), reads=[rs], writes=[rs])
    kb.op("dve", lambda e: e.reciprocal(out=rs[:], in_=rs[:]), reads=[rs], writes=[rs])
    kb.op("dve", lambda e: _stt(e, nmr[:], mu[:], -1.0, rs[:]), reads=[mu, rs], writes=[nmr])
    tmp = [kb.tile([128, L], F32, "ntmp") for _ in range(2)]
    for ft in range(16):
        t_ = tmp[ft % 2]
        kb.op("dve", lambda e, t_=t_, ft=ft: e.tensor_tensor(out=t_[:], in0=VN[:, ft, 0:L], in1=rs[:], op=ALU.mult), reads=[VN, rs], writes=[t_])
        kb.op("pool", lambda e, t_=t_: e.tensor_tensor(out=t_[:], in0=t_[:], in1=nmr[:], op=ALU.add), reads=[t_, nmr], writes=[t_])
        kb.op("dve", lambda e, t_=t_, ft=ft: e.tensor_scalar(out=VN[:, ft, 0:L], in0=t_[:], scalar1=lg[:, ft:ft + 1], scalar2=lbv[:, ft:ft + 1], op0=ALU.mult, op1=ALU.add),
              reads=[t_, lg, lbv], writes=[VN])
    sqs = kb.tile([128, 16, S], F32, "sqs")
    kb.op("dve", lambda e: e.tensor_tensor(out=sqs[:], in0=gsf[:], in1=gsf[:], op=ALU.mult), reads=[gsf], writes=[sqs])

    def sts(e):
        for ft in range(16):
            e.matmul(out=P[0][:, 0:S], lhsT=self.ones_f[:], rhs=gsf[:, ft, :], start=(ft == 0), stop=(ft == 15))
        for ft in range(16):
            ins = e.matmul(out=P[1][:, 0:S], lhsT=self.ones_f[:], rhs=sqs[:, ft, :], start=(ft == 0), stop=(ft == 15))
        return ins
    kb.op("pe", sts, reads=[gsf, sqs, self.ones_f], writes=[PB[0], PB[1]])
    mus = kb.tile([128, S], F32, "mus")
    rss = kb.tile([128, S], F32, "rss")
    nms = kb.tile([128, S], F32, "nms")
    kb.op("act", lambda e: e.activation(out=mus[:], in_=P[0][:, 0:S], func=AF.Copy, scale=1.0 / 2048), reads=[PB[0]], writes=[mus])
    kb.op("dve", lambda e: e.tensor_tensor(out=nms[:], in0=mus[:], in1=mus[:], op=ALU.mult), reads=[mus], writes=[nms])
    kb.op("dve", lambda e: _stt(e, rss[:], P[1][:, 0:S], 1.0 / 2048, nms[:], ALU.mult, ALU.subtract), reads=[PB[1], nms], writes=[rss])
    kb.op("act", lambda e: e.activation(out=rss[:], in_=rss[:], func=AF.Sqrt, bias=1e-5, scale=1.0), reads=[rss], writes=[rss])
    kb.op("dve", lambda e: e.reciprocal(out=rss[:], in_=rss[:]), reads=[rss], writes=[rss])
    kb.op("dve", lambda e: _stt(e, nms[:], mus[:], -1.0, rss[:]), reads=[mus, rss], writes=[nms])
    for ft in range(16):
        kb.op("dve", lambda e, ft=ft: e.tensor_tensor(out=vns[:, ft, :], in0=gsf[:, ft, :], in1=rss[:], op=ALU.mult), reads=[gsf, rss], writes=[vns])
        kb.op("dve", lambda e, ft=ft: e.tensor_tensor(out=vns[:, ft, :], in0=vns[:, ft, :], in1=nms[:], op=ALU.add), reads=[vns, nms], writes=[vns])
        kb.op("dve", lambda e, ft=ft: e.tensor_scalar(out=vns[:, ft, :], in0=vns[:, ft, :], scalar1=lg[:, ft:ft + 1], scalar2=lbv[:, ft:ft + 1], op0=ALU.mult, op1=ALU.add),
              reads=[vns, lg, lbv], writes=[vns])
    kb.op("dve", lambda e: e.tensor_copy(out=VN[:, :, L:T], in_=vns[:]), reads=[vns], writes=[VN])
    ysv = kb.tile([S, 2048], F32, "ysv")
    for f4 in range(4):
        b = 2 + (f4 % 2)

        def trv(e, b=b, f4=f4):
            for q in range(4):
                ins = e.transpose(out=P[b][0:S, q * 128:(q + 1) * 128], in_=vns[:, f4 * 4 + q, :], identity=self.ident)
            return ins
        kb.op("pe", trv, reads=[vns, self.cst], writes=[PB[b]])
        kb.op("dve", lambda e, b=b, f4=f4: e.tensor_copy(out=ysv[:, f4 * 512:(f4 + 1) * 512], in_=P[b][0:S, :]), reads=[PB[b]], writes=[ysv])
    kb.dma("sp", self.o["sgu_v_sample"][i], ysv[:], reads=[ysv])
    kb.barrier()
    kb.sb_reset(sgu_mark)
    Mv = self.MIX.rearrange("(kt p) t -> p kt t", p=128)
    vtok = [kb.tile([128, 2048], BF16, "vtk") for _ in range(2)]
    mo = [kb.tile([128, 16, 128], BF16, "mo") for _ in range(2)]
    for ch in range(16):
        vt = vtok[ch % 2]
        m_ = mo[ch % 2]
        for h in range(2):
            b = h
            pbf = P[b][:].bitcast(BF16)

            def trn(e, pbf=pbf, h=h, ch=ch):
                for q in range(8):
                    ins = e.transpose(out=pbf[:, q * 128:(q + 1) * 128], in_=VN[:, h * 8 + q, ch * 128:(ch + 1) * 128], identity=self.identb[:])
                return ins
            kb.op("pe", trn, reads=[VN, self.identb], writes=[PB[b]])
            kb.op("act", lambda e, pbf=pbf, vt=vt, h=h: e.copy(out=vt[:, h * 1024:(h + 1) * 1024], in_=pbf), reads=[PB[b]], writes=[vt])
        for f4 in range(4):
            b = 2 + (f4 + 4 * ch) % 6

            def mx(e, b=b, f4=f4, vt=vt):
                for q in range(4):
                    ft = f4 * 4 + q
                    g = ft // 2
                    e.matmul(out=P[b][:, q * 128:(q + 1) * 128], lhsT=vt[:, ft * 128:(ft + 1) * 128], rhs=wst[:, g, :], start=True, stop=False)
                    ins = e.matmul(out=P[b][:, q * 128:(q + 1) * 128], lhsT=self.ones_b[0:1, :], rhs=sbb[0:1, g * 128:(g + 1) * 128], start=False, stop=True)
                return ins
            kb.op("pe", mx, reads=[vt, wst, sbb, self.ones_b], writes=[PB[b]])
            kb.op("dve", lambda e, b=b, f4=f4, m_=m_, ch=ch: e.tensor_tensor(out=m_[:, f4 * 4:(f4 + 1) * 4, :], in0=P[b][:].rearrange("p (a c) -> p a c", c=128),
                                                                  in1=GUG[:, f4 * 4:(f4 + 1) * 4, ch * 128:(ch + 1) * 128], op=ALU.mult),
                  reads=[PB[b], GUG], writes=[m_])
        kb.dma("sp", Mv[:, 16:32, ch * 128:(ch + 1) * 128], m_[:], reads=[m_])
    vts = kb.tile([S, 2048], BF16, "vts")
    for h in range(2):
        b = h
        pbf = P[b][:].bitcast(BF16)

        def trn2(e, pbf=pbf, h=h):
            for q in range(8):
                ins = e.transpose(out=pbf[0:S, q * 128:(q + 1) * 128], in_=VN[:, h * 8 + q, L:T], identity=self.identb[:])
            return ins
        kb.op("pe", trn2, reads=[VN, self.identb], writes=[PB[b]])
        kb.op("act", lambda e, pbf=pbf, h=h: e.copy(out=vts[:, h * 1024:(h + 1) * 1024], in_=pbf[0:S, :]), reads=[PB[b]], writes=[vts])
    mos = kb.tile([128, 16, S], BF16, "mos")
    b = 2

    def mxs(e):
        for ft in range(16):
            g = ft // 2
            e.matmul(out=P[b][:, ft * S:(ft + 1) * S], lhsT=vts[0:S, ft * 128:(ft + 1) * 128], rhs=wst[0:S, g, 0:S], start=True, stop=False)
            ins = e.matmul(out=P[b][:, ft * S:(ft + 1) * S], lhsT=self.ones_b[0:1, :], rhs=sbb[0:1, g * 128:g * 128 + S], start=False, stop=True)
        return ins
    kb.op("pe", mxs, reads=[vts, wst, sbb, self.ones_b], writes=[PB[b]])
    kb.op("dve", lambda e: e.tensor_tensor(out=mos[:], in0=P[b][:, 0:16 * S].rearrange("p (a c) -> p a c", c=S), in1=GUG[:, :, L:T], op=ALU.mult), reads=[PB[b], GUG], writes=[mos])
    kb.dma("sp", Mv[:, 16:32, L:T], mos[:], reads=[mos])
    kb.bank_rr = 0
    kb.phase_end()


Prog.phase_attn = phase_attn
Prog.phase_sgu = phase_sgu


NK = 11


def phase_s5(self, i):
    kb = self.kb
    P = self.psum
    PB = self.psb
    V = "dve"
    NP = 5 + 2 * NK
    PT = kb.tile([128, NP + NK + 1, 64], F32, "PT")
    prep_mark = kb.sb_mark()

    def t64(name):
        return kb.tile([64, 128], F32, name)
    lre, lim, h0r, h0i = t64("lre"), t64("lim"), t64("h0r"), t64("h0i")
    ldt = kb.tile([64, 2], F32, "ldt")
    kb.dma("sp", lre[:], self.i["s5_lambda_re"][i], writes=[lre])
    kb.dma("sp", lim[:], self.i["s5_lambda_im"][i], writes=[lim])
    kb.dma("sp", ldt[:], self.i["s5_log_dt"][i], writes=[ldt])
    kb.dma("sp", h0r[:], self.i["state_s5_re"][i], writes=[h0r])
    kb.dma("sp", h0i[:], self.i["state_s5_im"][i], writes=[h0i])
    dt = kb.tile([64, 2], F32, "dt")
    kb.op("act", lambda e: e.activation(out=dt[:], in_=ldt[:], func=AF.Exp), reads=[ldt], writes=[dt])
    are, aim, mag, thr, tmp, sn, sh, cs = [t64(n) for n in ("are", "aim", "mag", "thr", "tmp", "sn", "sh", "cs")]
    for gl in range(2):
        sl = slice(gl * 64, (gl + 1) * 64)
        kb.op(V, lambda e, sl=sl, gl=gl: e.tensor_scalar(out=are[:, sl], in0=lre[:, sl], scalar1=dt[:, gl:gl + 1], scalar2=None, op0=ALU.mult), reads=[lre, dt], writes=[are])
        kb.op(V, lambda e, sl=sl, gl=gl: e.tensor_scalar(out=aim[:, sl], in0=lim[:, sl], scalar1=dt[:, gl:gl + 1], scalar2=None, op0=ALU.mult), reads=[lim, dt], writes=[aim])
    kb.op("act", lambda e: e.activation(out=mag[:], in_=are[:], func=AF.Exp), reads=[are], writes=[mag])
    kb.op(V, lambda e: e.tensor_copy(out=thr[:], in_=aim[:]), reads=[aim], writes=[thr])
    for m in range(1, 9):
        kb.op(V, lambda e, m=m: e.tensor_scalar(out=tmp[:], in0=aim[:], scalar1=(2 * m - 1) * PI, scalar2=-2.0 * PI, op0=ALU.is_ge, op1=ALU.mult), reads=[aim], writes=[tmp])
        kb.op(V, lambda e: e.tensor_tensor(out=thr[:], in0=thr[:], in1=tmp[:], op=ALU.add), reads=[thr, tmp], writes=[thr])
    kb.op(V, lambda e: e.tensor_scalar(out=thr[:], in0=thr[:], scalar1=-PI, scalar2=PI, op0=ALU.max, op1=ALU.min), reads=[thr], writes=[thr])
    kb.op("act", lambda e: e.activation(out=sn[:], in_=thr[:], func=AF.Sin), reads=[thr], writes=[sn])
    kb.op("act", lambda e: e.activation(out=sh[:], in_=thr[:], func=AF.Sin, scale=0.5), reads=[thr], writes=[sh])
    kb.op(V, lambda e: e.tensor_tensor(out=tmp[:], in0=sh[:], in1=sh[:], op=ALU.mult), reads=[sh], writes=[tmp])
    kb.op(V, lambda e: e.tensor_scalar(out=cs[:], in0=tmp[:], scalar1=-2.0, scalar2=1.0, op0=ALU.mult, op1=ALU.add), reads=[tmp], writes=[cs])
    lbr, lbi, nr, dd, fr, fi, t2 = [t64(n) for n in ("lbr", "lbi", "nr", "dd", "fr", "fi", "t2")]

    def tt(out, a, b, op):
        kb.op(V, lambda e: e.tensor_tensor(out=out[:], in0=a[:], in1=b[:], op=op), reads=[a, b], writes=[out])
    tt(lbr, mag, cs, ALU.mult)
    tt(lbi, mag, sn, ALU.mult)
    kb.op(V, lambda e: e.tensor_scalar(out=nr[:], in0=lbr[:], scalar1=-1.0, scalar2=None, op0=ALU.add), reads=[lbr], writes=[nr])
    tt(dd, lre, lre, ALU.mult)
    tt(tmp, lim, lim, ALU.mult)
    tt(dd, dd, tmp, ALU.add)
    kb.op(V, lambda e: e.reciprocal(out=dd[:], in_=dd[:]), reads=[dd], writes=[dd])
    tt(fr, nr, lre, ALU.mult)
    tt(tmp, lbi, lim, ALU.mult)
    tt(fr, fr, tmp, ALU.add)
    tt(fr, fr, dd, ALU.mult)
    tt(fi, lbi, lre, ALU.mult)
    tt(tmp, nr, lim, ALU.mult)
    tt(fi, fi, tmp, ALU.subtract)
    tt(fi, fi, dd, ALU.mult)
    inr, ini = t64("inr"), t64("ini")
    tt(inr, cs, h0r, ALU.mult)
    tt(tmp, sn, h0i, ALU.mult)
    tt(inr, inr, tmp, ALU.subtract)
    tt(ini, sn, h0r, ALU.mult)
    tt(tmp, cs, h0i, ALU.mult)
    tt(ini, ini, tmp, ALU.add)
    cks = [cs] + [t64(f"ck{k}") for k in range(1, NK)]
    sks = [sn] + [t64(f"sk{k}") for k in range(1, NK)]
    for k in range(NK - 1):
        tt(cks[k + 1], cks[k], cks[k], ALU.mult)
        tt(tmp, sks[k], sks[k], ALU.mult)
        tt(cks[k + 1], cks[k + 1], tmp, ALU.subtract)
        tt(t2, cks[k], sks[k], ALU.mult)
        kb.op(V, lambda e, k=k: e.tensor_scalar(out=sks[k + 1][:], in0=t2[:], scalar1=2.0, scalar2=None, op0=ALU.mult), reads=[t2], writes=[sks[k + 1]])
    srcs = [mag, fr, fi, inr, ini] + cks + sks
    assert NP == len(srcs)
    I_MAG, I_FR, I_FI, I_INR, I_INI, I_CK, I_SK = 0, 1, 2, 3, 4, 5, 5 + NK
    I_NSK = NP
    I_NFI = NP + NK
    for a in range(0, NP, 8):
        b = self.bank()
        grp = srcs[a:a + 8]

        def trp(e, b=b, grp=grp):
            for q, sx in enumerate(grp):
                ins = e.transpose(out=P[b][:, q * 64:(q + 1) * 64], in_=sx[:], identity=self.ident[0:64, 0:64])
            return ins
        kb.op("pe", trp, reads=grp + [self.cst], writes=[PB[b]])
        kb.op(V, lambda e, b=b, a=a, n=len(grp): e.tensor_copy(out=PT[:, a:a + n, :], in_=P[b][:, 0:n * 64].rearrange("p (a c) -> p a c", c=64)), reads=[PB[b]], writes=[PT])
    kb.op(V, lambda e: e.tensor_scalar(out=PT[:, I_NSK:I_NSK + NK, :], in0=PT[:, I_SK:I_SK + NK, :], scalar1=-1.0, scalar2=None, op0=ALU.mult), reads=[PT], writes=[PT])
    kb.op(V, lambda e: e.tensor_scalar(out=PT[:, I_NFI, :], in0=PT[:, I_FI, :], scalar1=-1.0, scalar2=None, op0=ALU.mult), reads=[PT], writes=[PT])
    kb.barrier()
    kb.sb_reset(prep_mark)
    dcol = self.load_cols(self.i["s5_d"][i], 16, "dcol")
    Ball = [kb.tile([128, 64, 16], F32, "Ball") for _ in range(2)]
    Call = [kb.tile([128, 16, 64], F32, "Call") for _ in range(2)]
    kb.dma("sp", Ball[0][:], self.i["s5_b_re"][i].rearrange("j q h -> q j h"), writes=[Ball[0]])
    kb.dma("sp", Ball[1][:], self.i["s5_b_im"][i].rearrange("j q h -> q j h"), writes=[Ball[1]])
    kb.dma("sp", Call[0][:], self.i["s5_c_re"][i].rearrange("(kt r) p -> r kt p", r=128), writes=[Call[0]])
    kb.dma("sp", Call[1][:], self.i["s5_c_im"][i].rearrange("(kt r) p -> r kt p", r=128), writes=[Call[1]])
    rm = self.cst.t[:, C_RM:C_RM + 8]
    preB = [[kb.tile([128, 128], F32, "preB") for _ in range(2)] for _ in range(4)]
    for a in range(4):
        for c in range(2):
            kb.op(V, lambda e, a=a, c=c: e.memset(preB[a][c][:], 0.0), writes=[preB[a][c]])
    preC = [kb.tile([128, 128], F32, "preC") for _ in range(2)]
    tb16 = kb.tile([128, 16], F32, "tb16")
    LB = [kb.tile([128, 4, 2, 128], BF16, "LB") for _ in range(2)]
    LC = [kb.tile([128, 4, 2, 128], BF16, "LC") for _ in range(2)]
    Er = [kb.tile([128, T], F32, "Er") for _ in range(2)]
    Ei = [kb.tile([128, T], F32, "Ei") for _ in range(2)]
    et1 = kb.tile([128, 4096], F32, "et1")
    uf = kb.tile([128, T], F32, "uf")
    ub = kb.tile([128, T], BF16, "ub")
    qr = kb.tile([128, T], F32, "qr")
    qi = kb.tile([128, T], F32, "qi")
    magb = kb.tile([128, 512], F32, "magb")
    ones512 = kb.tile([128, 512], F32, "ones512")
    kb.op(V, lambda e: e.memset(ones512[:], 1.0), writes=[ones512])
    w1, w2, w3, w4, qri, qii, hrf, hif = [kb.tile([128, 512], F32, n) for n in ("w1", "w2", "w3", "w4", "qri", "qii", "hrf", "hif")]
    HR = [kb.tile([128, T], BF16, "HR") for _ in range(4)]
    HI = [kb.tile([128, T], BF16, "HI") for _ in range(4)]
    yf = [kb.tile([128, 512], F32, "yf") for _ in range(2)]
    ygt = [kb.tile([128, T], BF16, "ygt") for _ in range(2)]
    ST = kb.tile([128, 4, 64], F32, "ST")
    bcnt = [0]
    ccnt = [0]

    def sc(idx, j):
        return PT[:, idx, j:j + 1]

    for j in range(64):
        kt, jj = j // 4, j % 4
        lb_, lc_ = LB[kt % 2], LC[kt % 2]
        er, ei = Er[j % 2], Ei[j % 2]
        G = "pool"
        kb.op(G, lambda e, er=er: e.memset(er[:, 0:1], 1.0), writes=[er])
        kb.op(G, lambda e, ei=ei: e.memset(ei[:, 0:1], 0.0), writes=[ei])
        for k in range(NK):
            n = 1 << k
            kb.op(G, lambda e, er=er, n=n, k=k, j=j: e.tensor_scalar(out=et1[:, 0:n], in0=er[:, 0:n], scalar1=sc(I_CK + k, j), scalar2=None, op0=ALU.mult), reads=[er, PT], writes=[et1])
            kb.op(G, lambda e, ei=ei, n=n, k=k, j=j: e.tensor_scalar(out=et1[:, 1024:1024 + n], in0=ei[:, 0:n], scalar1=sc(I_SK + k, j), scalar2=None, op0=ALU.mult), reads=[ei, PT], writes=[et1])
            kb.op(G, lambda e, er=er, n=n, k=k, j=j: e.tensor_scalar(out=et1[:, 2048:2048 + n], in0=er[:, 0:n], scalar1=sc(I_SK + k, j), scalar2=None, op0=ALU.mult), reads=[er, PT], writes=[et1])
            kb.op(G, lambda e, ei=ei, n=n, k=k, j=j: e.tensor_scalar(out=et1[:, 3072:3072 + n], in0=ei[:, 0:n], scalar1=sc(I_CK + k, j), scalar2=None, op0=ALU.mult), reads=[ei, PT], writes=[et1])
            kb.op(G, lambda e, er=er, n=n: e.tensor_tensor(out=er[:, n:2 * n], in0=et1[:, 0:n], in1=et1[:, 1024:1024 + n], op=ALU.subtract), reads=[et1], writes=[er])
            kb.op(G, lambda e, ei=ei, n=n: e.tensor_tensor(out=ei[:, n:2 * n], in0=et1[:, 2048:2048 + n], in1=et1[:, 3072:3072 + n], op=ALU.add), reads=[et1], writes=[ei])
        kb.op(G, lambda e, er=er: e.tensor_copy(out=er[:, L:T], in_=er[:, 0:S]), reads=[er], writes=[er])
        kb.op(G, lambda e, ei=ei: e.tensor_copy(out=ei[:, L:T], in_=ei[:, 0:S]), reads=[ei], writes=[ei])
        pb_re, pb_im = preB[jj]
        c0b = ((2 * j) % 8) * 16
        for gl in range(2):
            ps_ = slice(gl * 64, (gl + 1) * 64)
            cc = slice(c0b + gl * 16, c0b + gl * 16 + 16)
            kb.op(V, lambda e, ps_=ps_, j=j: e.tensor_scalar(out=tb16[ps_, :], in0=Ball[0][ps_, j, :], scalar1=PT[ps_, I_FR, j:j + 1], scalar2=None, op0=ALU.mult), reads=[Ball[0], PT], writes=[tb16])
            kb.op(V, lambda e, ps_=ps_, cc=cc, j=j, pb_re=pb_re: _stt(e, pb_re[ps_, cc], Ball[1][ps_, j, :], PT[ps_, I_NFI, j:j + 1], tb16[ps_, :], ALU.mult, ALU.add), reads=[Ball[1], PT, tb16], writes=[pb_re])
            kb.op(V, lambda e, ps_=ps_, j=j: e.tensor_scalar(out=tb16[ps_, :], in0=Ball[1][ps_, j, :], scalar1=PT[ps_, I_FR, j:j + 1], scalar2=None, op0=ALU.mult), reads=[Ball[1], PT], writes=[tb16])
            kb.op(V, lambda e, ps_=ps_, cc=cc, j=j, pb_im=pb_im: _stt(e, pb_im[ps_, cc], Ball[0][ps_, j, :], PT[ps_, I_FI, j:j + 1], tb16[ps_, :], ALU.mult, ALU.add), reads=[Ball[0], PT, tb16], writes=[pb_im])
        for c in range(2):
            pc = preC[c]
            sgn = 1.0 if c == 0 else -1.0
            for gl in range(2):
                m = (2 * j + gl) % 8
                kb.op(V, lambda e, pc=pc, c=c, gl=gl, m=m, kt=kt, sgn=sgn: e.tensor_scalar(out=pc[:, gl * 64:(gl + 1) * 64], in0=Call[c][:, kt, :], scalar1=rm[:, m:m + 1], scalar2=sgn, op0=ALU.mult, op1=ALU.mult),
                      reads=[Call[c], self.cst], writes=[pc])
        b = 6 + (j % 2)

        def trl(e, b=b, pb_re=pb_re, pb_im=pb_im):
            e.transpose(out=P[b][:, 0:128], in_=pb_re[:], identity=self.ident)
            e.transpose(out=P[b][:, 128:256], in_=pb_im[:], identity=self.ident)
            e.transpose(out=P[b][:, 256:384], in_=preC[0][:], identity=self.ident)
            return e.transpose(out=P[b][:, 384:512], in_=preC[1][:], identity=self.ident)
        kb.op("pe", trl, reads=[pb_re, pb_im, preC[0], preC[1], self.cst], writes=[PB[b]])
        kb.op("act", lambda e, b=b, lb_=lb_, jj=jj: e.copy(out=lb_[:, jj, :, :], in_=P[b][:, 0:256].rearrange("p (a c) -> p a c", c=128)), reads=[PB[b]], writes=[lb_])
        kb.op("act", lambda e, b=b, lc_=lc_, jj=jj: e.copy(out=lc_[:, jj, :, :], in_=P[b][:, 256:512].rearrange("p (a c) -> p a c", c=128)), reads=[PB[b]], writes=[lc_])
        if jj == 0:
            kb.dma("sp", uf[:], self.Z[kt * 128:(kt + 1) * 128, :], writes=[uf])
            kb.op("act", lambda e: e.copy(out=ub[:], in_=uf[:]), reads=[uf], writes=[ub])
        kb.op(V, lambda e, j=j: e.tensor_scalar(out=magb[:], in0=ones512[:], scalar1=sc(I_MAG, j), scalar2=None, op0=ALU.mult), reads=[ones512, PT], writes=[magb])
        for si, (c0, c1) in enumerate(SUBS):
            n = c1 - c0
            br = 2 * (bcnt[0] % 3)
            bi = br + 1
            bcnt[0] += 1

            def bp(e, br=br, bi=bi, lb_=lb_, jj=jj, c0=c0, c1=c1, n=n):
                e.matmul(out=P[br][:, 0:n], lhsT=lb_[:, jj, 0, :], rhs=ub[:, c0:c1], start=True, stop=True)
                return e.matmul(out=P[bi][:, 0:n], lhsT=lb_[:, jj, 1, :], rhs=ub[:, c0:c1], start=True, stop=True)
            kb.op("pe", bp, reads=[lb_, ub], writes=[PB[br], PB[bi]])
            def m2(out, a, b_, op, rd, wr):
                kb.op(V, lambda e: e.tensor_tensor(out=out, in0=a, in1=b_, op=op), reads=rd, writes=wr)
            m2(w1[:, 0:n], P[br][:, 0:n], er[:, c0:c1], ALU.mult, [PB[br], er], [w1])
            m2(w2[:, 0:n], P[bi][:, 0:n], ei[:, c0:c1], ALU.mult, [PB[bi], ei], [w2])
            m2(qri[:, 0:n], w1[:, 0:n], w2[:, 0:n], ALU.add, [w1, w2], [qri])
            m2(w3[:, 0:n], P[bi][:, 0:n], er[:, c0:c1], ALU.mult, [PB[bi], er], [w3])
            m2(w4[:, 0:n], P[br][:, 0:n], ei[:, c0:c1], ALU.mult, [PB[br], ei], [w4])
            m2(qii[:, 0:n], w3[:, 0:n], w4[:, 0:n], ALU.subtract, [w3, w4], [qii])
            if si == 0:
                inr_, ini_ = 0.0, 0.0
            elif si == 4:
                inr_, ini_ = sc(I_INR, j), sc(I_INI, j)
            else:
                inr_, ini_ = qr[:, c0 - 1:c0], qi[:, c0 - 1:c0]
            kb.op(V, lambda e, c0=c0, c1=c1, n=n, inr_=inr_: e.tensor_tensor_scan(out=qr[:, c0:c1], data0=magb[:, 0:n], data1=qri[:, 0:n], initial=inr_, op0=ALU.mult, op1=ALU.add),
                  reads=[magb, qri, qr, PT], writes=[qr])
            kb.op(V, lambda e, c0=c0, c1=c1, n=n, ini_=ini_: e.tensor_tensor_scan(out=qi[:, c0:c1], data0=magb[:, 0:n], data1=qii[:, 0:n], initial=ini_, op0=ALU.mult, op1=ALU.add),
                  reads=[magb, qii, qi, PT], writes=[qi])
            m2(w1[:, 0:n], qr[:, c0:c1], er[:, c0:c1], ALU.mult, [qr, er], [w1])
            m2(w2[:, 0:n], qi[:, c0:c1], ei[:, c0:c1], ALU.mult, [qi, ei], [w2])
            m2(hrf[:, 0:n], w1[:, 0:n], w2[:, 0:n], ALU.subtract, [w1, w2], [hrf])
            m2(w3[:, 0:n], qr[:, c0:c1], ei[:, c0:c1], ALU.mult, [qr, ei], [w3])
            m2(w4[:, 0:n], qi[:, c0:c1], er[:, c0:c1], ALU.mult, [qi, er], [w4])
            m2(hif[:, 0:n], w3[:, 0:n], w4[:, 0:n], ALU.add, [w3, w4], [hif])
            kb.op("act", lambda e, jj=jj, c0=c0, c1=c1, n=n: e.copy(out=HR[jj][:, c0:c1], in_=hrf[:, 0:n]), reads=[hrf], writes=[HR[jj]])
            kb.op("act", lambda e, jj=jj, c0=c0, c1=c1, n=n: e.copy(out=HI[jj][:, c0:c1], in_=hif[:, 0:n]), reads=[hif], writes=[HI[jj]])
            if si in (3, 4):
                a = 0 if si == 3 else 2
                kb.op("act", lambda e, a=a, j=j, n=n: e.copy(out=ST[:, a, j:j + 1], in_=hrf[:, n - 1:n]), reads=[hrf], writes=[ST])
                kb.op("act", lambda e, a=a, j=j, n=n: e.copy(out=ST[:, a + 1, j:j + 1], in_=hif[:, n - 1:n]), reads=[hif], writes=[ST])
        if jj == 3:
            yg_ = ygt[kt % 2]
            for si, (c0, c1) in enumerate(SUBS):
                n = c1 - c0
                b = 6 + (ccnt[0] % 2)
                y_ = yf[ccnt[0] % 2]
                ccnt[0] += 1

                def cp(e, b=b, lc_=lc_, c0=c0, c1=c1, n=n):
                    for a in range(4):
                        e.matmul(out=P[b][:, 0:n], lhsT=lc_[:, a, 0, :], rhs=HR[a][:, c0:c1], start=(a == 0), stop=False)
                        ins = e.matmul(out=P[b][:, 0:n], lhsT=lc_[:, a, 1, :], rhs=HI[a][:, c0:c1], start=False, stop=(a == 3))
                    return ins
                kb.op("pe", cp, reads=[lc_] + HR + HI, writes=[PB[b]])
                kb.op(V, lambda e, b=b, y_=y_, c0=c0, c1=c1, n=n, kt=kt: _stt(e, y_[:, 0:n], uf[:, c0:c1], dcol[:, kt:kt + 1], P[b][:, 0:n], ALU.mult, ALU.add), reads=[uf, dcol, PB[b]], writes=[y_])
                kb.op("act", lambda e, y_=y_, yg_=yg_, c0=c0, c1=c1, n=n: e.activation(out=yg_[:, c0:c1], in_=y_[:, 0:n], func=AF.Gelu), reads=[y_], writes=[yg_])
            kb.dma("sp", self.YG[kt * 128:(kt + 1) * 128, :], yg_[:], reads=[yg_])
    b = 0

    def trs(e):
        for a in range(4):
            ins = e.transpose(out=P[b][0:64, a * 128:(a + 1) * 128], in_=ST[:, a, :], identity=self.ident)
        return ins
    kb.op("pe", trs, reads=[ST, self.cst], writes=[PB[b]])
    sto = kb.tile([64, 512], F32, "sto")
    kb.op(V, lambda e: e.tensor_copy(out=sto[:], in_=P[b][0:64, :]), reads=[PB[b]], writes=[sto])
    for a, nm in enumerate(("s5_re_prompt", "s5_im_prompt", "s5_re_sample", "s5_im_sample")):
        kb.dma("sp", self.o[nm][i], sto[:, a * 128:(a + 1) * 128], reads=[sto])
    kb.bank_rr = 0
    kb.phase_end()
    bg = self.load_cols(self.i["s5_b_glu"][i], 16, "bg")
    YGt = kb.tile([128, 16, T], BF16, "YGt")
    kb.dma("sp", YGt[:], self.YG.rearrange("(kt p) t -> p kt t", p=128), writes=[YGt])
    wv = self.i["s5_w_glu"][i].rearrange("(kt p) n -> p kt n", p=128)
    sg = [kb.tile([128, 512], F32, "sg") for _ in range(2)]
    mo = [kb.tile([128, T], BF16, "mo") for _ in range(2)]
    gc = [0]

    def epi(n, si, c0, c1, ps, pb):
        s_ = sg[gc[0] % 2]
        gc[0] += 1
        m_ = mo[n % 2]
        kb.op("act", lambda e: e.activation(out=s_[:, 0:c1 - c0], in_=ps, func=AF.Sigmoid, bias=bg[:, n:n + 1], scale=1.0), reads=[pb, bg], writes=[s_])
        kb.op(V, lambda e: e.tensor_tensor(out=m_[:, c0:c1], in0=s_[:, 0:c1 - c0], in1=YGt[:, n, c0:c1], op=ALU.mult), reads=[s_, YGt], writes=[m_])
        if si == len(SUBS) - 1:
            kb.dma("sp", self.MIX[n * 128:(n + 1) * 128, :], m_[:], reads=[m_])
    self.gemm(lambda n: wv[:, :, n * 128:(n + 1) * 128], 16, 16, lambda kt, c0, c1: YGt[:, kt, c0:c1], [YGt], epi)
    kb.phase_end()


def phase_pool(self, i):
    kb = self.kb
    P = self.psum
    PB = self.psb
    V = "dve"
    pscale = self.load_cols(self.i["pool_scale"][i], 16, "pscale")
    ZP = kb.tile([128, 16, T], BF16, "ZP")
    spin = kb.tile([15, 2048], F32, "spin")
    kb.dma("sp", spin[:], self.i["state_pool"][i], writes=[spin])
    kb.dma("sp", self.o["pool_sample"][i][0:7, :], self.i["state_pool"][i][8:15, :])
    pp = kb.tile([15, 2048], F32, "pp")
    psm = kb.tile([S, 2048], F32, "psm")
    EA = [kb.tile([128, 15 + L], F32, "EA") for _ in range(2)]
    EB = [kb.tile([128, 15 + L], F32, "EB") for _ in range(2)]
    ES = [kb.tile([128, 15 + S], F32, "ES") for _ in range(3)]
    zpf = kb.tile([128, L], F32, "zpf")
    t15 = kb.tile([128, 15], F32, "t15")
    for a in range(2):
        kb.op(V, lambda e, a=a: e.memset(EA[a][:, 0:15], 0.0), writes=[EA[a]])
    for ft in range(16):
        g = ft // 4
        w = (2, 4, 8, 16)[g]
        E0 = EA[ft % 2]
        kb.dma("sp", E0[:, 15:15 + L], self.Z[2048 + ft * 128:2048 + (ft + 1) * 128, 0:L], writes=[E0])
        Es0 = ES[0]
        kb.dma("sp", Es0[:, 15:15 + S], self.Z[2048 + ft * 128:2048 + (ft + 1) * 128, L:T], writes=[Es0])
        b = ft % 4
        kb.op("pe", lambda e, b=b, ft=ft: e.transpose(out=P[b][:, 0:15], in_=spin[:, ft * 128:(ft + 1) * 128], identity=self.ident[0:15, 0:15]), reads=[spin, self.cst], writes=[PB[b]])
        kb.op(V, lambda e, b=b: e.tensor_copy(out=Es0[:, 0:15], in_=P[b][:, 0:15]), reads=[PB[b]], writes=[Es0])
        A, Bt = E0, EB[0]
        As, Bs = Es0, ES[1]
        nb = 0
        step = 1
        while step < w:
            dst = EB[nb % 2]
            dsts = ES[1 + nb % 2]
            nb += 1
            kb.op(V, lambda e, A=A, dst=dst, step=step: e.tensor_tensor(out=dst[:, step:], in0=A[:, step:], in1=A[:, 0:15 + L - step], op=ALU.add), reads=[A], writes=[dst])
            kb.op(V, lambda e, As=As, dsts=dsts, step=step: e.tensor_tensor(out=dsts[:, step:], in0=As[:, step:], in1=As[:, 0:15 + S - step], op=ALU.add), reads=[As], writes=[dsts])
            A, As = dst, dsts
            step *= 2
        kb.op(V, lambda e, A=A, E0=E0, w=w: _stt(e, zpf[:], A[:, 15:15 + L], 1.0 / w, E0[:, 15:15 + L], ALU.mult, ALU.subtract), reads=[A, E0], writes=[zpf])
        kb.op(V, lambda e, A=A, g=g: e.tensor_tensor(out=t15[:], in0=A[:, 15:30], in1=self.cst.t[:, C_FIX + g * 15:C_FIX + (g + 1) * 15], op=ALU.mult), reads=[A, self.cst], writes=[t15])
        kb.op(V, lambda e: e.tensor_tensor(out=zpf[:, 0:15], in0=zpf[:, 0:15], in1=t15[:], op=ALU.add), reads=[zpf, t15], writes=[zpf])
        kb.op("act", lambda e, ft=ft: e.copy(out=ZP[:, ft, 0:L], in_=zpf[:]), reads=[zpf], writes=[ZP])
        kb.op(V, lambda e, As=As, Es0=Es0, w=w, ft=ft: _stt(e, ZP[:, ft, L:T], As[:, 15:15 + S], 1.0 / w, Es0[:, 15:15 + S], ALU.mult, ALU.subtract), reads=[As, Es0], writes=[ZP])
        b2 = 4 + ft % 4

        def tro(e, b2=b2, E0=E0, Es0=Es0):
            e.transpose(out=P[b2][0:15, 0:128], in_=E0[:, L:L + 15], identity=self.ident)
            return e.transpose(out=P[b2][0:S, 128:256], in_=Es0[:, 15:15 + S], identity=self.ident)
        kb.op("pe", tro, reads=[E0, Es0, self.cst], writes=[PB[b2]])
        kb.op("act", lambda e, b2=b2, ft=ft: e.copy(out=pp[:, ft * 128:(ft + 1) * 128], in_=P[b2][0:15, 0:128]), reads=[PB[b2]], writes=[pp])
        kb.op("act", lambda e, b2=b2, ft=ft: e.copy(out=psm[:, ft * 128:(ft + 1) * 128], in_=P[b2][0:S, 128:256]), reads=[PB[b2]], writes=[psm])
    kb.dma("sp", self.o["pool_prompt"][i], pp[:], reads=[pp])
    kb.dma("sp", self.o["pool_sample"][i][7:15, :], psm[:], reads=[psm])
    kb.barrier()
    kb.bank_rr = 0
    mo = [kb.tile([128, T], BF16, "pmo") for _ in range(2)]
    for g in range(4):
        wv = self.i["pool_w"][i, g].rearrange("(kt p) n -> p kt n", p=128)

        def epi(n, si, c0, c1, ps, pb, g=g):
            m_ = mo[n % 2]
            f = g * 4 + n
            kb.op(V, lambda e: e.tensor_scalar(out=m_[:, c0:c1], in0=ps, scalar1=pscale[:, f:f + 1], scalar2=None, op0=ALU.mult), reads=[pb, pscale], writes=[m_])
            if si == len(SUBS) - 1:
                kb.dma("sp", self.MIX[2048 + f * 128:2048 + (f + 1) * 128, :], m_[:], reads=[m_])
        self.gemm(lambda n, wv=wv: wv[:, :, n * 128:(n + 1) * 128], 4, 4, lambda kt, c0, c1, g=g: ZP[:, g * 4 + kt, c0:c1], [ZP], epi, nw=2, wname=f"pw{g}")
    kb.phase_end()


Prog.phase_s5 = phase_s5
Prog.phase_pool = phase_pool


def build_all(self):
    self.phase_in()
    for l in range(self.depth):
        i = l // 2
        if l % 2 == 0:
            self.phase_win(self.i["norm_mix"][l], self.i["ev_w_in"][i], 4096)
            self.phase_s5(i)
            self.phase_pool(i)
            self.phase_wout(self.i["ev_w_out"][i])
        else:
            self.phase_win(self.i["norm_mix"][l], self.i["od_w_in"][i], 10240)
            self.phase_attn(i)
            self.phase_sgu(i)
            self.phase_wout(self.i["od_w_out"][i])
        self.phase_ffn(l)
    self.phase_out()
    self.kb.emit()


Prog.build_all = build_all


_CONSTS = None


def core_inputs(inp, c):
    global _CONSTS
    if _CONSTS is None:
        _CONSTS = make_consts()
    bp = c % 4
    f = lambda a: np.ascontiguousarray(a, dtype=np.float32)
    m = dict(
        x_prompt=f(inp["x_prompt"][bp]), x_sample=f(inp["x_sample"][c]),
        state_s5_re=f(inp["state_s5_re"][:, c]).reshape(2, 64, 128), state_s5_im=f(inp["state_s5_im"][:, c]).reshape(2, 64, 128),
        state_pool=f(inp["state_pool"][:, c]),
        cache_k=f(inp["cache_k"][:, c]), cache_v=f(inp["cache_v"][:, c]),
        s5_lambda_re=f(inp["s5_lambda_re"]).reshape(2, 64, 128), s5_lambda_im=f(inp["s5_lambda_im"]).reshape(2, 64, 128),
        s5_log_dt=f(inp["s5_log_dt"]).reshape(2, 64, 2),
        s5_b_re=f(inp["s5_b_re"]).reshape(2, 64, 128, 16), s5_b_im=f(inp["s5_b_im"]).reshape(2, 64, 128, 16),
        s5_c_re=f(inp["s5_c_re"]).reshape(2, 2048, 64), s5_c_im=f(inp["s5_c_im"]).reshape(2, 2048, 64),
        consts=_CONSTS,
    )
    for k in ("norm_mix", "norm_ffn", "ev_w_in", "ev_w_out", "s5_d", "s5_w_glu", "s5_b_glu", "pool_w", "pool_scale",
              "od_w_in", "od_w_out", "q_norm", "k_norm", "sgu_ln_g", "sgu_ln_b", "sgu_w", "sgu_b", "ffn_w1", "ffn_w3", "ffn_w2"):
        m[k] = f(inp[k])
    return m


def kernel(**inputs):
    n = 8
    prog = Prog(depth=4)
    prog.build_all()
    in_maps = [core_inputs(inputs, c) for c in range(n)]
    res = run_bass_kernel_spmd(prog.nc, in_maps, core_ids=list(range(n)))
    r = res.results

    def stack(name, cores, shape=None):
        a = np.stack([np.asarray(r[c][name], dtype=np.float32) for c in cores], axis=1)
        return a if shape is None else a.reshape(shape)

    pc = list(range(4))
    sc_ = list(range(8))
    return (
        stack("y_prompt", pc)[0] if False else np.stack([r[c]["y_prompt"] for c in pc], axis=0).astype(np.float32),
        np.stack([r[c]["y_sample"] for c in sc_], axis=0).astype(np.float32),
        stack("s5_re_prompt", pc, (2, 4, 128, 64)),
        stack("s5_im_prompt", pc, (2, 4, 128, 64)),
        stack("pool_prompt", pc),
        stack("k_prompt", pc),
        stack("v_prompt", pc),
        stack("s5_re_sample", sc_, (2, 8, 128, 64)),
        stack("s5_im_sample", sc_, (2, 8, 128, 64)),
        stack("pool_sample", sc_),
        stack("k_sample", sc_),
        stack("v_sample", sc_),
        stack("sgu_v_sample", sc_),
    )
```

```python
import math
import numpy as np
import concourse.bass as bass
import concourse.mybir as mybir
from concourse.bass_utils import run_bass_kernel_spmd

F32 = mybir.dt.float32
BF16 = mybir.dt.bfloat16
ALU = mybir.AluOpType
AF = mybir.ActivationFunctionType
AX = mybir.AxisListType
SEM_LIMIT = 30000
NSLOTS = 6
SB_BASE = 16512
SB_TOP = 229344

D = 4096
L = 2048
S = 8
T = L + S
DFF = 11008
NJ = DFF // 128
SUBS = [(0, 512), (512, 1024), (1024, 1536), (1536, 2048), (2048, 2056)]
ATT_SCALE = 128 ** -0.5
EXP_SHIFT = -8.0
ATT_HEADS = 16
ATT_Q = "sp"
ATT_T4 = 4
ATT_PARTS = ('norm', 'tr', 'trdma', 'trs', 'prompt', 'sample')
PI = math.pi


class Eng:
    def __init__(self, kb, name):
        self.kb = kb
        self.name = name
        self.epoch = 0
        self.sems = [kb.nc.alloc_semaphore(f"s_{name}_0")]
        self.n = 0
        self.waited = {}

    def bump(self, inc):
        if self.n + inc > SEM_LIMIT:
            self.epoch += 1
            self.sems.append(self.kb.nc.alloc_semaphore(f"s_{self.name}_{self.epoch}"))
            self.n = 0
        self.n += inc
        return (self, self.epoch, self.n)


class Buf:
    __slots__ = ("w", "r")

    def __init__(self):
        self.w = None
        self.r = {}


class Tile:
    def __init__(self, t):
        self.t = t
        self.b = Buf()

    def __getitem__(self, k):
        return self.t[k]


class KB:
    def __init__(self, nc):
        self.nc = nc
        self.ops = {k: [] for k in ("pe", "act", "dve", "pool", "sp")}
        self.engs = {k: Eng(self, k) for k in ("pe", "act", "dve", "pool", "sp")}
        self.dslots = {}
        self.dslot_rr = {}
        self.sb_off = SB_BASE
        self.sb_cnt = 0
        self.bank_rr = 0

    def tile(self, shape, dtype, name="t"):
        esz = 2 if dtype == BF16 else 4
        per_part = int(np.prod(shape[1:])) * esz
        off = (self.sb_off + 63) // 64 * 64
        self.sb_off = off + per_part
        assert self.sb_off <= SB_TOP, f"sbuf overflow {self.sb_off} ({name})"
        self.sb_cnt += 1
        return Tile(self.nc.alloc_sbuf_tensor_at(f"{name}_{self.sb_cnt}", list(shape), dtype, offset=off))

    def sb_mark(self):
        return self.sb_off

    def sb_reset(self, mark=SB_BASE):
        self.sb_off = mark

    def _waits(self, E, reads, writes):
        deps = {}

        def add(d):
            if d is None:
                return
            F, ep, c = d
            if deps.get((F, ep), 0) < c:
                deps[(F, ep)] = c

        for b in reads:
            add(b.w)
        for b in writes:
            add(b.w)
            for d in b.r.values():
                add(d)
        waits = []
        for (F, ep), c in deps.items():
            if F is E and E.name == "pe":
                continue
            done = False
            for (F2, ep2), c2 in E.waited.items():
                if F2 is F and (ep2 > ep or (ep2 == ep and c2 >= c)):
                    done = True
                    break
            if done:
                continue
            E.waited[(F, ep)] = c
            waits.append((F.sems[ep], c))
        return waits

    def _mark(self, tok, reads, writes):
        E = tok[0]
        for b in reads:
            b.r[E] = tok
        for b in writes:
            b.w = tok
            b.r = {}

    def op(self, ename, fn, reads=(), writes=()):
        reads = [x.b if isinstance(x, Tile) else x for x in reads]
        writes = [x.b if isinstance(x, Tile) else x for x in writes]
        E = self.engs[ename]
        waits = self._waits(E, reads, writes)
        tok = E.bump(1)
        sem = E.sems[tok[1]]

        def rec(e, waits=waits, fn=fn, sem=sem):
            for s, v in waits:
                e.wait_ge(s, v)
            fn(e).then_inc(sem, 1)

        self.ops[ename].append(rec)
        self._mark(tok, reads, writes)
        return tok

    def dma(self, qname, out, in_, reads=(), writes=(), nslots=None, **kw):
        nslots = nslots or NSLOTS
        reads = [x.b if isinstance(x, Tile) else x for x in reads]
        writes = [x.b if isinstance(x, Tile) else x for x in writes]
        Q = self.engs[qname]
        rr = self.dslot_rr.get(qname, 0)
        self.dslot_rr[qname] = rr + 1
        key = (qname, rr % nslots)
        if key not in self.dslots:
            self.dslots[key] = Eng(self, f"d{qname}{rr % nslots}")
        Dm = self.dslots[key]
        waits = self._waits(Q, reads, writes)
        if Dm.n > 0:
            k = (Dm, Dm.epoch)
            if Q.waited.get(k, 0) < Dm.n:
                Q.waited[k] = Dm.n
                waits.append((Dm.sems[Dm.epoch], Dm.n))
        tok = Dm.bump(16)
        sem = Dm.sems[tok[1]]

        def rec(e, waits=waits, sem=sem, out=out, in_=in_, kw=kw):
            for s, v in waits:
                e.wait_ge(s, v)
            e.dma_start(out=out, in_=in_, **kw).then_inc(sem, 16)

        self.ops[qname].append(rec)
        self._mark(tok, reads, writes)
        return tok

    def barrier(self):
        allE = list(self.engs.values()) + list(self.dslots.values())
        for ename, E in self.engs.items():
            waits = []
            for F in allE:
                if F is E or (F.n == 0 and F.epoch == 0):
                    continue
                k = (F, F.epoch)
                if E.waited.get(k, 0) < F.n:
                    E.waited[k] = F.n
                    waits.append((F.sems[F.epoch], F.n))
            if waits:
                def rec(e, waits=waits):
                    for s, v in waits:
                        e.wait_ge(s, v)
                self.ops[ename].append(rec)

    def phase_end(self):
        self.barrier()
        self.sb_reset(self.persist_mark)

    def emit(self):
        self.barrier()
        with self.nc.Block() as block:
            @block.tensor
            def _(e):
                for f in self.ops["pe"]:
                    f(e)

            @block.scalar
            def _(e):
                for f in self.ops["act"]:
                    f(e)

            @block.vector
            def _(e):
                for f in self.ops["dve"]:
                    f(e)

            @block.gpsimd
            def _(e):
                for f in self.ops["pool"]:
                    f(e)

            @block.sync
            def _(e):
                for f in self.ops["sp"]:
                    f(e)


C_ID = 0
C_WM = 128
C_WS = C_WM + 2560
C_WSN = C_WS + 128
C_TRIL = C_WSN + 8
C_RM = C_TRIL + 128
C_FIX = C_RM + 8
NCONST = C_FIX + 60


def _mult(d):
    d = np.asarray(d)
    m = ((d >= 0) & (d <= 128)).astype(np.float32)
    m += ((d >= 0) & (d <= 512) & (d % 4 == 0)).astype(np.float32)
    m += ((d >= 0) & (d <= 2048) & (d % 16 == 0)).astype(np.float32)
    return m


def make_consts():
    c = np.zeros((128, NCONST), np.float32)
    c[:, C_ID:C_ID + 128] = np.eye(128, dtype=np.float32)
    i = np.arange(128)[:, None]
    x = np.arange(2560)[None, :]
    c[:, C_WM:C_WM + 2560] = _mult(x - 384 - i)
    kt = (np.arange(128) // 8)[None, :]
    s = (np.arange(128) % 8)[None, :]
    c[:, C_WS:C_WS + 128] = _mult(2048 + s - kt * 128 - i)
    sp = np.arange(8)[:, None]
    sq = np.arange(8)[None, :]
    c[0:8, C_WSN:C_WSN + 8] = _mult(sq - sp)
    c[:, C_TRIL:C_TRIL + 128] = (np.arange(128)[None, :] <= i).astype(np.float32)
    c[:, C_RM:C_RM + 8] = ((i // 16) == np.arange(8)[None, :]).astype(np.float32)
    for g, w in enumerate((2, 4, 8, 16)):
        t = np.arange(15)
        c[:, C_FIX + g * 15:C_FIX + (g + 1) * 15] = (1.0 / np.minimum(t + 1, w) - 1.0 / w)[None, :]
    return c


class _Lazy(dict):
    def __init__(self, mk):
        super().__init__()
        self.mk = mk
        self.shapes = {}

    def __missing__(self, k):
        v = self.mk(k, self.shapes[k])
        self[k] = v
        return v


class Prog:
    def __init__(self, depth=4, dbg=False):
        self.depth = depth
        self.dbg = dbg
        nc = self.nc = bass.Bass("TRN2", target_bir_lowering=False)
        kb = self.kb = KB(nc)
        ne, no = 2, 2

        def din(name, shape):
            return nc.dram_tensor(name, list(shape), F32, kind="ExternalInput").ap()

        def dout(name, shape):
            return nc.dram_tensor(name, list(shape), F32, kind="ExternalOutput").ap()

        self.i = _Lazy(din)
        self.i.shapes = dict(
            x_prompt=(L, D), x_sample=(S, D),
            state_s5_re=(ne, 64, 128), state_s5_im=(ne, 64, 128),
            state_pool=(ne, 15, 2048),
            cache_k=(no, 2048, 16, 128), cache_v=(no, 2048, 16, 128),
            norm_mix=(4, D), norm_ffn=(4, D),
            ev_w_in=(ne, D, D), ev_w_out=(ne, D, D),
            s5_lambda_re=(ne, 64, 128), s5_lambda_im=(ne, 64, 128),
            s5_log_dt=(ne, 64, 2),
            s5_b_re=(ne, 64, 128, 16), s5_b_im=(ne, 64, 128, 16),
            s5_c_re=(ne, 2048, 64), s5_c_im=(ne, 2048, 64),
            s5_d=(ne, 2048), s5_w_glu=(ne, 2048, 2048), s5_b_glu=(ne, 2048),
            pool_w=(ne, 4, 512, 512), pool_scale=(ne, 2048),
            od_w_in=(no, D, 10240), od_w_out=(no, D, D),
            q_norm=(no, 128), k_norm=(no, 128),
            sgu_ln_g=(no, 2048), sgu_ln_b=(no, 2048),
            sgu_w=(no, 8, 128, 128), sgu_b=(no, 8, 128),
            ffn_w1=(4, D, DFF), ffn_w3=(4, D, DFF), ffn_w2=(4, DFF, D),
            consts=(128, NCONST),
        )
        self.o = dict(
            y_prompt=dout("y_prompt", (L, D)), y_sample=dout("y_sample", (S, D)),
            s5_re_prompt=dout("s5_re_prompt", (ne, 64, 128)), s5_im_prompt=dout("s5_im_prompt", (ne, 64, 128)),
            pool_prompt=dout("pool_prompt", (ne, 15, 2048)),
            k_prompt=dout("k_prompt", (no, 2048, 16, 128)), v_prompt=dout("v_prompt", (no, 2048, 16, 128)),
            s5_re_sample=dout("s5_re_sample", (ne, 64, 128)), s5_im_sample=dout("s5_im_sample", (ne, 64, 128)),
            pool_sample=dout("pool_sample", (ne, 15, 2048)),
            k_sample=dout("k_sample", (no, 8, 16, 128)), v_sample=dout("v_sample", (no, 8, 16, 128)),
            sgu_v_sample=dout("sgu_v_sample", (no, 8, 2048)),
        )
        kind = "ExternalOutput" if dbg else "Internal"
        self.X = nc.dram_tensor("Xs", [D, T], F32, kind=kind).ap()
        self.Z = nc.dram_tensor("Zs", [10240, T], F32, kind=kind).ap()
        self.MIX = nc.dram_tensor("MIXs", [D, T], BF16, kind="Internal").ap()
        self.ACTB = nc.dram_tensor("ACTs", [DFF, T], BF16, kind="Internal").ap()
        self.YG = nc.dram_tensor("YGs", [2048, T], BF16, kind="Internal").ap()
        self.psum = [nc.alloc_psum_tensor(f"ps{i}", [128, 512], F32) for i in range(8)]
        self.psb = [Buf() for _ in range(8)]

        self.cst = kb.tile([128, NCONST], F32, "cst")
        kb.dma("sp", self.cst[:], self.i["consts"], writes=[self.cst])
        self.ident = self.cst.t[:, C_ID:C_ID + 128]
        self.identb = kb.tile([128, 128], BF16, "identb")
        kb.op("dve", lambda e: e.tensor_copy(out=self.identb[:], in_=self.ident), reads=[self.cst], writes=[self.identb])
        self.ones_b = kb.tile([128, 128], BF16, "ones_b")
        kb.op("dve", lambda e: e.memset(self.ones_b[:], 1.0), writes=[self.ones_b])
        self.ones_f = kb.tile([128, 128], F32, "ones_f")
        kb.op("dve", lambda e: e.memset(self.ones_f[:], 1.0), writes=[self.ones_f])
        kb.persist_mark = kb.sb_mark()
        kb.barrier()

    def bank(self):
        b = self.kb.bank_rr % 8
        self.kb.bank_rr += 1
        return b

    def load_cols(self, dram_vec, n, name="col"):
        kb = self.kb
        if n == 1:
            out = kb.tile([128, 1], F32, name)
            kb.dma("sp", out[:], dram_vec.rearrange("(p o) -> p o", o=1), writes=[out])
            return out
        raw = kb.tile([n, 128], F32, name + "r")
        kb.dma("sp", raw[:], dram_vec.rearrange("(k p) -> k p", p=128), writes=[raw])
        out = kb.tile([128, n], F32, name)
        bk = self.bank()
        ps = self.psum[bk]
        kb.op("pe", lambda e: e.transpose(out=ps[:, 0:n], in_=raw[:], identity=self.ident[0:n, 0:n]),
              reads=[raw, self.cst], writes=[self.psb[bk]])
        kb.op("dve", lambda e: e.tensor_copy(out=out[:], in_=ps[:, 0:n]), reads=[self.psb[bk]], writes=[out])
        return out

    def gemm(self, wview, KT, n_tiles, rhs_fn, rhs_bufs, epilogue, subs=SUBS, nw=3, wname="w"):
        kb = self.kb
        wts = [kb.tile([128, KT, 128], BF16, wname) for _ in range(nw)]
        for n in range(n_tiles):
            wt = wts[n % nw]
            kb.dma("pool", wt[:], wview(n), writes=[wt])
            for si, (c0, c1) in enumerate(subs):
                bk = self.bank()
                ps = self.psum[bk]

                def mm(e, wt=wt, c0=c0, c1=c1, ps=ps):
                    for kt in range(KT):
                        ins = e.matmul(out=ps[:, 0:c1 - c0], lhsT=wt[:, kt, :], rhs=rhs_fn(kt, c0, c1),
                                       start=(kt == 0), stop=(kt == KT - 1))
                    return ins

                kb.op("pe", mm, reads=[wt] + list(rhs_bufs), writes=[self.psb[bk]])
                epilogue(n, si, c0, c1, ps[:, 0:c1 - c0], self.psb[bk])

    def phase_in(self):
        kb = self.kb
        Xv = self.X.rearrange("(kt p) t -> p kt t", p=128)
        xin = [kb.tile([128, D], F32, "xin") for _ in range(2)]
        xo = [kb.tile([128, 32, 128], F32, "xo") for _ in range(2)]
        for tt in range(16):
            xi = xin[tt % 2]
            xt = xo[tt % 2]
            kb.dma("sp", xi[:], self.i["x_prompt"][tt * 128:(tt + 1) * 128, :], writes=[xi])
            for k4 in range(8):
                bk = self.bank()
                ps = self.psum[bk]

                def tr(e, xi=xi, ps=ps, k4=k4):
                    for q in range(4):
                        kt = k4 * 4 + q
                        ins = e.transpose(out=ps[:, q * 128:(q + 1) * 128], in_=xi[:, kt * 128:(kt + 1) * 128], identity=self.ident)
                    return ins
                kb.op("pe", tr, reads=[xi, self.cst], writes=[self.psb[bk]])
                eng = "dve" if k4 % 2 == 0 else "act"
                if eng == "dve":
                    kb.op("dve", lambda e, xt=xt, ps=ps, k4=k4: e.tensor_copy(out=xt[:, k4 * 4:(k4 + 1) * 4, :], in_=ps[:].rearrange("p (a b) -> p a b", b=128)),
                          reads=[self.psb[bk]], writes=[xt])
                else:
                    kb.op("act", lambda e, xt=xt, ps=ps, k4=k4: e.copy(out=xt[:, k4 * 4:(k4 + 1) * 4, :], in_=ps[:].rearrange("p (a b) -> p a b", b=128)),
                          reads=[self.psb[bk]], writes=[xt])
            kb.dma("sp", Xv[:, :, tt * 128:(tt + 1) * 128], xt[:], reads=[xt])
        xs = kb.tile([S, D], F32, "xs")
        kb.dma("sp", xs[:], self.i["x_sample"], writes=[xs])
        xso = kb.tile([128, 32, S], F32, "xso")
        bk = self.bank()
        ps = self.psum[bk]

        def trs(e):
            for kt in range(32):
                ins = e.transpose(out=ps[:, kt * S:(kt + 1) * S], in_=xs[:, kt * 128:(kt + 1) * 128], identity=self.ident[0:S, 0:S])
            return ins
        kb.op("pe", trs, reads=[xs, self.cst], writes=[self.psb[bk]])
        kb.op("dve", lambda e: e.tensor_copy(out=xso[:], in_=ps[:, 0:32 * S].rearrange("p (a b) -> p a b", b=S)), reads=[self.psb[bk]], writes=[xso])
        kb.dma("sp", Xv[:, :, L:T], xso[:], reads=[xso])
        kb.phase_end()

    def phase_out(self):
        kb = self.kb
        Xv = self.X.rearrange("(kt p) t -> p kt t", p=128)
        xin = [kb.tile([128, 32, 128], F32, "yin") for _ in range(2)]
        yo = [kb.tile([128, D], F32, "yo") for _ in range(2)]
        for tt in range(16):
            xi = xin[tt % 2]
            yt = yo[tt % 2]
            kb.dma("sp", xi[:], Xv[:, :, tt * 128:(tt + 1) * 128], writes=[xi])
            for k4 in range(8):
                bk = self.bank()
                ps = self.psum[bk]

                def tr(e, xi=xi, ps=ps, k4=k4):
                    for q in range(4):
                        ins = e.transpose(out=ps[:, q * 128:(q + 1) * 128], in_=xi[:, k4 * 4 + q, :], identity=self.ident)
                    return ins
                kb.op("pe", tr, reads=[xi, self.cst], writes=[self.psb[bk]])
                if k4 % 2 == 0:
                    kb.op("dve", lambda e, yt=yt, ps=ps, k4=k4: e.tensor_copy(out=yt[:, k4 * 512:(k4 + 1) * 512], in_=ps[:]), reads=[self.psb[bk]], writes=[yt])
                else:
                    kb.op("act", lambda e, yt=yt, ps=ps, k4=k4: e.copy(out=yt[:, k4 * 512:(k4 + 1) * 512], in_=ps[:]), reads=[self.psb[bk]], writes=[yt])
            kb.dma("sp", self.o["y_prompt"][tt * 128:(tt + 1) * 128, :], yt[:], reads=[yt])
        xs = kb.tile([128, 32, S], F32, "ysin")
        kb.dma("sp", xs[:], Xv[:, :, L:T], writes=[xs])
        ys = kb.tile([S, D], F32, "ys")
        for k4 in range(8):
            bk = self.bank()
            ps = self.psum[bk]

            def trs(e, ps=ps, k4=k4):
                for q in range(4):
                    ins = e.transpose(out=ps[0:S, q * 128:(q + 1) * 128], in_=xs[:, k4 * 4 + q, :], identity=self.ident)
                return ins
            kb.op("pe", trs, reads=[xs, self.cst], writes=[self.psb[bk]])
            kb.op("dve", lambda e, ps=ps, k4=k4: e.tensor_copy(out=ys[:, k4 * 512:(k4 + 1) * 512], in_=ps[0:S, :]), reads=[self.psb[bk]], writes=[ys])
        kb.dma("sp", self.o["y_sample"], ys[:], reads=[ys])
        kb.phase_end()

    def norm_prologue(self, gamma_row):
        kb = self.kb
        Xv = self.X.rearrange("(kt p) t -> p kt t", p=128)
        g_t = self.load_cols(gamma_row, 32, "gam")
        xg = kb.tile([128, 32, T], BF16, "xg")
        rstd = kb.tile([128, T], F32, "rstd")
        pm_ = kb.sb_mark()
        xf = [kb.tile([128, T], F32, "xf") for _ in range(2)]
        sq = [kb.tile([128, T], BF16, "sq") for _ in range(2)]
        banks = [0, 1, 2, 3, 4]
        for kt in range(32):
            x_ = xf[kt % 2]
            s_ = sq[kt % 2]
            kb.dma("sp", x_[:], Xv[:, kt, :], writes=[x_])
            kb.op("act", lambda e, x_=x_, s_=s_: e.activation(out=s_[:], in_=x_[:], func=AF.Square), reads=[x_], writes=[s_])
            kb.op("dve", lambda e, x_=x_, kt=kt: e.tensor_scalar(out=xg[:, kt, :], in0=x_[:], scalar1=g_t[:, kt:kt + 1], scalar2=None, op0=ALU.mult),
                  reads=[x_, g_t], writes=[xg])

            def mm(e, s_=s_, kt=kt):
                for si, (c0, c1) in enumerate(SUBS):
                    ins = e.matmul(out=self.psum[banks[si]][:, 0:c1 - c0], lhsT=self.ones_b[:], rhs=s_[:, c0:c1], start=(kt == 0), stop=(kt == 31))
                return ins
            kb.op("pe", mm, reads=[s_, self.ones_b], writes=[self.psb[b] for b in banks])
        for si, (c0, c1) in enumerate(SUBS):
            b = banks[si]
            kb.op("act", lambda e, b=b, c0=c0, c1=c1: e.activation(out=rstd[:, c0:c1], in_=self.psum[b][:, 0:c1 - c0], func=AF.Sqrt, bias=1e-6, scale=1.0 / D),
                  reads=[self.psb[b]], writes=[rstd])
        kb.op("dve", lambda e: e.reciprocal(out=rstd[:], in_=rstd[:]), reads=[rstd], writes=[rstd])
        self.kb.bank_rr = 5
        kb.barrier()
        kb.sb_reset(pm_)
        return xg, rstd

    def phase_win(self, gamma_row, w, n_out):
        kb = self.kb
        xg, rstd = self.norm_prologue(gamma_row)
        wv = w.rearrange("(kt p) n -> p kt n", p=128)
        zo = [kb.tile([128, T], F32, "zo") for _ in range(2)]

        def epi(n, si, c0, c1, ps, pb):
            z_ = zo[n % 2]
            eng = "dve"
            kb.op(eng, lambda e: e.tensor_tensor(out=z_[:, c0:c1], in0=ps, in1=rstd[:, c0:c1], op=ALU.mult), reads=[pb, rstd], writes=[z_])
            if si == len(SUBS) - 1:
                kb.dma("sp", self.Z[n * 128:(n + 1) * 128, :], z_[:], reads=[z_])

        self.gemm(lambda n: wv[:, :, n * 128:(n + 1) * 128], 32, n_out // 128, lambda kt, c0, c1: xg[:, kt, c0:c1], [xg], epi)
        kb.phase_end()

    def phase_wout(self, w):
        kb = self.kb
        mx = kb.tile([128, 32, T], BF16, "mx")
        Mv = self.MIX.rearrange("(kt p) t -> p kt t", p=128)
        for q in range(4):
            kb.dma("sp", mx[:, q * 8:(q + 1) * 8, :], Mv[:, q * 8:(q + 1) * 8, :], writes=[mx])
        wv = w.rearrange("(kt p) n -> p kt n", p=128)
        self.resid_gemm(wv, 32, mx, SUBS)
        kb.phase_end()

    def resid_gemm(self, wv, KT, rhs_tile, subs, col_lo=0, col_hi=T, nw=3):
        kb = self.kb
        ncols = col_hi - col_lo
        xr = [kb.tile([128, ncols], F32, "xr") for _ in range(3)]
        xbufs = [Buf() for _ in range(32)]

        def wview(n):
            return wv[:, :, n * 128:(n + 1) * 128]

        def epi(n, si, c0, c1, ps, pb):
            x_ = xr[n % 3]
            if si == 0:
                kb.dma("sp", x_[:], self.X[n * 128:(n + 1) * 128, col_lo:col_hi], reads=[xbufs[n]], writes=[x_])
            kb.op("dve", lambda e: e.tensor_tensor(out=x_[:, c0:c1], in0=ps, in1=x_[:, c0:c1], op=ALU.add), reads=[pb, x_], writes=[x_])
            if si == len(subs) - 1:
                kb.dma("sp", self.X[n * 128:(n + 1) * 128, col_lo:col_hi], x_[:], reads=[x_], writes=[xbufs[n]])

        self.gemm(wview, KT, 32, lambda kt, c0, c1: rhs_tile[:, kt, c0:c1], [rhs_tile], epi, subs=subs, nw=nw)

    def phase_ffn(self, l):
        kb = self.kb
        xg, rstd = self.norm_prologue(self.i["norm_ffn"][l])
        w1v = self.i["ffn_w1"][l].rearrange("(kt p) n -> p kt n", p=128)
        w3v = self.i["ffn_w3"][l].rearrange("(kt p) n -> p kt n", p=128)
        sa = [kb.tile([128, T], F32, "sa") for _ in range(2)]
        ao = [kb.tile([128, T], BF16, "ao") for _ in range(2)]
        tb = [kb.tile([128, 512], F32, "tb") for _ in range(2)]
        cnt = [0]

        def epi(n, si, c0, c1, ps, pb):
            j, which = n // 2, n % 2
            s_ = sa[j % 2]
            a_ = ao[j % 2]
            if which == 0:
                kb.op("dve", lambda e: e.tensor_tensor(out=s_[:, c0:c1], in0=ps, in1=rstd[:, c0:c1], op=ALU.mult), reads=[pb, rstd], writes=[s_])
                kb.op("act", lambda e: e.activation(out=s_[:, c0:c1], in_=s_[:, c0:c1], func=AF.Silu), reads=[s_], writes=[s_])
            else:
                t_ = tb[cnt[0] % 2]
                cnt[0] += 1
                kb.op("dve", lambda e: e.tensor_tensor(out=t_[:, 0:c1 - c0], in0=ps, in1=rstd[:, c0:c1], op=ALU.mult), reads=[pb, rstd], writes=[t_])
                kb.op("pool", lambda e: e.tensor_tensor(out=a_[:, c0:c1], in0=t_[:, 0:c1 - c0], in1=s_[:, c0:c1], op=ALU.mult), reads=[t_, s_], writes=[a_])
                if si == len(SUBS) - 1:
                    kb.dma("sp", self.ACTB[j * 128:(j + 1) * 128, :], a_[:], reads=[a_])

        def wview(n):
            j, which = n // 2, n % 2
            return (w1v if which == 0 else w3v)[:, :, j * 128:(j + 1) * 128]

        self.gemm(wview, 32, 2 * NJ, lambda kt, c0, c1: xg[:, kt, c0:c1], [xg], epi)
        kb.phase_end()
        w2 = self.i["ffn_w2"][l]
        Av = self.ACTB.rearrange("(kt p) t -> p kt t", p=128)
        Xv = self.X.rearrange("(kt p) t -> p kt t", p=128)
        P = self.psum
        PB = self.psb
        NB = 8
        for (lo, hi) in [(0, 512), (512, 1024), (1024, 1536), (1536, T)]:
            has_s = (hi - lo) > 512
            cw = 7 if has_s else 8
            at = kb.tile([128, NJ, hi - lo], BF16, "at")
            for q in range(0, NJ, 22):
                q1 = min(NJ, q + 22)
                kb.dma("sp", at[:, q:q1, :], Av[:, q:q1, lo:hi], writes=[at])
            wts = [kb.tile([128, 8 * 128], BF16, "w2t") for _ in range(NB)]
            xrs = [kb.tile([128, 8, hi - lo], F32, "xr2") for _ in range(2)]
            zb = kb.tile([128, 128], BF16, "zb")
            kb.op("dve", lambda e, zb=zb: e.memset(zb[:], 0.0), writes=[zb])
            wi = 0
            for ci, n0 in enumerate(range(0, 32, cw)):
                n1 = min(32, n0 + cw)
                nn = n1 - n0
                xr = xrs[ci % 2]
                kb.dma("sp", xr[:, 0:nn, :], Xv[:, n0:n1, lo:hi], writes=[xr])
                if has_s:
                    kb.op("pe", lambda e, nn=nn, at=at, zb=zb: e.matmul(out=P[7][:, 0:nn * S], lhsT=zb[:], rhs=at[:, 0, 0:nn * S], start=True, stop=False, skip_group_check=True),
                          reads=[zb, at], writes=[PB[7]])
                for kt in range(NJ):
                    wt = wts[wi % NB]
                    wi += 1
                    kb.dma("pool", wt[:, 0:nn * 128], w2[kt * 128:(kt + 1) * 128, n0 * 128:n1 * 128], writes=[wt])

                    def mm(e, wt=wt, kt=kt, nn=nn, at=at, has_s=has_s):
                        for n in range(nn):
                            ins = e.matmul(out=P[n][:, 0:512], lhsT=wt[:, n * 128:(n + 1) * 128], rhs=at[:, kt, 0:512], start=(kt == 0), stop=(kt == NJ - 1))
                            if has_s:
                                ins = e.matmul(out=P[7][:, n * S:(n + 1) * S], lhsT=wt[:, n * 128:(n + 1) * 128], rhs=at[:, kt, 512:512 + S], start=False, stop=(kt == NJ - 1), skip_group_check=True)
                        return ins
                    kb.op("pe", mm, reads=[wt, at], writes=[PB[n] for n in range(nn)] + ([PB[7]] if has_s else []))
                for n in range(nn):
                    kb.op("dve", lambda e, n=n, xr=xr: e.tensor_tensor(out=xr[:, n, 0:512], in0=P[n][:, 0:512], in1=xr[:, n, 0:512], op=ALU.add), reads=[PB[n], xr], writes=[xr])
                if has_s:
                    kb.op("dve", lambda e, nn=nn, xr=xr: e.tensor_tensor(out=xr[:, 0:nn, 512:512 + S], in0=P[7][:, 0:nn * S].rearrange("p (a c) -> p a c", c=S), in1=xr[:, 0:nn, 512:512 + S], op=ALU.add),
                          reads=[PB[7], xr], writes=[xr])
                kb.dma("sp", Xv[:, n0:n1, lo:hi], xr[:, 0:nn, :], reads=[xr])
            kb.bank_rr = 0
            kb.phase_end()

def _stt(e, out, in0, scalar, in1, op0=ALU.mult, op1=ALU.mult):
    return e.scalar_tensor_tensor(out=out, in0=in0, scalar=scalar, in1=in1, op0=op0, op1=op1)


def phase_attn(self, i):
    kb = self.kb
    P = self.psum
    PB = self.psb
    qn = self.load_cols(self.i["q_norm"][i], 1, "qn")
    kn = self.load_cols(self.i["k_norm"][i], 1, "kn")
    qns = kb.tile([128, 1], F32, "qns")
    kb.op("dve", lambda e: e.tensor_scalar(out=qns[:], in0=qn[:], scalar1=ATT_SCALE, scalar2=None, op0=ALU.mult), reads=[qn], writes=[qns])
    wm = kb.tile([128, 2560], BF16, "wm")
    kb.op("dve", lambda e: e.tensor_copy(out=wm[:], in_=self.cst.t[:, C_WM:C_WM + 2560]), reads=[self.cst], writes=[wm])
    ws = kb.tile([128, 136], BF16, "ws")
    kb.op("dve", lambda e: e.tensor_copy(out=ws[:], in_=self.cst.t[:, C_WS:C_WS + 136]), reads=[self.cst], writes=[ws])
    kpv = self.o["k_prompt"][i].rearrange("(tt p) h d -> p tt h d", p=128)
    vpv = self.o["v_prompt"][i].rearrange("(tt p) h d -> p tt h d", p=128)
    ckv = self.i["cache_k"][i].rearrange("(tt p) h d -> p tt h d", p=128)
    cvv = self.i["cache_v"][i].rearrange("(tt p) h d -> p tt h d", p=128)
    qf = kb.tile([128, T], F32, "qf")
    kf = kb.tile([128, T], F32, "kf")
    vf = kb.tile([128, T], F32, "vf")
    sqb = kb.tile([128, T], BF16, "sqb")
    rq = kb.tile([128, T], F32, "rq")
    rk = kb.tile([128, T], F32, "rk")
    qb = kb.tile([128, T], BF16, "qb")
    knf = kb.tile([128, T], F32, "knf")
    kbt = kb.tile([128, T], BF16, "kbt")
    kout = kb.tile([128, 16, 128], F32, "kout")
    vout = kb.tile([128, 16, 128], F32, "vout")
    vtok = kb.tile([128, 16, 128], BF16, "vtok")
    ks = kb.tile([S, 128], F32, "ks")
    vs = kb.tile([S, 128], F32, "vs")
    vsb = kb.tile([S, 128], BF16, "vsb")
    pe_ = [kb.tile([128, 512], BF16, "pe") for _ in range(3)]
    pm_ = [kb.tile([128, 512], BF16, "pm") for _ in range(3)]
    rl = kb.tile([128, 512], F32, "rl")
    ao = [kb.tile([128, T], BF16, "ao") for _ in range(2)]
    ck = kb.tile([128, 16, 128], F32, "ck")
    cv = kb.tile([128, 16, 128], F32, "cv")
    vcb = kb.tile([128, 16, 128], BF16, "vcb")
    kcT = kb.tile([128, 2048], BF16, "kcT")
    srr = [0]

    def sbank():
        b = srr[0] % 4
        srr[0] += 1
        return b

    for hd in range(ATT_HEADS):
        a_ = ao[hd % 2]
        kb.dma("sp", qf[:], self.Z[hd * 128:(hd + 1) * 128, :], writes=[qf])
        kb.dma("sp", kf[:], self.Z[2048 + hd * 128:2048 + (hd + 1) * 128, :], writes=[kf])
        kb.dma("sp", vf[:], self.Z[4096 + hd * 128:4096 + (hd + 1) * 128, :], writes=[vf])
        for hh in range(2):
            kb.dma("sp", ck[:, hh * 8:(hh + 1) * 8, :], ckv[:, hh * 8:(hh + 1) * 8, hd, :], writes=[ck])
            kb.dma("sp", cv[:, hh * 8:(hh + 1) * 8, :], cvv[:, hh * 8:(hh + 1) * 8, hd, :], writes=[cv])
        if 'early' in ATT_PARTS:
            for hh in range(2):
                kb.dma(ATT_Q, kpv[:, hh * 8:(hh + 1) * 8, hd, :], ck[:, hh * 8:(hh + 1) * 8, :], reads=[ck])
        if 'xload' in ATT_PARTS:
            for hh in range(2):
                kb.dma(ATT_Q, vcb[:, hh * 8:(hh + 1) * 8, :].bitcast(F32)[:, :, 0:64] if False else kout[:, hh * 8:(hh + 1) * 8, :], ckv[:, hh * 8:(hh + 1) * 8, hd, :], writes=[kout])
        for (src, dst) in (((qf, rq), (kf, rk)) if 'norm' in ATT_PARTS else []):
            kb.op("act", lambda e, src=src: e.activation(out=sqb[:], in_=src[:], func=AF.Square), reads=[src], writes=[sqb])
            for (c0, c1) in SUBS:
                b = sbank()
                kb.op("pe", lambda e, b=b, c0=c0, c1=c1: e.matmul(out=P[b][:, 0:c1 - c0], lhsT=self.ones_b[:], rhs=sqb[:, c0:c1], start=True, stop=True),
                      reads=[sqb, self.ones_b], writes=[PB[b]])
                kb.op("act", lambda e, b=b, c0=c0, c1=c1, dst=dst: e.activation(out=dst[:, c0:c1], in_=P[b][:, 0:c1 - c0], func=AF.Sqrt, bias=1e-6, scale=1.0 / 128),
                      reads=[PB[b]], writes=[dst])
            kb.op("dve", lambda e, dst=dst: e.reciprocal(out=dst[:], in_=dst[:]), reads=[dst], writes=[dst])
        kb.op("dve", lambda e: _stt(e, qb[:], qf[:], qns[:, 0:1], rq[:]), reads=[qf, qns, rq], writes=[qb])
        kb.op("dve", lambda e: _stt(e, knf[:], kf[:], kn[:, 0:1], rk[:]), reads=[kf, kn, rk], writes=[knf])
        kb.op("act", lambda e: e.copy(out=kbt[:], in_=knf[:]), reads=[knf], writes=[kbt])
        if 'tr' not in ATT_PARTS:
            continue
        for (src, dstf, dstb) in (((knf, kout, None), (vf, vout, vtok)) if 'trk' not in ATT_PARTS else ((knf, kout, None),)):
            for t4 in range(int(ATT_T4)):
                b = sbank()

                def tr(e, b=b, src=src, t4=t4):
                    for q in range(4):
                        tt = t4 * 4 + q
                        ins = e.transpose(out=P[b][:, q * 128:(q + 1) * 128], in_=src[:, tt * 128:(tt + 1) * 128], identity=self.ident)
                    return ins
                kb.op("pe", tr, reads=[src, self.cst], writes=[PB[b]])
                if 'trnoevac' in ATT_PARTS:
                    continue
                kb.op("dve", lambda e, b=b, dstf=dstf, t4=t4: e.tensor_copy(out=dstf[:, t4 * 4:(t4 + 1) * 4, :], in_=P[b][:].rearrange("p (a c) -> p a c", c=128)),
                      reads=[PB[b]], writes=[dstf])
                if dstb is not None:
                    kb.op("act", lambda e, dstf=dstf, dstb=dstb, t4=t4: e.copy(out=dstb[:, t4 * 4:(t4 + 1) * 4, :], in_=dstf[:, t4 * 4:(t4 + 1) * 4, :]),
                          reads=[dstf], writes=[dstb])
        if 'trdma' in ATT_PARTS:
            for hh in range(2):
                if 'nok' not in ATT_PARTS:
                    src_t = ck if 'srcck' in ATT_PARTS else kout
                    dst_ap = self.Z[0:128, hh * 1024:(hh + 1) * 1024].rearrange("p (a b) -> p a b", b=128) if 'dstz' in ATT_PARTS else kpv[:, hh * 8:(hh + 1) * 8, hd, :]
                    kb.dma(ATT_Q, dst_ap, src_t[:, hh * 8:(hh + 1) * 8, :], reads=[src_t])
                if 'nov' not in ATT_PARTS:
                    kb.dma("sp", vpv[:, hh * 8:(hh + 1) * 8, hd, :], vout[:, hh * 8:(hh + 1) * 8, :], reads=[vout])
        if 'trs' not in ATT_PARTS:
            continue
        b = sbank()

        def trs(e, b=b):
            e.transpose(out=P[b][0:S, 0:128], in_=knf[:, L:T], identity=self.ident)
            return e.transpose(out=P[b][0:S, 128:256], in_=vf[:, L:T], identity=self.ident)
        kb.op("pe", trs, reads=[knf, vf, self.cst], writes=[PB[b]])
        kb.op("dve", lambda e, b=b: e.tensor_copy(out=ks[:], in_=P[b][0:S, 0:128]), reads=[PB[b]], writes=[ks])
        kb.op("dve", lambda e, b=b: e.tensor_copy(out=vs[:], in_=P[b][0:S, 128:256]), reads=[PB[b]], writes=[vs])
        kb.op("act", lambda e: e.copy(out=vsb[:], in_=vs[:]), reads=[vs], writes=[vsb])
        kb.dma("sp", self.o["k_sample"][i][:, hd, :], ks[:], reads=[ks])
        kb.dma("sp", self.o["v_sample"][i][:, hd, :], vs[:], reads=[vs])
        cnt = 0
        for qs in (range(4) if 'prompt' in ATT_PARTS else []):
            ob, lb = (4, 5) if qs % 2 == 0 else (6, 7)
            nk = 4 * qs + 4
            for kt in range(nk):
                b = sbank()
                p1 = pe_[cnt % 3]
                p2 = pm_[cnt % 3]
                cnt += 1
                off = 384 + qs * 512 - kt * 128
                kb.op("pe", lambda e, b=b, kt=kt, qs=qs: e.matmul(out=P[b][:], lhsT=kbt[:, kt * 128:(kt + 1) * 128], rhs=qb[:, qs * 512:(qs + 1) * 512], start=True, stop=True),
                      reads=[kbt, qb], writes=[PB[b]])
                kb.op("act", lambda e, b=b, p1=p1: e.activation(out=p1[:], in_=P[b][:], func=AF.Exp, bias=EXP_SHIFT, scale=1.0), reads=[PB[b]], writes=[p1])
                eng = "dve" if cnt % 3 != 0 else "pool"
                kb.op(eng, lambda e, p1=p1, p2=p2, off=off: e.tensor_tensor(out=p2[:], in0=p1[:], in1=wm[:, off:off + 512], op=ALU.mult), reads=[p1, wm], writes=[p2])

                def pv(e, p2=p2, kt=kt, nk=nk, ob=ob, lb=lb):
                    e.matmul(out=P[ob][:], lhsT=vtok[:, kt, :], rhs=p2[:], start=(kt == 0), stop=(kt == nk - 1))
                    return e.matmul(out=P[lb][:], lhsT=self.ones_b[:], rhs=p2[:], start=(kt == 0), stop=(kt == nk - 1))
                kb.op("pe", pv, reads=[p2, vtok, self.ones_b], writes=[PB[ob], PB[lb]])
            kb.op("dve", lambda e, lb=lb: e.reciprocal(out=rl[:], in_=P[lb][:]), reads=[PB[lb]], writes=[rl])
            kb.op("dve", lambda e, ob=ob, qs=qs, a_=a_: e.tensor_tensor(out=a_[:, qs * 512:(qs + 1) * 512], in0=P[ob][:], in1=rl[:], op=ALU.mult), reads=[PB[ob], rl], writes=[a_])
        if 'sample' not in ATT_PARTS:
            continue
        kb.op("pool", lambda e: e.tensor_copy(out=vcb[:], in_=cv[:]), reads=[cv], writes=[vcb])
        for t4 in range(4):
            b = sbank()

            def trc(e, b=b, t4=t4):
                for q in range(4):
                    ins = e.transpose(out=P[b][:, q * 128:(q + 1) * 128], in_=ck[:, t4 * 4 + q, :], identity=self.ident)
                return ins
            kb.op("pe", trc, reads=[ck, self.cst], writes=[PB[b]])
            kb.op("act", lambda e, b=b, t4=t4: e.copy(out=kcT[:, t4 * 512:(t4 + 1) * 512], in_=P[b][:]), reads=[PB[b]], writes=[kcT])
        b = sbank()

        def sc(e, b=b):
            for kt in range(16):
                e.matmul(out=P[b][:, kt * S:(kt + 1) * S], lhsT=kcT[:, kt * 128:(kt + 1) * 128], rhs=qb[:, L:T], start=True, stop=True)
            return e.matmul(out=P[b][0:S, 128:128 + S], lhsT=kbt[:, L:T], rhs=qb[:, L:T], start=True, stop=True)
        kb.op("pe", sc, reads=[kcT, kbt, qb], writes=[PB[b]])
        p1 = pe_[cnt % 3]
        p2 = pm_[cnt % 3]
        cnt += 1
        kb.op("act", lambda e, b=b, p1=p1: e.activation(out=p1[:, 0:128], in_=P[b][:, 0:128], func=AF.Exp, bias=EXP_SHIFT, scale=1.0), reads=[PB[b]], writes=[p1])
        kb.op("act", lambda e, b=b, p1=p1: e.activation(out=p1[0:S, 128:136], in_=P[b][0:S, 128:136], func=AF.Exp, bias=EXP_SHIFT, scale=1.0), reads=[PB[b]], writes=[p1])
        kb.op("dve", lambda e, p1=p1, p2=p2: e.tensor_tensor(out=p2[:, 0:128], in0=p1[:, 0:128], in1=ws[:, 0:128], op=ALU.mult), reads=[p1, ws], writes=[p2])
        kb.op("dve", lambda e, p1=p1, p2=p2: e.tensor_tensor(out=p2[0:S, 128:136], in0=p1[0:S, 128:136], in1=ws[0:S, 128:136], op=ALU.mult), reads=[p1, ws], writes=[p2])
        ob, lb = 4, 5

        def pvs(e, p2=p2):
            for kt in range(16):
                e.matmul(out=P[ob][:, 0:S], lhsT=vcb[:, kt, :], rhs=p2[:, kt * S:(kt + 1) * S], start=(kt == 0), stop=False)
            e.matmul(out=P[ob][:, 0:S], lhsT=vsb[:], rhs=p2[0:S, 128:136], start=False, stop=True)
            for kt in range(16):
                e.matmul(out=P[lb][:, 0:S], lhsT=self.ones_b[:], rhs=p2[:, kt * S:(kt + 1) * S], start=(kt == 0), stop=False)
            return e.matmul(out=P[lb][:, 0:S], lhsT=self.ones_b[0:S, :], rhs=p2[0:S, 128:136], start=False, stop=True)
        kb.op("pe", pvs, reads=[p2, vcb, vsb, self.ones_b], writes=[PB[ob], PB[lb]])
        kb.op("dve", lambda e: e.reciprocal(out=rl[:, 0:S], in_=P[lb][:, 0:S]), reads=[PB[lb]], writes=[rl])
        kb.op("dve", lambda e, a_=a_: e.tensor_tensor(out=a_[:, L:T], in0=P[ob][:, 0:S], in1=rl[:, 0:S], op=ALU.mult), reads=[PB[ob], rl], writes=[a_])
        kb.dma("sp", self.MIX[hd * 128:(hd + 1) * 128, :], a_[:], reads=[a_])
    kb.bank_rr = 0
    kb.phase_end()


def phase_sgu(self, i):
    kb = self.kb
    P = self.psum
    PB = self.psb
    lg = self.load_cols(self.i["sgu_ln_g"][i], 16, "lg")
    lbv = self.load_cols(self.i["sgu_ln_b"][i], 16, "lb")
    tril = self.cst.t[:, C_TRIL:C_TRIL + 128]
    wst = kb.tile([128, 8, 128], BF16, "wst")
    wraw = [kb.tile([128, 128], F32, "wraw") for _ in range(2)]
    for g in range(8):
        w_ = wraw[g % 2]
        kb.dma("sp", w_[:], self.i["sgu_w"][i, g], writes=[w_])
        kb.op("dve", lambda e, w_=w_: e.tensor_tensor(out=w_[:], in0=w_[:], in1=tril, op=ALU.mult), reads=[w_, self.cst], writes=[w_])
        b = g % 4
        kb.op("pe", lambda e, b=b, w_=w_: e.transpose(out=P[b][:, 0:128], in_=w_[:], identity=self.ident), reads=[w_, self.cst], writes=[PB[b]])
        kb.op("act", lambda e, b=b, g=g: e.copy(out=wst[:, g, :], in_=P[b][:, 0:128]), reads=[PB[b]], writes=[wst])
    sbr = kb.tile([1, 1024], F32, "sbr")
    kb.dma("sp", sbr[:], self.i["sgu_b"][i].rearrange("g t -> (g t)").rearrange("(o n) -> o n", o=1), writes=[sbr])
    sbb = kb.tile([1, 1024], BF16, "sbb")
    kb.op("dve", lambda e: e.tensor_copy(out=sbb[:], in_=sbr[:]), reads=[sbr], writes=[sbb])
    GUG = kb.tile([128, 16, T], BF16, "GUG")
    VN = kb.tile([128, 16, T], BF16, "VN")
    gsf = kb.tile([128, 16, S], F32, "gsf")
    vns = kb.tile([128, 16, S], F32, "vns")
    sgu_mark = kb.sb_mark()
    gvf = [kb.tile([128, T], F32, "gvf") for _ in range(2)]
    sqf = [kb.tile([128, L], F32, "sqf") for _ in range(2)]
    guf = [kb.tile([128, T], F32, "guf") for _ in range(2)]
    for ft in range(16):
        g_ = gvf[ft % 2]
        s_ = sqf[ft % 2]
        u_ = guf[ft % 2]
        kb.dma("sp", g_[:], self.Z[8192 + ft * 128:8192 + (ft + 1) * 128, :], writes=[g_])
        kb.dma("sp", u_[:], self.Z[6144 + ft * 128:6144 + (ft + 1) * 128, :], writes=[u_])
        kb.op("act", lambda e, g_=g_: e.activation(out=g_[:], in_=g_[:], func=AF.Gelu), reads=[g_], writes=[g_])
        kb.op("dve", lambda e, g_=g_, ft=ft: e.tensor_copy(out=VN[:, ft, :], in_=g_[:]), reads=[g_], writes=[VN])
        kb.op("dve", lambda e, g_=g_, ft=ft: e.tensor_copy(out=gsf[:, ft, :], in_=g_[:, L:T]), reads=[g_], writes=[gsf])
        kb.op("pool", lambda e, g_=g_, s_=s_: e.tensor_tensor(out=s_[:], in0=g_[:, 0:L], in1=g_[:, 0:L], op=ALU.mult), reads=[g_], writes=[s_])

        def st(e, g_=g_, s_=s_, ft=ft):
            for si in range(4):
                c0, c1 = SUBS[si]
                e.matmul(out=P[si][:], lhsT=self.ones_f[:], rhs=g_[:, c0:c1], start=(ft == 0), stop=(ft == 15))
                ins = e.matmul(out=P[4 + si][:], lhsT=self.ones_f[:], rhs=s_[:, c0:c1], start=(ft == 0), stop=(ft == 15))
            return ins
        kb.op("pe", st, reads=[g_, s_, self.ones_f], writes=PB)
        kb.op("act", lambda e, u_=u_, ft=ft: e.activation(out=GUG[:, ft, :], in_=u_[:], func=AF.Gelu), reads=[u_], writes=[GUG])
    kb.barrier()
    kb.sb_reset(sgu_mark)
    mu = kb.tile([128, L], F32, "mu")
    rs = kb.tile([128, L], F32, "rs")
    nmr = kb.tile([128, L], F32, "nmr")
    for si in range(4):
        c0, c1 = SUBS[si]
        kb.op("act", lambda e, si=si, c0=c0, c1=c1: e.activation(out=mu[:, c0:c1], in_=P[si][:], func=AF.Copy, scale=1.0 / 2048), reads=[PB[si]], writes=[mu])
        kb.op("dve", lambda e, c0=c0, c1=c1: e.tensor_tensor(out=nmr[:, c0:c1], in0=mu[:, c0:c1], in1=mu[:, c0:c1], op=ALU.mult), reads=[mu], writes=[nmr])
        kb.op("dve", lambda e, si=si, c0=c0, c1=c1: _stt(e, rs[:, c0:c1], P[4 + si][:], 1.0 / 2048, nmr[:, c0:c1], ALU.mult, ALU.subtract), reads=[PB[4 + si], nmr], writes=[rs])
    kb.op("act", lambda e: e.activation(out=rs[:], in_=rs[:], func=AF.Sqrt, bias=1e-5, scale=1.0), reads=[rs], writes=[rs])
    kb.op("dve", lambda e: e.reciprocal(out=rs[:], in_=rs[:]), reads=[rs], writes=[rs])
    kb.op("dve", lambda e: _stt(e, nmr[:], mu[:], -1.0, rs[:]), reads=[mu, rs], writes=[nmr])
    tmp = [kb.tile([128, L], F32, "ntmp") for _ in range(2)]
    for ft in range(16):
        t_ = tmp[ft % 2]
        kb.op("dve", lambda e, t_=t_, ft=ft: e.tensor_tensor(out=t_[:], in0=VN[:, ft, 0:L], in1=rs[:], op=ALU.mult), reads=[VN, rs], writes=[t_])
        kb.op("pool", lambda e, t_=t_: e.tensor_tensor(out=t_[:], in0=t_[:], in1=nmr[:], op=ALU.add), reads=[t_, nmr], writes=[t_])
        kb.op("dve", lambda e, t_=t_, ft=ft: e.tensor_scalar(out=VN[:, ft, 0:L], in0=t_[:], scalar1=lg[:, ft:ft + 1], scalar2=lbv[:, ft:ft + 1], op0=ALU.mult, op1=ALU.add),
              reads=[t_, lg, lbv], writes=[VN])
    sqs = kb.tile([128, 16, S], F32, "sqs")
    kb.op("dve", lambda e: e.tensor_tensor(out=sqs[:], in0=gsf[:], in1=gsf[:], op=ALU.mult), reads=[gsf], writes=[sqs])

    def sts(e):
        for ft in range(16):
            e.matmul(out=P[0][:, 0:S], lhsT=self.ones_f[:], rhs=gsf[:, ft, :], start=(ft == 0), stop=(ft == 15))
        for ft in range(16):
            ins = e.matmul(out=P[1][:, 0:S], lhsT=self.ones_f[:], rhs=sqs[:, ft, :], start=(ft == 0), stop=(ft == 15))
        return ins
    kb.op("pe", sts, reads=[gsf, sqs, self.ones_f], writes=[PB[0], PB[1]])
    mus = kb.tile([128, S], F32, "mus")
    rss = kb.tile([128, S], F32, "rss")
    nms = kb.tile([128, S], F32, "nms")
    kb.op("act", lambda e: e.activation(out=mus[:], in_=P[0][:, 0:S], func=AF.Copy, scale=1.0 / 2048), reads=[PB[0]], writes=[mus])
    kb.op("dve", lambda e: e.tensor_tensor(out=nms[:], in0=mus[:], in1=mus[:], op=ALU.mult), reads=[mus], writes=[nms])
    kb.op("dve", lambda e: _stt(e, rss[:], P[1][:, 0:S], 1.0 / 2048, nms[:], ALU.mult, ALU.subtract), reads=[PB[1], nms], writes=[rss])
    kb.op("act", lambda e: e.activation(out=rss[:], in_=rss[:], func=AF.Sqrt, bias=1e-5, scale=1.0), reads=[rss], writes=[rss])
    kb.op("dve", lambda e: e.reciprocal(out=rss[:], in_=rss[:]), reads=[rss], writes=[rss])
    kb.op("dve", lambda e: _stt(e, nms[:], mus[:], -1.0, rss[:]), reads=[mus, rss], writes=[nms])
    for ft in range(16):
        kb.op("dve", lambda e, ft=ft: e.tensor_tensor(out=vns[:, ft, :], in0=gsf[:, ft, :], in1=rss[:], op=ALU.mult), reads=[gsf, rss], writes=[vns])
        kb.op("dve", lambda e, ft=ft: e.tensor_tensor(out=vns[:, ft, :], in0=vns[:, ft, :], in1=nms[:], op=ALU.add), reads=[vns, nms], writes=[vns])
        kb.op("dve", lambda e, ft=ft: e.tensor_scalar(out=vns[:, ft, :], in0=vns[:, ft, :], scalar1=lg[:, ft:ft + 1], scalar2=lbv[:, ft:ft + 1], op0=ALU.mult, op1=ALU.add),
              reads=[vns, lg, lbv], writes=[vns])
    kb.op("dve", lambda e: e.tensor_copy(out=VN[:, :, L:T], in_=vns[:]), reads=[vns], writes=[VN])
    ysv = kb.tile([S, 2048], F32, "ysv")
    for f4 in range(4):
        b = 2 + (f4 % 2)

        def trv(e, b=b, f4=f4):
            for q in range(4):
                ins = e.transpose(out=P[b][0:S, q * 128:(q + 1) * 128], in_=vns[:, f4 * 4 + q, :], identity=self.ident)
            return ins
        kb.op("pe", trv, reads=[vns, self.cst], writes=[PB[b]])
        kb.op("dve", lambda e, b=b, f4=f4: e.tensor_copy(out=ysv[:, f4 * 512:(f4 + 1) * 512], in_=P[b][0:S, :]), reads=[PB[b]], writes=[ysv])
    kb.dma("sp", self.o["sgu_v_sample"][i], ysv[:], reads=[ysv])
    kb.barrier()
    kb.sb_reset(sgu_mark)
    Mv = self.MIX.rearrange("(kt p) t -> p kt t", p=128)
    vtok = [kb.tile([128, 2048], BF16, "vtk") for _ in range(2)]
    mo = [kb.tile([128, 16, 128], BF16, "mo") for _ in range(2)]
    for ch in range(16):
        vt = vtok[ch % 2]
        m_ = mo[ch % 2]
        for h in range(2):
            b = h
            pbf = P[b][:].bitcast(BF16)

            def trn(e, pbf=pbf, h=h, ch=ch):
                for q in range(8):
                    ins = e.transpose(out=pbf[:, q * 128:(q + 1) * 128], in_=VN[:, h * 8 + q, ch * 128:(ch + 1) * 128], identity=self.identb[:])
                return ins
            kb.op("pe", trn, reads=[VN, self.identb], writes=[PB[b]])
            kb.op("act", lambda e, pbf=pbf, vt=vt, h=h: e.copy(out=vt[:, h * 1024:(h + 1) * 1024], in_=pbf), reads=[PB[b]], writes=[vt])
        for f4 in range(4):
            b = 2 + (f4 + 4 * ch) % 6

            def mx(e, b=b, f4=f4, vt=vt):
                for q in range(4):
                    ft = f4 * 4 + q
                    g = ft // 2
                    e.matmul(out=P[b][:, q * 128:(q + 1) * 128], lhsT=vt[:, ft * 128:(ft + 1) * 128], rhs=wst[:, g, :], start=True, stop=False)
                    ins = e.matmul(out=P[b][:, q * 128:(q + 1) * 128], lhsT=self.ones_b[0:1, :], rhs=sbb[0:1, g * 128:(g + 1) * 128], start=False, stop=True)
                return ins
            kb.op("pe", mx, reads=[vt, wst, sbb, self.ones_b], writes=[PB[b]])
            kb.op("dve", lambda e, b=b, f4=f4, m_=m_, ch=ch: e.tensor_tensor(out=m_[:, f4 * 4:(f4 + 1) * 4, :], in0=P[b][:].rearrange("p (a c) -> p a c", c=128),
                                                                  in1=GUG[:, f4 * 4:(f4 + 1) * 4, ch * 128:(ch + 1) * 128], op=ALU.mult),
                  reads=[PB[b], GUG], writes=[m_])
        kb.dma("sp", Mv[:, 16:32, ch * 128:(ch + 1) * 128], m_[:], reads=[m_])
    vts = kb.tile([S, 2048], BF16, "vts")
    for h in range(2):
        b = h
        pbf = P[b][:].bitcast(BF16)

        def trn2(e, pbf=pbf, h=h):
            for q in range(8):
                ins = e.transpose(out=pbf[0:S, q * 128:(q + 1) * 128], in_=VN[:, h * 8 + q, L:T], identity=self.identb[:])
            return ins
        kb.op("pe", trn2, reads=[VN, self.identb], writes=[PB[b]])
        kb.op("act", lambda e, pbf=pbf, h=h: e.copy(out=vts[:, h * 1024:(h + 1) * 1024], in_=pbf[0:S, :]), reads=[PB[b]], writes=[vts])
    mos = kb.tile([128, 16, S], BF16, "mos")
    b = 2

    def mxs(e):
        for ft in range(16):
            g = ft // 2
            e.matmul(out=P[b][:, ft * S:(ft + 1) * S], lhsT=vts[0:S, ft * 128:(ft + 1) * 128], rhs=wst[0:S, g, 0:S], start=True, stop=False)
            ins = e.matmul(out=P[b][:, ft * S:(ft + 1) * S], lhsT=self.ones_b[0:1, :], rhs=sbb[0:1, g * 128:g * 128 + S], start=False, stop=True)
        return ins
    kb.op("pe", mxs, reads=[vts, wst, sbb, self.ones_b], writes=[PB[b]])
    kb.op("dve", lambda e: e.tensor_tensor(out=mos[:], in0=P[b][:, 0:16 * S].rearrange("p (a c) -> p a c", c=S), in1=GUG[:, :, L:T], op=ALU.mult), reads=[PB[b], GUG], writes=[mos])
    kb.dma("sp", Mv[:, 16:32, L:T], mos[:], reads=[mos])
    kb.bank_rr = 0
    kb.phase_end()


Prog.phase_attn = phase_attn
Prog.phase_sgu = phase_sgu


NK = 11


def phase_s5(self, i):
    kb = self.kb
    P = self.psum
    PB = self.psb
    V = "dve"
    NP = 5 + 2 * NK
    PT = kb.tile([128, NP + NK + 1, 64], F32, "PT")
    prep_mark = kb.sb_mark()

    def t64(name):
        return kb.tile([64, 128], F32, name)
    lre, lim, h0r, h0i = t64("lre"), t64("lim"), t64("h0r"), t64("h0i")
    ldt = kb.tile([64, 2], F32, "ldt")
    kb.dma("sp", lre[:], self.i["s5_lambda_re"][i], writes=[lre])
    kb.dma("sp", lim[:], self.i["s5_lambda_im"][i], writes=[lim])
    kb.dma("sp", ldt[:], self.i["s5_log_dt"][i], writes=[ldt])
    kb.dma("sp", h0r[:], self.i["state_s5_re"][i], writes=[h0r])
    kb.dma("sp", h0i[:], self.i["state_s5_im"][i], writes=[h0i])
    dt = kb.tile([64, 2], F32, "dt")
    kb.op("act", lambda e: e.activation(out=dt[:], in_=ldt[:], func=AF.Exp), reads=[ldt], writes=[dt])
    are, aim, mag, thr, tmp, sn, sh, cs = [t64(n) for n in ("are", "aim", "mag", "thr", "tmp", "sn", "sh", "cs")]
    for gl in range(2):
        sl = slice(gl * 64, (gl + 1) * 64)
        kb.op(V, lambda e, sl=sl, gl=gl: e.tensor_scalar(out=are[:, sl], in0=lre[:, sl], scalar1=dt[:, gl:gl + 1], scalar2=None, op0=ALU.mult), reads=[lre, dt], writes=[are])
        kb.op(V, lambda e, sl=sl, gl=gl: e.tensor_scalar(out=aim[:, sl], in0=lim[:, sl], scalar1=dt[:, gl:gl + 1], scalar2=None, op0=ALU.mult), reads=[lim, dt], writes=[aim])
    kb.op("act", lambda e: e.activation(out=mag[:], in_=are[:], func=AF.Exp), reads=[are], writes=[mag])
    kb.op(V, lambda e: e.tensor_copy(out=thr[:], in_=aim[:]), reads=[aim], writes=[thr])
    for m in range(1, 9):
        kb.op(V, lambda e, m=m: e.tensor_scalar(out=tmp[:], in0=aim[:], scalar1=(2 * m - 1) * PI, scalar2=-2.0 * PI, op0=ALU.is_ge, op1=ALU.mult), reads=[aim], writes=[tmp])
        kb.op(V, lambda e: e.tensor_tensor(out=thr[:], in0=thr[:], in1=tmp[:], op=ALU.add), reads=[thr, tmp], writes=[thr])
    kb.op(V, lambda e: e.tensor_scalar(out=thr[:], in0=thr[:], scalar1=-PI, scalar2=PI, op0=ALU.max, op1=ALU.min), reads=[thr], writes=[thr])
    kb.op("act", lambda e: e.activation(out=sn[:], in_=thr[:], func=AF.Sin), reads=[thr], writes=[sn])
    kb.op("act", lambda e: e.activation(out=sh[:], in_=thr[:], func=AF.Sin, scale=0.5), reads=[thr], writes=[sh])
    kb.op(V, lambda e: e.tensor_tensor(out=tmp[:], in0=sh[:], in1=sh[:], op=ALU.mult), reads=[sh], writes=[tmp])
    kb.op(V, lambda e: e.tensor_scalar(out=cs[:], in0=tmp[:], scalar1=-2.0, scalar2=1.0, op0=ALU.mult, op1=ALU.add), reads=[tmp], writes=[cs])
    lbr, lbi, nr, dd, fr, fi, t2 = [t64(n) for n in ("lbr", "lbi", "nr", "dd", "fr", "fi", "t2")]

    def tt(out, a, b, op):
        kb.op(V, lambda e: e.tensor_tensor(out=out[:], in0=a[:], in1=b[:], op=op), reads=[a, b], writes=[out])
    tt(lbr, mag, cs, ALU.mult)
    tt(lbi, mag, sn, ALU.mult)
    kb.op(V, lambda e: e.tensor_scalar(out=nr[:], in0=lbr[:], scalar1=-1.0, scalar2=None, op0=ALU.add), reads=[lbr], writes=[nr])
    tt(dd, lre, lre, ALU.mult)
    tt(tmp, lim, lim, ALU.mult)
    tt(dd, dd, tmp, ALU.add)
    kb.op(V, lambda e: e.reciprocal(out=dd[:], in_=dd[:]), reads=[dd], writes=[dd])
    tt(fr, nr, lre, ALU.mult)
    tt(tmp, lbi, lim, ALU.mult)
    tt(fr, fr, tmp, ALU.add)
    tt(fr, fr, dd, ALU.mult)
    tt(fi, lbi, lre, ALU.mult)
    tt(tmp, nr, lim, ALU.mult)
    tt(fi, fi, tmp, ALU.subtract)
    tt(fi, fi, dd, ALU.mult)
    inr, ini = t64("inr"), t64("ini")
    tt(inr, cs, h0r, ALU.mult)
    tt(tmp, sn, h0i, ALU.mult)
    tt(inr, inr, tmp, ALU.subtract)
    tt(ini, sn, h0r, ALU.mult)
    tt(tmp, cs, h0i, ALU.mult)
    tt(ini, ini, tmp, ALU.add)
    cks = [cs] + [t64(f"ck{k}") for k in range(1, NK)]
    sks = [sn] + [t64(f"sk{k}") for k in range(1, NK)]
    for k in range(NK - 1):
        tt(cks[k + 1], cks[k], cks[k], ALU.mult)
        tt(tmp, sks[k], sks[k], ALU.mult)
        tt(cks[k + 1], cks[k + 1], tmp, ALU.subtract)
        tt(t2, cks[k], sks[k], ALU.mult)
        kb.op(V, lambda e, k=k: e.tensor_scalar(out=sks[k + 1][:], in0=t2[:], scalar1=2.0, scalar2=None, op0=ALU.mult), reads=[t2], writes=[sks[k + 1]])
    srcs = [mag, fr, fi, inr, ini] + cks + sks
    assert NP == len(srcs)
    I_MAG, I_FR, I_FI, I_INR, I_INI, I_CK, I_SK = 0, 1, 2, 3, 4, 5, 5 + NK
    I_NSK = NP
    I_NFI = NP + NK
    for a in range(0, NP, 8):
        b = self.bank()
        grp = srcs[a:a + 8]

        def trp(e, b=b, grp=grp):
            for q, sx in enumerate(grp):
                ins = e.transpose(out=P[b][:, q * 64:(q + 1) * 64], in_=sx[:], identity=self.ident[0:64, 0:64])
            return ins
        kb.op("pe", trp, reads=grp + [self.cst], writes=[PB[b]])
        kb.op(V, lambda e, b=b, a=a, n=len(grp): e.tensor_copy(out=PT[:, a:a + n, :], in_=P[b][:, 0:n * 64].rearrange("p (a c) -> p a c", c=64)), reads=[PB[b]], writes=[PT])
    kb.op(V, lambda e: e.tensor_scalar(out=PT[:, I_NSK:I_NSK + NK, :], in0=PT[:, I_SK:I_SK + NK, :], scalar1=-1.0, scalar2=None, op0=ALU.mult), reads=[PT], writes=[PT])
    kb.op(V, lambda e: e.tensor_scalar(out=PT[:, I_NFI, :], in0=PT[:, I_FI, :], scalar1=-1.0, scalar2=None, op0=ALU.mult), reads=[PT], writes=[PT])
    kb.barrier()
    kb.sb_reset(prep_mark)
    dcol = self.load_cols(self.i["s5_d"][i], 16, "dcol")
    Ball = [kb.tile([128, 64, 16], F32, "Ball") for _ in range(2)]
    Call = [kb.tile([128, 16, 64], F32, "Call") for _ in range(2)]
    kb.dma("sp", Ball[0][:], self.i["s5_b_re"][i].rearrange("j q h -> q j h"), writes=[Ball[0]])
    kb.dma("sp", Ball[1][:], self.i["s5_b_im"][i].rearrange("j q h -> q j h"), writes=[Ball[1]])
    kb.dma("sp", Call[0][:], self.i["s5_c_re"][i].rearrange("(kt r) p -> r kt p", r=128), writes=[Call[0]])
    kb.dma("sp", Call[1][:], self.i["s5_c_im"][i].rearrange("(kt r) p -> r kt p", r=128), writes=[Call[1]])
    rm = self.cst.t[:, C_RM:C_RM + 8]
    preB = [[kb.tile([128, 128], F32, "preB") for _ in range(2)] for _ in range(4)]
    for a in range(4):
        for c in range(2):
            kb.op(V, lambda e, a=a, c=c: e.memset(preB[a][c][:], 0.0), writes=[preB[a][c]])
    preC = [kb.tile([128, 128], F32, "preC") for _ in range(2)]
    tb16 = kb.tile([128, 16], F32, "tb16")
    LB = [kb.tile([128, 4, 2, 128], BF16, "LB") for _ in range(2)]
    LC = [kb.tile([128, 4, 2, 128], BF16, "LC") for _ in range(2)]
    Er = [kb.tile([128, T], F32, "Er") for _ in range(2)]
    Ei = [kb.tile([128, T], F32, "Ei") for _ in range(2)]
    et1 = kb.tile([128, 4096], F32, "et1")
    uf = kb.tile([128, T], F32, "uf")
    ub = kb.tile([128, T], BF16, "ub")
    qr = kb.tile([128, T], F32, "qr")
    qi = kb.tile([128, T], F32, "qi")
    magb = kb.tile([128, 512], F32, "magb")
    ones512 = kb.tile([128, 512], F32, "ones512")
    kb.op(V, lambda e: e.memset(ones512[:], 1.0), writes=[ones512])
    w1, w2, w3, w4, qri, qii, hrf, hif = [kb.tile([128, 512], F32, n) for n in ("w1", "w2", "w3", "w4", "qri", "qii", "hrf", "hif")]
    HR = [kb.tile([128, T], BF16, "HR") for _ in range(4)]
    HI = [kb.tile([128, T], BF16, "HI") for _ in range(4)]
    yf = [kb.tile([128, 512], F32, "yf") for _ in range(2)]
    ygt = [kb.tile([128, T], BF16, "ygt") for _ in range(2)]
    ST = kb.tile([128, 4, 64], F32, "ST")
    bcnt = [0]
    ccnt = [0]

    def sc(idx, j):
        return PT[:, idx, j:j + 1]

    for j in range(64):
        kt, jj = j // 4, j % 4
        lb_, lc_ = LB[kt % 2], LC[kt % 2]
        er, ei = Er[j % 2], Ei[j % 2]
        G = "pool"
        kb.op(G, lambda e, er=er: e.memset(er[:, 0:1], 1.0), writes=[er])
        kb.op(G, lambda e, ei=ei: e.memset(ei[:, 0:1], 0.0), writes=[ei])
        for k in range(NK):
            n = 1 << k
            kb.op(G, lambda e, er=er, n=n, k=k, j=j: e.tensor_scalar(out=et1[:, 0:n], in0=er[:, 0:n], scalar1=sc(I_CK + k, j), scalar2=None, op0=ALU.mult), reads=[er, PT], writes=[et1])
            kb.op(G, lambda e, ei=ei, n=n, k=k, j=j: e.tensor_scalar(out=et1[:, 1024:1024 + n], in0=ei[:, 0:n], scalar1=sc(I_SK + k, j), scalar2=None, op0=ALU.mult), reads=[ei, PT], writes=[et1])
            kb.op(G, lambda e, er=er, n=n, k=k, j=j: e.tensor_scalar(out=et1[:, 2048:2048 + n], in0=er[:, 0:n], scalar1=sc(I_SK + k, j), scalar2=None, op0=ALU.mult), reads=[er, PT], writes=[et1])
            kb.op(G, lambda e, ei=ei, n=n, k=k, j=j: e.tensor_scalar(out=et1[:, 3072:3072 + n], in0=ei[:, 0:n], scalar1=sc(I_CK + k, j), scalar2=None, op0=ALU.mult), reads=[ei, PT], writes=[et1])
            kb.op(G, lambda e, er=er, n=n: e.tensor_tensor(out=er[:, n:2 * n], in0=et1[:, 0:n], in1=et1[:, 1024:1024 + n], op=ALU.subtract), reads=[et1], writes=[er])
            kb.op(G, lambda e, ei=ei, n=n: e.tensor_tensor(out=ei[:, n:2 * n], in0=et1[:, 2048:2048 + n], in1=et1[:, 3072:3072 + n], op=ALU.add), reads=[et1], writes=[ei])
        kb.op(G, lambda e, er=er: e.tensor_copy(out=er[:, L:T], in_=er[:, 0:S]), reads=[er], writes=[er])
        kb.op(G, lambda e, ei=ei: e.tensor_copy(out=ei[:, L:T], in_=ei[:, 0:S]), reads=[ei], writes=[ei])
        pb_re, pb_im = preB[jj]
        c0b = ((2 * j) % 8) * 16
        for gl in range(2):
            ps_ = slice(gl * 64, (gl + 1) * 64)
            cc = slice(c0b + gl * 16, c0b + gl * 16 + 16)
            kb.op(V, lambda e, ps_=ps_, j=j: e.tensor_scalar(out=tb16[ps_, :], in0=Ball[0][ps_, j, :], scalar1=PT[ps_, I_FR, j:j + 1], scalar2=None, op0=ALU.mult), reads=[Ball[0], PT], writes=[tb16])
            kb.op(V, lambda e, ps_=ps_, cc=cc, j=j, pb_re=pb_re: _stt(e, pb_re[ps_, cc], Ball[1][ps_, j, :], PT[ps_, I_NFI, j:j + 1], tb16[ps_, :], ALU.mult, ALU.add), reads=[Ball[1], PT, tb16], writes=[pb_re])
            kb.op(V, lambda e, ps_=ps_, j=j: e.tensor_scalar(out=tb16[ps_, :], in0=Ball[1][ps_, j, :], scalar1=PT[ps_, I_FR, j:j + 1], scalar2=None, op0=ALU.mult), reads=[Ball[1], PT], writes=[tb16])
            kb.op(V, lambda e, ps_=ps_, cc=cc, j=j, pb_im=pb_im: _stt(e, pb_im[ps_, cc], Ball[0][ps_, j, :], PT[ps_, I_FI, j:j + 1], tb16[ps_, :], ALU.mult, ALU.add), reads=[Ball[0], PT, tb16], writes=[pb_im])
        for c in range(2):
            pc = preC[c]
            sgn = 1.0 if c == 0 else -1.0
            for gl in range(2):
                m = (2 * j + gl) % 8
                kb.op(V, lambda e, pc=pc, c=c, gl=gl, m=m, kt=kt, sgn=sgn: e.tensor_scalar(out=pc[:, gl * 64:(gl + 1) * 64], in0=Call[c][:, kt, :], scalar1=rm[:, m:m + 1], scalar2=sgn, op0=ALU.mult, op1=ALU.mult),
                      reads=[Call[c], self.cst], writes=[pc])
        b = 6 + (j % 2)

        def trl(e, b=b, pb_re=pb_re, pb_im=pb_im):
            e.transpose(out=P[b][:, 0:128], in_=pb_re[:], identity=self.ident)
            e.transpose(out=P[b][:, 128:256], in_=pb_im[:], identity=self.ident)
            e.transpose(out=P[b][:, 256:384], in_=preC[0][:], identity=self.ident)
            return e.transpose(out=P[b][:, 384:512], in_=preC[1][:], identity=self.ident)
        kb.op("pe", trl, reads=[pb_re, pb_im, preC[0], preC[1], self.cst], writes=[PB[b]])
        kb.op("act", lambda e, b=b, lb_=lb_, jj=jj: e.copy(out=lb_[:, jj, :, :], in_=P[b][:, 0:256].rearrange("p (a c) -> p a c", c=128)), reads=[PB[b]], writes=[lb_])
        kb.op("act", lambda e, b=b, lc_=lc_, jj=jj: e.copy(out=lc_[:, jj, :, :], in_=P[b][:, 256:512].rearrange("p (a c) -> p a c", c=128)), reads=[PB[b]], writes=[lc_])
        if jj == 0:
            kb.dma("sp", uf[:], self.Z[kt * 128:(kt + 1) * 128, :], writes=[uf])
            kb.op("act", lambda e: e.copy(out=ub[:], in_=uf[:]), reads=[uf], writes=[ub])
        kb.op(V, lambda e, j=j: e.tensor_scalar(out=magb[:], in0=ones512[:], scalar1=sc(I_MAG, j), scalar2=None, op0=ALU.mult), reads=[ones512, PT], writes=[magb])
        for si, (c0, c1) in enumerate(SUBS):
            n = c1 - c0
            br = 2 * (bcnt[0] % 3)
            bi = br + 1
            bcnt[0] += 1

            def bp(e, br=br, bi=bi, lb_=lb_, jj=jj, c0=c0, c1=c1, n=n):
                e.matmul(out=P[br][:, 0:n], lhsT=lb_[:, jj, 0, :], rhs=ub[:, c0:c1], start=True, stop=True)
                return e.matmul(out=P[bi][:, 0:n], lhsT=lb_[:, jj, 1, :], rhs=ub[:, c0:c1], start=True, stop=True)
            kb.op("pe", bp, reads=[lb_, ub], writes=[PB[br], PB[bi]])
            def m2(out, a, b_, op, rd, wr):
                kb.op(V, lambda e: e.tensor_tensor(out=out, in0=a, in1=b_, op=op), reads=rd, writes=wr)
            m2(w1[:, 0:n], P[br][:, 0:n], er[:, c0:c1], ALU.mult, [PB[br], er], [w1])
            m2(w2[:, 0:n], P[bi][:, 0:n], ei[:, c0:c1], ALU.mult, [PB[bi], ei], [w2])
            m2(qri[:, 0:n], w1[:, 0:n], w2[:, 0:n], ALU.add, [w1, w2], [qri])
            m2(w3[:, 0:n], P[bi][:, 0:n], er[:, c0:c1], ALU.mult, [PB[bi], er], [w3])
            m2(w4[:, 0:n], P[br][:, 0:n], ei[:, c0:c1], ALU.mult, [PB[br], ei], [w4])
            m2(qii[:, 0:n], w3[:, 0:n], w4[:, 0:n], ALU.subtract, [w3, w4], [qii])
            if si == 0:
                inr_, ini_ = 0.0, 0.0
            elif si == 4:
                inr_, ini_ = sc(I_INR, j), sc(I_INI, j)
            else:
                inr_, ini_ = qr[:, c0 - 1:c0], qi[:, c0 - 1:c0]
            kb.op(V, lambda e, c0=c0, c1=c1, n=n, inr_=inr_: e.tensor_tensor_scan(out=qr[:, c0:c1], data0=magb[:, 0:n], data1=qri[:, 0:n], initial=inr_, op0=ALU.mult, op1=ALU.add),
                  reads=[magb, qri, qr, PT], writes=[qr])
            kb.op(V, lambda e, c0=c0, c1=c1, n=n, ini_=ini_: e.tensor_tensor_scan(out=qi[:, c0:c1], data0=magb[:, 0:n], data1=qii[:, 0:n], initial=ini_, op0=ALU.mult, op1=ALU.add),
                  reads=[magb, qii, qi, PT], writes=[qi])
            m2(w1[:, 0:n], qr[:, c0:c1], er[:, c0:c1], ALU.mult, [qr, er], [w1])
            m2(w2[:, 0:n], qi[:, c0:c1], ei[:, c0:c1], ALU.mult, [qi, ei], [w2])
            m2(hrf[:, 0:n], w1[:, 0:n], w2[:, 0:n], ALU.subtract, [w1, w2], [hrf])
            m2(w3[:, 0:n], qr[:, c0:c1], ei[:, c0:c1], ALU.mult, [qr, ei], [w3])
            m2(w4[:, 0:n], qi[:, c0:c1], er[:, c0:c1], ALU.mult, [qi, er], [w4])
            m2(hif[:, 0:n], w3[:, 0:n], w4[:, 0:n], ALU.add, [w3, w4], [hif])
            kb.op("act", lambda e, jj=jj, c0=c0, c1=c1, n=n: e.copy(out=HR[jj][:, c0:c1], in_=hrf[:, 0:n]), reads=[hrf], writes=[HR[jj]])
            kb.op("act", lambda e, jj=jj, c0=c0, c1=c1, n=n: e.copy(out=HI[jj][:, c0:c1], in_=hif[:, 0:n]), reads=[hif], writes=[HI[jj]])
            if si in (3, 4):
                a = 0 if si == 3 else 2
                kb.op("act", lambda e, a=a, j=j, n=n: e.copy(out=ST[:, a, j:j + 1], in_=hrf[:, n - 1:n]), reads=[hrf], writes=[ST])
                kb.op("act", lambda e, a=a, j=j, n=n: e.copy(out=ST[:, a + 1, j:j + 1], in_=hif[:, n - 1:n]), reads=[hif], writes=[ST])
        if jj == 3:
            yg_ = ygt[kt % 2]
            for si, (c0, c1) in enumerate(SUBS):
                n = c1 - c0
                b = 6 + (ccnt[0] % 2)
                y_ = yf[ccnt[0] % 2]
                ccnt[0] += 1

                def cp(e, b=b, lc_=lc_, c0=c0, c1=c1, n=n):
                    for a in range(4):
                        e.matmul(out=P[b][:, 0:n], lhsT=lc_[:, a, 0, :], rhs=HR[a][:, c0:c1], start=(a == 0), stop=False)
                        ins = e.matmul(out=P[b][:, 0:n], lhsT=lc_[:, a, 1, :], rhs=HI[a][:, c0:c1], start=False, stop=(a == 3))
                    return ins
                kb.op("pe", cp, reads=[lc_] + HR + HI, writes=[PB[b]])
                kb.op(V, lambda e, b=b, y_=y_, c0=c0, c1=c1, n=n, kt=kt: _stt(e, y_[:, 0:n], uf[:, c0:c1], dcol[:, kt:kt + 1], P[b][:, 0:n], ALU.mult, ALU.add), reads=[uf, dcol, PB[b]], writes=[y_])
                kb.op("act", lambda e, y_=y_, yg_=yg_, c0=c0, c1=c1, n=n: e.activation(out=yg_[:, c0:c1], in_=y_[:, 0:n], func=AF.Gelu), reads=[y_], writes=[yg_])
            kb.dma("sp", self.YG[kt * 128:(kt + 1) * 128, :], yg_[:], reads=[yg_])
    b = 0

    def trs(e):
        for a in range(4):
            ins = e.transpose(out=P[b][0:64, a * 128:(a + 1) * 128], in_=ST[:, a, :], identity=self.ident)
        return ins
    kb.op("pe", trs, reads=[ST, self.cst], writes=[PB[b]])
    sto = kb.tile([64, 512], F32, "sto")
    kb.op(V, lambda e: e.tensor_copy(out=sto[:], in_=P[b][0:64, :]), reads=[PB[b]], writes=[sto])
    for a, nm in enumerate(("s5_re_prompt", "s5_im_prompt", "s5_re_sample", "s5_im_sample")):
        kb.dma("sp", self.o[nm][i], sto[:, a * 128:(a + 1) * 128], reads=[sto])
    kb.bank_rr = 0
    kb.phase_end()
    bg = self.load_cols(self.i["s5_b_glu"][i], 16, "bg")
    YGt = kb.tile([128, 16, T], BF16, "YGt")
    kb.dma("sp", YGt[:], self.YG.rearrange("(kt p) t -> p kt t", p=128), writes=[YGt])
    wv = self.i["s5_w_glu"][i].rearrange("(kt p) n -> p kt n", p=128)
    sg = [kb.tile([128, 512], F32, "sg") for _ in range(2)]
    mo = [kb.tile([128, T], BF16, "mo") for _ in range(2)]
    gc = [0]

    def epi(n, si, c0, c1, ps, pb):
        s_ = sg[gc[0] % 2]
        gc[0] += 1
        m_ = mo[n % 2]
        kb.op("act", lambda e: e.activation(out=s_[:, 0:c1 - c0], in_=ps, func=AF.Sigmoid, bias=bg[:, n:n + 1], scale=1.0), reads=[pb, bg], writes=[s_])
        kb.op(V, lambda e: e.tensor_tensor(out=m_[:, c0:c1], in0=s_[:, 0:c1 - c0], in1=YGt[:, n, c0:c1], op=ALU.mult), reads=[s_, YGt], writes=[m_])
        if si == len(SUBS) - 1:
            kb.dma("sp", self.MIX[n * 128:(n + 1) * 128, :], m_[:], reads=[m_])
    self.gemm(lambda n: wv[:, :, n * 128:(n + 1) * 128], 16, 16, lambda kt, c0, c1: YGt[:, kt, c0:c1], [YGt], epi)
    kb.phase_end()


def phase_pool(self, i):
    kb = self.kb
    P = self.psum
    PB = self.psb
    V = "dve"
    pscale = self.load_cols(self.i["pool_scale"][i], 16, "pscale")
    ZP = kb.tile([128, 16, T], BF16, "ZP")
    spin = kb.tile([15, 2048], F32, "spin")
    kb.dma("sp", spin[:], self.i["state_pool"][i], writes=[spin])
    kb.dma("sp", self.o["pool_sample"][i][0:7, :], self.i["state_pool"][i][8:15, :])
    pp = kb.tile([15, 2048], F32, "pp")
    psm = kb.tile([S, 2048], F32, "psm")
    EA = [kb.tile([128, 15 + L], F32, "EA") for _ in range(2)]
    EB = [kb.tile([128, 15 + L], F32, "EB") for _ in range(2)]
    ES = [kb.tile([128, 15 + S], F32, "ES") for _ in range(3)]
    zpf = kb.tile([128, L], F32, "zpf")
    t15 = kb.tile([128, 15], F32, "t15")
    for a in range(2):
        kb.op(V, lambda e, a=a: e.memset(EA[a][:, 0:15], 0.0), writes=[EA[a]])
    for ft in range(16):
        g = ft // 4
        w = (2, 4, 8, 16)[g]
        E0 = EA[ft % 2]
        kb.dma("sp", E0[:, 15:15 + L], self.Z[2048 + ft * 128:2048 + (ft + 1) * 128, 0:L], writes=[E0])
        Es0 = ES[0]
        kb.dma("sp", Es0[:, 15:15 + S], self.Z[2048 + ft * 128:2048 + (ft + 1) * 128, L:T], writes=[Es0])
        b = ft % 4
        kb.op("pe", lambda e, b=b, ft=ft: e.transpose(out=P[b][:, 0:15], in_=spin[:, ft * 128:(ft + 1) * 128], identity=self.ident[0:15, 0:15]), reads=[spin, self.cst], writes=[PB[b]])
        kb.op(V, lambda e, b=b: e.tensor_copy(out=Es0[:, 0:15], in_=P[b][:, 0:15]), reads=[PB[b]], writes=[Es0])
        A, Bt = E0, EB[0]
        As, Bs = Es0, ES[1]
        nb = 0
        step = 1
        while step < w:
            dst = EB[nb % 2]
            dsts = ES[1 + nb % 2]
            nb += 1
            kb.op(V, lambda e, A=A, dst=dst, step=step: e.tensor_tensor(out=dst[:, step:], in0=A[:, step:], in1=A[:, 0:15 + L - step], op=ALU.add), reads=[A], writes=[dst])
            kb.op(V, lambda e, As=As, dsts=dsts, step=step: e.tensor_tensor(out=dsts[:, step:], in0=As[:, step:], in1=As[:, 0:15 + S - step], op=ALU.add), reads=[As], writes=[dsts])
            A, As = dst, dsts
            step *= 2
        kb.op(V, lambda e, A=A, E0=E0, w=w: _stt(e, zpf[:], A[:, 15:15 + L], 1.0 / w, E0[:, 15:15 + L], ALU.mult, ALU.subtract), reads=[A, E0], writes=[zpf])
        kb.op(V, lambda e, A=A, g=g: e.tensor_tensor(out=t15[:], in0=A[:, 15:30], in1=self.cst.t[:, C_FIX + g * 15:C_FIX + (g + 1) * 15], op=ALU.mult), reads=[A, self.cst], writes=[t15])
        kb.op(V, lambda e: e.tensor_tensor(out=zpf[:, 0:15], in0=zpf[:, 0:15], in1=t15[:], op=ALU.add), reads=[zpf, t15], writes=[zpf])
        kb.op("act", lambda e, ft=ft: e.copy(out=ZP[:, ft, 0:L], in_=zpf[:]), reads=[zpf], writes=[ZP])
        kb.op(V, lambda e, As=As, Es0=Es0, w=w, ft=ft: _stt(e, ZP[:, ft, L:T], As[:, 15:15 + S], 1.0 / w, Es0[:, 15:15 + S], ALU.mult, ALU.subtract), reads=[As, Es0], writes=[ZP])
        b2 = 4 + ft % 4

        def tro(e, b2=b2, E0=E0, Es0=Es0):
            e.transpose(out=P[b2][0:15, 0:128], in_=E0[:, L:L + 15], identity=self.ident)
            return e.transpose(out=P[b2][0:S, 128:256], in_=Es0[:, 15:15 + S], identity=self.ident)
        kb.op("pe", tro, reads=[E0, Es0, self.cst], writes=[PB[b2]])
        kb.op("act", lambda e, b2=b2, ft=ft: e.copy(out=pp[:, ft * 128:(ft + 1) * 128], in_=P[b2][0:15, 0:128]), reads=[PB[b2]], writes=[pp])
        kb.op("act", lambda e, b2=b2, ft=ft: e.copy(out=psm[:, ft * 128:(ft + 1) * 128], in_=P[b2][0:S, 128:256]), reads=[PB[b2]], writes=[psm])
    kb.dma("sp", self.o["pool_prompt"][i], pp[:], reads=[pp])
    kb.dma("sp", self.o["pool_sample"][i][7:15, :], psm[:], reads=[psm])
    kb.barrier()
    kb.bank_rr = 0
    mo = [kb.tile([128, T], BF16, "pmo") for _ in range(2)]
    for g in range(4):
        wv = self.i["pool_w"][i, g].rearrange("(kt p) n -> p kt n", p=128)

        def epi(n, si, c0, c1, ps, pb, g=g):
            m_ = mo[n % 2]
            f = g * 4 + n
            kb.op(V, lambda e: e.tensor_scalar(out=m_[:, c0:c1], in0=ps, scalar1=pscale[:, f:f + 1], scalar2=None, op0=ALU.mult), reads=[pb, pscale], writes=[m_])
            if si == len(SUBS) - 1:
                kb.dma("sp", self.MIX[2048 + f * 128:2048 + (f + 1) * 128, :], m_[:], reads=[m_])
        self.gemm(lambda n, wv=wv: wv[:, :, n * 128:(n + 1) * 128], 4, 4, lambda kt, c0, c1, g=g: ZP[:, g * 4 + kt, c0:c1], [ZP], epi, nw=2, wname=f"pw{g}")
    kb.phase_end()


Prog.phase_s5 = phase_s5
Prog.phase_pool = phase_pool


def build_all(self):
    self.phase_in()
    for l in range(self.depth):
        i = l // 2
        if l % 2 == 0:
            self.phase_win(self.i["norm_mix"][l], self.i["ev_w_in"][i], 4096)
            self.phase_s5(i)
            self.phase_pool(i)
            self.phase_wout(self.i["ev_w_out"][i])
        else:
            self.phase_win(self.i["norm_mix"][l], self.i["od_w_in"][i], 10240)
            self.phase_attn(i)
            self.phase_sgu(i)
            self.phase_wout(self.i["od_w_out"][i])
        self.phase_ffn(l)
    self.phase_out()
    self.kb.emit()


Prog.build_all = build_all


_CONSTS = None


def core_inputs(inp, c):
    global _CONSTS
    if _CONSTS is None:
        _CONSTS = make_consts()
    bp = c % 4
    f = lambda a: np.ascontiguousarray(a, dtype=np.float32)
    m = dict(
        x_prompt=f(inp["x_prompt"][bp]), x_sample=f(inp["x_sample"][c]),
        state_s5_re=f(inp["state_s5_re"][:, c]).reshape(2, 64, 128), state_s5_im=f(inp["state_s5_im"][:, c]).reshape(2, 64, 128),
        state_pool=f(inp["state_pool"][:, c]),
        cache_k=f(inp["cache_k"][:, c]), cache_v=f(inp["cache_v"][:, c]),
        s5_lambda_re=f(inp["s5_lambda_re"]).reshape(2, 64, 128), s5_lambda_im=f(inp["s5_lambda_im"]).reshape(2, 64, 128),
        s5_log_dt=f(inp["s5_log_dt"]).reshape(2, 64, 2),
        s5_b_re=f(inp["s5_b_re"]).reshape(2, 64, 128, 16), s5_b_im=f(inp["s5_b_im"]).reshape(2, 64, 128, 16),
        s5_c_re=f(inp["s5_c_re"]).reshape(2, 2048, 64), s5_c_im=f(inp["s5_c_im"]).reshape(2, 2048, 64),
        consts=_CONSTS,
    )
    for k in ("norm_mix", "norm_ffn", "ev_w_in", "ev_w_out", "s5_d", "s5_w_glu", "s5_b_glu", "pool_w", "pool_scale",
              "od_w_in", "od_w_out", "q_norm", "k_norm", "sgu_ln_g", "sgu_ln_b", "sgu_w", "sgu_b", "ffn_w1", "ffn_w3", "ffn_w2"):
        m[k] = f(inp[k])
    return m


def kernel(**inputs):
    n = 8
    prog = Prog(depth=4)
    prog.build_all()
    in_maps = [core_inputs(inputs, c) for c in range(n)]
    res = run_bass_kernel_spmd(prog.nc, in_maps, core_ids=list(range(n)))
    r = res.results

    def stack(name, cores, shape=None):
        a = np.stack([np.asarray(r[c][name], dtype=np.float32) for c in cores], axis=1)
        return a if shape is None else a.reshape(shape)

    pc = list(range(4))
    sc_ = list(range(8))
    return (
        stack("y_prompt", pc)[0] if False else np.stack([r[c]["y_prompt"] for c in pc], axis=0).astype(np.float32),
        np.stack([r[c]["y_sample"] for c in sc_], axis=0).astype(np.float32),
        stack("s5_re_prompt", pc, (2, 4, 128, 64)),
        stack("s5_im_prompt", pc, (2, 4, 128, 64)),
        stack("pool_prompt", pc),
        stack("k_prompt", pc),
        stack("v_prompt", pc),
        stack("s5_re_sample", sc_, (2, 8, 128, 64)),
        stack("s5_im_sample", sc_, (2, 8, 128, 64)),
        stack("pool_sample", sc_),
        stack("k_sample", sc_),
        stack("v_sample", sc_),
        stack("sgu_v_sample", sc_),
    )
```

```python
import math
import numpy as np
import concourse.bass as bass
import concourse.mybir as mybir
from concourse.bass_utils import run_bass_kernel_spmd

F32 = mybir.dt.float32
BF16 = mybir.dt.bfloat16
ALU = mybir.AluOpType
AF = mybir.ActivationFunctionType
AX = mybir.AxisListType
SEM_LIMIT = 30000
NSLOTS = 6
SB_BASE = 16512
SB_TOP = 229344

D = 4096
L = 2048
S = 8
T = L + S
DFF = 11008
NJ = DFF // 128
SUBS = [(0, 512), (512, 1024), (1024, 1536), (1536, 2048), (2048, 2056)]
ATT_SCALE = 128 ** -0.5
EXP_SHIFT = -8.0
ATT_HEADS = 16
ATT_Q = "sp"
ATT_T4 = 4
ATT_PARTS = ('norm', 'tr', 'trdma', 'trs', 'prompt', 'sample')
PI = math.pi


class Eng:
    def __init__(self, kb, name):
        self.kb = kb
        self.name = name
        self.epoch = 0
        self.sems = [kb.nc.alloc_semaphore(f"s_{name}_0")]
        self.n = 0
        self.waited = {}

    def bump(self, inc):
        if self.n + inc > SEM_LIMIT:
            self.epoch += 1
            self.sems.append(self.kb.nc.alloc_semaphore(f"s_{self.name}_{self.epoch}"))
            self.n = 0
        self.n += inc
        return (self, self.epoch, self.n)


class Buf:
    __slots__ = ("w", "r")

    def __init__(self):
        self.w = None
        self.r = {}


class Tile:
    def __init__(self, t):
        self.t = t
        self.b = Buf()

    def __getitem__(self, k):
        return self.t[k]


class KB:
    def __init__(self, nc):
        self.nc = nc
        self.ops = {k: [] for k in ("pe", "act", "dve", "pool", "sp")}
        self.engs = {k: Eng(self, k) for k in ("pe", "act", "dve", "pool", "sp")}
        self.dslots = {}
        self.dslot_rr = {}
        self.sb_off = SB_BASE
        self.sb_cnt = 0
        self.bank_rr = 0

    def tile(self, shape, dtype, name="t"):
        esz = 2 if dtype == BF16 else 4
        per_part = int(np.prod(shape[1:])) * esz
        off = (self.sb_off + 63) // 64 * 64
        self.sb_off = off + per_part
        assert self.sb_off <= SB_TOP, f"sbuf overflow {self.sb_off} ({name})"
        self.sb_cnt += 1
        return Tile(self.nc.alloc_sbuf_tensor_at(f"{name}_{self.sb_cnt}", list(shape), dtype, offset=off))

    def sb_mark(self):
        return self.sb_off

    def sb_reset(self, mark=SB_BASE):
        self.sb_off = mark

    def _waits(self, E, reads, writes):
        deps = {}

        def add(d):
            if d is None:
                return
            F, ep, c = d
            if deps.get((F, ep), 0) < c:
                deps[(F, ep)] = c

        for b in reads:
            add(b.w)
        for b in writes:
            add(b.w)
            for d in b.r.values():
                add(d)
        waits = []
        for (F, ep), c in deps.items():
            if F is E and E.name == "pe":
                continue
            done = False
            for (F2, ep2), c2 in E.waited.items():
                if F2 is F and (ep2 > ep or (ep2 == ep and c2 >= c)):
                    done = True
                    break
            if done:
                continue
            E.waited[(F, ep)] = c
            waits.append((F.sems[ep], c))
        return waits

    def _mark(self, tok, reads, writes):
        E = tok[0]
        for b in reads:
            b.r[E] = tok
        for b in writes:
            b.w = tok
            b.r = {}

    def op(self, ename, fn, reads=(), writes=()):
        reads = [x.b if isinstance(x, Tile) else x for x in reads]
        writes = [x.b if isinstance(x, Tile) else x for x in writes]
        E = self.engs[ename]
        waits = self._waits(E, reads, writes)
        tok = E.bump(1)
        sem = E.sems[tok[1]]

        def rec(e, waits=waits, fn=fn, sem=sem):
            for s, v in waits:
                e.wait_ge(s, v)
            fn(e).then_inc(sem, 1)

        self.ops[ename].append(rec)
        self._mark(tok, reads, writes)
        return tok

    def dma(self, qname, out, in_, reads=(), writes=(), nslots=None, **kw):
        nslots = nslots or NSLOTS
        reads = [x.b if isinstance(x, Tile) else x for x in reads]
        writes = [x.b if isinstance(x, Tile) else x for x in writes]
        Q = self.engs[qname]
        rr = self.dslot_rr.get(qname, 0)
        self.dslot_rr[qname] = rr + 1
        key = (qname, rr % nslots)
        if key not in self.dslots:
            self.dslots[key] = Eng(self, f"d{qname}{rr % nslots}")
        Dm = self.dslots[key]
        waits = self._waits(Q, reads, writes)
        if Dm.n > 0:
            k = (Dm, Dm.epoch)
            if Q.waited.get(k, 0) < Dm.n:
                Q.waited[k] = Dm.n
                waits.append((Dm.sems[Dm.epoch], Dm.n))
        tok = Dm.bump(16)
        sem = Dm.sems[tok[1]]

        def rec(e, waits=waits, sem=sem, out=out, in_=in_, kw=kw):
            for s, v in waits:
                e.wait_ge(s, v)
            e.dma_start(out=out, in_=in_, **kw).then_inc(sem, 16)

        self.ops[qname].append(rec)
        self._mark(tok, reads, writes)
        return tok

    def barrier(self):
        allE = list(self.engs.values()) + list(self.dslots.values())
        for ename, E in self.engs.items():
            waits = []
            for F in allE:
                if F is E or (F.n == 0 and F.epoch == 0):
                    continue
                k = (F, F.epoch)
                if E.waited.get(k, 0) < F.n:
                    E.waited[k] = F.n
                    waits.append((F.sems[F.epoch], F.n))
            if waits:
                def rec(e, waits=waits):
                    for s, v in waits:
                        e.wait_ge(s, v)
                self.ops[ename].append(rec)

    def phase_end(self):
        self.barrier()
        self.sb_reset(self.persist_mark)

    def emit(self):
        self.barrier()
        with self.nc.Block() as block:
            @block.tensor
            def _(e):
                for f in self.ops["pe"]:
                    f(e)

            @block.scalar
            def _(e):
                for f in self.ops["act"]:
                    f(e)

            @block.vector
            def _(e):
                for f in self.ops["dve"]:
                    f(e)

            @block.gpsimd
            def _(e):
                for f in self.ops["pool"]:
                    f(e)

            @block.sync
            def _(e):
                for f in self.ops["sp"]:
                    f(e)


C_ID = 0
C_WM = 128
C_WS = C_WM + 2560
C_WSN = C_WS + 128
C_TRIL = C_WSN + 8
C_RM = C_TRIL + 128
C_FIX = C_RM + 8
NCONST = C_FIX + 60


def _mult(d):
    d = np.asarray(d)
    m = ((d >= 0) & (d <= 128)).astype(np.float32)
    m += ((d >= 0) & (d <= 512) & (d % 4 == 0)).astype(np.float32)
    m += ((d >= 0) & (d <= 2048) & (d % 16 == 0)).astype(np.float32)
    return m


def make_consts():
    c = np.zeros((128, NCONST), np.float32)
    c[:, C_ID:C_ID + 128] = np.eye(128, dtype=np.float32)
    i = np.arange(128)[:, None]
    x = np.arange(2560)[None, :]
    c[:, C_WM:C_WM + 2560] = _mult(x - 384 - i)
    kt = (np.arange(128) // 8)[None, :]
    s = (np.arange(128) % 8)[None, :]
    c[:, C_WS:C_WS + 128] = _mult(2048 + s - kt * 128 - i)
    sp = np.arange(8)[:, None]
    sq = np.arange(8)[None, :]
    c[0:8, C_WSN:C_WSN + 8] = _mult(sq - sp)
    c[:, C_TRIL:C_TRIL + 128] = (np.arange(128)[None, :] <= i).astype(np.float32)
    c[:, C_RM:C_RM + 8] = ((i // 16) == np.arange(8)[None, :]).astype(np.float32)
    for g, w in enumerate((2, 4, 8, 16)):
        t = np.arange(15)
        c[:, C_FIX + g * 15:C_FIX + (g + 1) * 15] = (1.0 / np.minimum(t + 1, w) - 1.0 / w)[None, :]
    return c


class _Lazy(dict):
    def __init__(self, mk):
        super().__init__()
        self.mk = mk
        self.shapes = {}

    def __missing__(self, k):
        v = self.mk(k, self.shapes[k])
        self[k] = v
        return v


class Prog:
    def __init__(self, depth=4, dbg=False):
        self.depth = depth
        self.dbg = dbg
        nc = self.nc = bass.Bass("TRN2", target_bir_lowering=False)
        kb = self.kb = KB(nc)
        ne, no = 2, 2

        def din(name, shape):
            return nc.dram_tensor(name, list(shape), F32, kind="ExternalInput").ap()

        def dout(name, shape):
            return nc.dram_tensor(name, list(shape), F32, kind="ExternalOutput").ap()

        self.i = _Lazy(din)
        self.i.shapes = dict(
            x_prompt=(L, D), x_sample=(S, D),
            state_s5_re=(ne, 64, 128), state_s5_im=(ne, 64, 128),
            state_pool=(ne, 15, 2048),
            cache_k=(no, 2048, 16, 128), cache_v=(no, 2048, 16, 128),
            norm_mix=(4, D), norm_ffn=(4, D),
            ev_w_in=(ne, D, D), ev_w_out=(ne, D, D),
            s5_lambda_re=(ne, 64, 128), s5_lambda_im=(ne, 64, 128),
            s5_log_dt=(ne, 64, 2),
            s5_b_re=(ne, 64, 128, 16), s5_b_im=(ne, 64, 128, 16),
            s5_c_re=(ne, 2048, 64), s5_c_im=(ne, 2048, 64),
            s5_d=(ne, 2048), s5_w_glu=(ne, 2048, 2048), s5_b_glu=(ne, 2048),
            pool_w=(ne, 4, 512, 512), pool_scale=(ne, 2048),
            od_w_in=(no, D, 10240), od_w_out=(no, D, D),
            q_norm=(no, 128), k_norm=(no, 128),
            sgu_ln_g=(no, 2048), sgu_ln_b=(no, 2048),
            sgu_w=(no, 8, 128, 128), sgu_b=(no, 8, 128),
            ffn_w1=(4, D, DFF), ffn_w3=(4, D, DFF), ffn_w2=(4, DFF, D),
            consts=(128, NCONST),
        )
        self.o = dict(
            y_prompt=dout("y_prompt", (L, D)), y_sample=dout("y_sample", (S, D)),
            s5_re_prompt=dout("s5_re_prompt", (ne, 64, 128)), s5_im_prompt=dout("s5_im_prompt", (ne, 64, 128)),
            pool_prompt=dout("pool_prompt", (ne, 15, 2048)),
            k_prompt=dout("k_prompt", (no, 2048, 16, 128)), v_prompt=dout("v_prompt", (no, 2048, 16, 128)),
            s5_re_sample=dout("s5_re_sample", (ne, 64, 128)), s5_im_sample=dout("s5_im_sample", (ne, 64, 128)),
            pool_sample=dout("pool_sample", (ne, 15, 2048)),
            k_sample=dout("k_sample", (no, 8, 16, 128)), v_sample=dout("v_sample", (no, 8, 16, 128)),
            sgu_v_sample=dout("sgu_v_sample", (no, 8, 2048)),
        )
        kind = "ExternalOutput" if dbg else "Internal"
        self.X = nc.dram_tensor("Xs", [D, T], F32, kind=kind).ap()
        self.Z = nc.dram_tensor("Zs", [10240, T], F32, kind=kind).ap()
        self.MIX = nc.dram_tensor("MIXs", [D, T], BF16, kind="Internal").ap()
        self.ACTB = nc.dram_tensor("ACTs", [DFF, T], BF16, kind="Internal").ap()
        self.YG = nc.dram_tensor("YGs", [2048, T], BF16, kind="Internal").ap()
        self.psum = [nc.alloc_psum_tensor(f"ps{i}", [128, 512], F32) for i in range(8)]
        self.psb = [Buf() for _ in range(8)]

        self.cst = kb.tile([128, NCONST], F32, "cst")
        kb.dma("sp", self.cst[:], self.i["consts"], writes=[self.cst])
        self.ident = self.cst.t[:, C_ID:C_ID + 128]
        self.identb = kb.tile([128, 128], BF16, "identb")
        kb.op("dve", lambda e: e.tensor_copy(out=self.identb[:], in_=self.ident), reads=[self.cst], writes=[self.identb])
        self.ones_b = kb.tile([128, 128], BF16, "ones_b")
        kb.op("dve", lambda e: e.memset(self.ones_b[:], 1.0), writes=[self.ones_b])
        self.ones_f = kb.tile([128, 128], F32, "ones_f")
        kb.op("dve", lambda e: e.memset(self.ones_f[:], 1.0), writes=[self.ones_f])
        kb.persist_mark = kb.sb_mark()
        kb.barrier()

    def bank(self):
        b = self.kb.bank_rr % 8
        self.kb.bank_rr += 1
        return b

    def load_cols(self, dram_vec, n, name="col"):
        kb = self.kb
        if n == 1:
            out = kb.tile([128, 1], F32, name)
            kb.dma("sp", out[:], dram_vec.rearrange("(p o) -> p o", o=1), writes=[out])
            return out
        raw = kb.tile([n, 128], F32, name + "r")
        kb.dma("sp", raw[:], dram_vec.rearrange("(k p) -> k p", p=128), writes=[raw])
        out = kb.tile([128, n], F32, name)
        bk = self.bank()
        ps = self.psum[bk]
        kb.op("pe", lambda e: e.transpose(out=ps[:, 0:n], in_=raw[:], identity=self.ident[0:n, 0:n]),
              reads=[raw, self.cst], writes=[self.psb[bk]])
        kb.op("dve", lambda e: e.tensor_copy(out=out[:], in_=ps[:, 0:n]), reads=[self.psb[bk]], writes=[out])
        return out

    def gemm(self, wview, KT, n_tiles, rhs_fn, rhs_bufs, epilogue, subs=SUBS, nw=3, wname="w", wn=1):
        kb = self.kb
        wts = [kb.tile([128, KT, 128 * wn], BF16, wname) for _ in range(nw)]
        for n in range(n_tiles):
            wt = wts[(n // wn) % nw]
            h = n % wn
            if h == 0:
                kb.dma("pool", wt[:], wview(n), writes=[wt])
            for si, (c0, c1) in enumerate(subs):
                bk = self.bank()
                ps = self.psum[bk]

                def mm(e, wt=wt, c0=c0, c1=c1, ps=ps, h=h):
                    for kt in range(KT):
                        ins = e.matmul(out=ps[:, 0:c1 - c0], lhsT=wt[:, kt, h * 128:(h + 1) * 128], rhs=rhs_fn(kt, c0, c1),
                                       start=(kt == 0), stop=(kt == KT - 1))
                    return ins

                kb.op("pe", mm, reads=[wt] + list(rhs_bufs), writes=[self.psb[bk]])
                epilogue(n, si, c0, c1, ps[:, 0:c1 - c0], self.psb[bk])

    def phase_in(self):
        kb = self.kb
        Xv = self.X.rearrange("(kt p) t -> p kt t", p=128)
        xin = [kb.tile([128, D], F32, "xin") for _ in range(2)]
        xo = [kb.tile([128, 32, 128], F32, "xo") for _ in range(2)]
        for tt in range(16):
            xi = xin[tt % 2]
            xt = xo[tt % 2]
            kb.dma("sp", xi[:], self.i["x_prompt"][tt * 128:(tt + 1) * 128, :], writes=[xi])
            for k4 in range(8):
                bk = self.bank()
                ps = self.psum[bk]

                def tr(e, xi=xi, ps=ps, k4=k4):
                    for q in range(4):
                        kt = k4 * 4 + q
                        ins = e.transpose(out=ps[:, q * 128:(q + 1) * 128], in_=xi[:, kt * 128:(kt + 1) * 128], identity=self.ident)
                    return ins
                kb.op("pe", tr, reads=[xi, self.cst], writes=[self.psb[bk]])
                eng = "dve" if k4 % 2 == 0 else "act"
                if eng == "dve":
                    kb.op("dve", lambda e, xt=xt, ps=ps, k4=k4: e.tensor_copy(out=xt[:, k4 * 4:(k4 + 1) * 4, :], in_=ps[:].rearrange("p (a b) -> p a b", b=128)),
                          reads=[self.psb[bk]], writes=[xt])
                else:
                    kb.op("act", lambda e, xt=xt, ps=ps, k4=k4: e.copy(out=xt[:, k4 * 4:(k4 + 1) * 4, :], in_=ps[:].rearrange("p (a b) -> p a b", b=128)),
                          reads=[self.psb[bk]], writes=[xt])
            kb.dma("sp", Xv[:, :, tt * 128:(tt + 1) * 128], xt[:], reads=[xt])
        xs = kb.tile([S, D], F32, "xs")
        kb.dma("sp", xs[:], self.i["x_sample"], writes=[xs])
        xso = kb.tile([128, 32, S], F32, "xso")
        bk = self.bank()
        ps = self.psum[bk]

        def trs(e):
            for kt in range(32):
                ins = e.transpose(out=ps[:, kt * S:(kt + 1) * S], in_=xs[:, kt * 128:(kt + 1) * 128], identity=self.ident[0:S, 0:S])
            return ins
        kb.op("pe", trs, reads=[xs, self.cst], writes=[self.psb[bk]])
        kb.op("dve", lambda e: e.tensor_copy(out=xso[:], in_=ps[:, 0:32 * S].rearrange("p (a b) -> p a b", b=S)), reads=[self.psb[bk]], writes=[xso])
        kb.dma("sp", Xv[:, :, L:T], xso[:], reads=[xso])
        kb.phase_end()

    def phase_out(self):
        kb = self.kb
        Xv = self.X.rearrange("(kt p) t -> p kt t", p=128)
        xin = [kb.tile([128, 32, 128], F32, "yin") for _ in range(2)]
        yo = [kb.tile([128, D], F32, "yo") for _ in range(2)]
        for tt in range(16):
            xi = xin[tt % 2]
            yt = yo[tt % 2]
            kb.dma("sp", xi[:], Xv[:, :, tt * 128:(tt + 1) * 128], writes=[xi])
            for k4 in range(8):
                bk = self.bank()
                ps = self.psum[bk]

                def tr(e, xi=xi, ps=ps, k4=k4):
                    for q in range(4):
                        ins = e.transpose(out=ps[:, q * 128:(q + 1) * 128], in_=xi[:, k4 * 4 + q, :], identity=self.ident)
                    return ins
                kb.op("pe", tr, reads=[xi, self.cst], writes=[self.psb[bk]])
                if k4 % 2 == 0:
                    kb.op("dve", lambda e, yt=yt, ps=ps, k4=k4: e.tensor_copy(out=yt[:, k4 * 512:(k4 + 1) * 512], in_=ps[:]), reads=[self.psb[bk]], writes=[yt])
                else:
                    kb.op("act", lambda e, yt=yt, ps=ps, k4=k4: e.copy(out=yt[:, k4 * 512:(k4 + 1) * 512], in_=ps[:]), reads=[self.psb[bk]], writes=[yt])
            kb.dma("sp", self.o["y_prompt"][tt * 128:(tt + 1) * 128, :], yt[:], reads=[yt])
        xs = kb.tile([128, 32, S], F32, "ysin")
        kb.dma("sp", xs[:], Xv[:, :, L:T], writes=[xs])
        ys = kb.tile([S, D], F32, "ys")
        for k4 in range(8):
            bk = self.bank()
            ps = self.psum[bk]

            def trs(e, ps=ps, k4=k4):
                for q in range(4):
                    ins = e.transpose(out=ps[0:S, q * 128:(q + 1) * 128], in_=xs[:, k4 * 4 + q, :], identity=self.ident)
                return ins
            kb.op("pe", trs, reads=[xs, self.cst], writes=[self.psb[bk]])
            kb.op("dve", lambda e, ps=ps, k4=k4: e.tensor_copy(out=ys[:, k4 * 512:(k4 + 1) * 512], in_=ps[0:S, :]), reads=[self.psb[bk]], writes=[ys])
        kb.dma("sp", self.o["y_sample"], ys[:], reads=[ys])
        kb.phase_end()

    def norm_prologue(self, gamma_row):
        kb = self.kb
        Xv = self.X.rearrange("(kt p) t -> p kt t", p=128)
        g_t = self.load_cols(gamma_row, 32, "gam")
        xg = kb.tile([128, 32, T], BF16, "xg")
        rstd = kb.tile([128, T], F32, "rstd")
        pm_ = kb.sb_mark()
        xf = [kb.tile([128, T], F32, "xf") for _ in range(2)]
        sq = [kb.tile([128, T], BF16, "sq") for _ in range(2)]
        banks = [0, 1, 2, 3, 4]
        for kt in range(32):
            x_ = xf[kt % 2]
            s_ = sq[kt % 2]
            kb.dma("sp", x_[:], Xv[:, kt, :], writes=[x_])
            kb.op("act", lambda e, x_=x_, s_=s_: e.activation(out=s_[:], in_=x_[:], func=AF.Square), reads=[x_], writes=[s_])
            kb.op("dve", lambda e, x_=x_, kt=kt: e.tensor_scalar(out=xg[:, kt, :], in0=x_[:], scalar1=g_t[:, kt:kt + 1], scalar2=None, op0=ALU.mult),
                  reads=[x_, g_t], writes=[xg])

            def mm(e, s_=s_, kt=kt):
                for si, (c0, c1) in enumerate(SUBS):
                    ins = e.matmul(out=self.psum[banks[si]][:, 0:c1 - c0], lhsT=self.ones_b[:], rhs=s_[:, c0:c1], start=(kt == 0), stop=(kt == 31))
                return ins
            kb.op("pe", mm, reads=[s_, self.ones_b], writes=[self.psb[b] for b in banks])
        for si, (c0, c1) in enumerate(SUBS):
            b = banks[si]
            kb.op("act", lambda e, b=b, c0=c0, c1=c1: e.activation(out=rstd[:, c0:c1], in_=self.psum[b][:, 0:c1 - c0], func=AF.Sqrt, bias=1e-6, scale=1.0 / D),
                  reads=[self.psb[b]], writes=[rstd])
        kb.op("dve", lambda e: e.reciprocal(out=rstd[:], in_=rstd[:]), reads=[rstd], writes=[rstd])
        self.kb.bank_rr = 5
        kb.barrier()
        kb.sb_reset(pm_)
        return xg, rstd

    def phase_win(self, gamma_row, w, n_out):
        kb = self.kb
        xg, rstd = self.norm_prologue(gamma_row)
        wv = w.rearrange("(kt p) n -> p kt n", p=128)
        zo = [kb.tile([128, T], F32, "zo") for _ in range(2)]

        def epi(n, si, c0, c1, ps, pb):
            z_ = zo[n % 2]
            eng = "dve"
            kb.op(eng, lambda e: e.tensor_tensor(out=z_[:, c0:c1], in0=ps, in1=rstd[:, c0:c1], op=ALU.mult), reads=[pb, rstd], writes=[z_])
            if si == len(SUBS) - 1:
                kb.dma("sp", self.Z[n * 128:(n + 1) * 128, :], z_[:], reads=[z_])

        self.gemm(lambda n: wv[:, :, n * 128:(n + 2) * 128], 32, n_out // 128, lambda kt, c0, c1: xg[:, kt, c0:c1], [xg], epi, nw=2, wn=2)
        kb.phase_end()

    def phase_wout(self, w):
        kb = self.kb
        mx = kb.tile([128, 32, T], BF16, "mx")
        Mv = self.MIX.rearrange("(kt p) t -> p kt t", p=128)
        for q in range(4):
            kb.dma("sp", mx[:, q * 8:(q + 1) * 8, :], Mv[:, q * 8:(q + 1) * 8, :], writes=[mx])
        wv = w.rearrange("(kt p) n -> p kt n", p=128)
        self.resid_gemm(wv, 32, mx, SUBS)
        kb.phase_end()

    def resid_gemm(self, wv, KT, rhs_tile, subs, col_lo=0, col_hi=T, nw=3):
        kb = self.kb
        ncols = col_hi - col_lo
        xr = [kb.tile([128, ncols], F32, "xr") for _ in range(3)]
        xbufs = [Buf() for _ in range(32)]

        def wview(n):
            return wv[:, :, n * 128:(n + 2) * 128]

        def epi(n, si, c0, c1, ps, pb):
            x_ = xr[n % 3]
            if si == 0:
                kb.dma("sp", x_[:], self.X[n * 128:(n + 1) * 128, col_lo:col_hi], reads=[xbufs[n]], writes=[x_])
            kb.op("dve", lambda e: e.tensor_tensor(out=x_[:, c0:c1], in0=ps, in1=x_[:, c0:c1], op=ALU.add), reads=[pb, x_], writes=[x_])
            if si == len(subs) - 1:
                kb.dma("sp", self.X[n * 128:(n + 1) * 128, col_lo:col_hi], x_[:], reads=[x_], writes=[xbufs[n]])

        self.gemm(wview, KT, 32, lambda kt, c0, c1: rhs_tile[:, kt, c0:c1], [rhs_tile], epi, subs=subs, nw=2, wn=2)

    def phase_ffn(self, l):
        kb = self.kb
        xg, rstd = self.norm_prologue(self.i["norm_ffn"][l])
        w1v = self.i["ffn_w1"][l].rearrange("(kt p) n -> p kt n", p=128)
        w3v = self.i["ffn_w3"][l].rearrange("(kt p) n -> p kt n", p=128)
        sa = [kb.tile([128, T], F32, "sa") for _ in range(2)]
        ao = [kb.tile([128, T], BF16, "ao") for _ in range(2)]
        tb = [kb.tile([128, 512], F32, "tb") for _ in range(2)]
        cnt = [0]

        def epi(n, si, c0, c1, ps, pb):
            j, which = n // 2, n % 2
            s_ = sa[j % 2]
            a_ = ao[j % 2]
            if which == 0:
                kb.op("dve", lambda e: e.tensor_tensor(out=s_[:, c0:c1], in0=ps, in1=rstd[:, c0:c1], op=ALU.mult), reads=[pb, rstd], writes=[s_])
                kb.op("act", lambda e: e.activation(out=s_[:, c0:c1], in_=s_[:, c0:c1], func=AF.Silu), reads=[s_], writes=[s_])
            else:
                t_ = tb[cnt[0] % 2]
                cnt[0] += 1
                kb.op("dve", lambda e: e.tensor_tensor(out=t_[:, 0:c1 - c0], in0=ps, in1=rstd[:, c0:c1], op=ALU.mult), reads=[pb, rstd], writes=[t_])
                kb.op("dve", lambda e: e.tensor_tensor(out=a_[:, c0:c1], in0=t_[:, 0:c1 - c0], in1=s_[:, c0:c1], op=ALU.mult), reads=[t_, s_], writes=[a_])
                if si == len(SUBS) - 1:
                    kb.dma("sp", self.ACTB[j * 128:(j + 1) * 128, :], a_[:], reads=[a_])

        def wview(n):
            j, which = n // 2, n % 2
            return (w1v if which == 0 else w3v)[:, :, j * 128:(j + 1) * 128]

        self.gemm(wview, 32, 2 * NJ, lambda kt, c0, c1: xg[:, kt, c0:c1], [xg], epi)
        kb.phase_end()
        w2 = self.i["ffn_w2"][l]
        Av = self.ACTB.rearrange("(kt p) t -> p kt t", p=128)
        Xv = self.X.rearrange("(kt p) t -> p kt t", p=128)
        P = self.psum
        PB = self.psb
        NB = 8
        for (lo, hi) in [(0, 512), (512, 1024), (1024, 1536), (1536, T)]:
            has_s = (hi - lo) > 512
            cw = 7 if has_s else 8
            at = kb.tile([128, NJ, hi - lo], BF16, "at")
            for q in range(0, NJ, 22):
                q1 = min(NJ, q + 22)
                kb.dma("sp", at[:, q:q1, :], Av[:, q:q1, lo:hi], writes=[at])
            wts = [kb.tile([128, 8 * 128], BF16, "w2t") for _ in range(NB)]
            wfs = [kb.tile([128, 8 * 128], F32, "w2f") for _ in range(NB)]
            xrs = [kb.tile([128, 8, hi - lo], F32, "xr2") for _ in range(2)]
            zb = kb.tile([128, 128], BF16, "zb")
            kb.op("dve", lambda e, zb=zb: e.memset(zb[:], 0.0), writes=[zb])
            wi = 0
            for ci, n0 in enumerate(range(0, 32, cw)):
                n1 = min(32, n0 + cw)
                nn = n1 - n0
                xr = xrs[ci % 2]
                kb.dma("pool", xr[:, 0:nn, :], Xv[:, n0:n1, lo:hi], writes=[xr])
                if has_s:
                    kb.op("pe", lambda e, nn=nn, at=at, zb=zb: e.matmul(out=P[7][:, 0:nn * S], lhsT=zb[:], rhs=at[:, 0, 0:nn * S], start=True, stop=False, skip_group_check=True),
                          reads=[zb, at], writes=[PB[7]])
                for kt in range(NJ):
                    wt = wts[wi % NB]
                    wi += 1
                    wf = wfs[wi % NB]
                    kb.dma("sp", wf[:, 0:nn * 128], w2[kt * 128:(kt + 1) * 128, n0 * 128:n1 * 128], writes=[wf], nslots=8)
                    kb.op("act", lambda e, wt=wt, wf=wf, nn=nn: e.copy(out=wt[:, 0:nn * 128], in_=wf[:, 0:nn * 128]), reads=[wf], writes=[wt])

                    def mm(e, wt=wt, kt=kt, nn=nn, at=at, has_s=has_s):
                        for n in range(nn):
                            ins = e.matmul(out=P[n][:, 0:512], lhsT=wt[:, n * 128:(n + 1) * 128], rhs=at[:, kt, 0:512], start=(kt == 0), stop=(kt == NJ - 1))
                            if has_s:
                                ins = e.matmul(out=P[7][:, n * S:(n + 1) * S], lhsT=wt[:, n * 128:(n + 1) * 128], rhs=at[:, kt, 512:512 + S], start=False, stop=(kt == NJ - 1), skip_group_check=True)
                        return ins
                    kb.op("pe", mm, reads=[wt, at], writes=[PB[n] for n in range(nn)] + ([PB[7]] if has_s else []))
                for n in range(nn):
                    kb.op("dve", lambda e, n=n, xr=xr: e.tensor_tensor(out=xr[:, n, 0:512], in0=P[n][:, 0:512], in1=xr[:, n, 0:512], op=ALU.add), reads=[PB[n], xr], writes=[xr])
                if has_s:
                    kb.op("dve", lambda e, nn=nn, xr=xr: e.tensor_tensor(out=xr[:, 0:nn, 512:512 + S], in0=P[7][:, 0:nn * S].rearrange("p (a c) -> p a c", c=S), in1=xr[:, 0:nn, 512:512 + S], op=ALU.add),
                          reads=[PB[7], xr], writes=[xr])
                kb.dma("pool", Xv[:, n0:n1, lo:hi], xr[:, 0:nn, :], reads=[xr])
            kb.bank_rr = 0
            kb.phase_end()

def _stt(e, out, in0, scalar, in1, op0=ALU.mult, op1=ALU.mult):
    return e.scalar_tensor_tensor(out=out, in0=in0, scalar=scalar, in1=in1, op0=op0, op1=op1)


def phase_attn(self, i):
    kb = self.kb
    P = self.psum
    PB = self.psb
    qn = self.load_cols(self.i["q_norm"][i], 1, "qn")
    kn = self.load_cols(self.i["k_norm"][i], 1, "kn")
    qns = kb.tile([128, 1], F32, "qns")
    kb.op("dve", lambda e: e.tensor_scalar(out=qns[:], in0=qn[:], scalar1=ATT_SCALE, scalar2=None, op0=ALU.mult), reads=[qn], writes=[qns])
    wm = kb.tile([128, 2560], BF16, "wm")
    kb.op("dve", lambda e: e.tensor_copy(out=wm[:], in_=self.cst.t[:, C_WM:C_WM + 2560]), reads=[self.cst], writes=[wm])
    ws = kb.tile([128, 136], BF16, "ws")
    kb.op("dve", lambda e: e.tensor_copy(out=ws[:], in_=self.cst.t[:, C_WS:C_WS + 136]), reads=[self.cst], writes=[ws])
    kpv = self.o["k_prompt"][i].rearrange("(tt p) h d -> p tt h d", p=128)
    vpv = self.o["v_prompt"][i].rearrange("(tt p) h d -> p tt h d", p=128)
    ckv = self.i["cache_k"][i].rearrange("(tt p) h d -> p tt h d", p=128)
    cvv = self.i["cache_v"][i].rearrange("(tt p) h d -> p tt h d", p=128)
    qf = kb.tile([128, T], F32, "qf")
    kf = kb.tile([128, T], F32, "kf")
    vf = kb.tile([128, T], F32, "vf")
    sqb = kb.tile([128, T], BF16, "sqb")
    rq = kb.tile([128, T], F32, "rq")
    rk = kb.tile([128, T], F32, "rk")
    qb = kb.tile([128, T], BF16, "qb")
    knf = kb.tile([128, T], F32, "knf")
    kbt = kb.tile([128, T], BF16, "kbt")
    kout = kb.tile([128, 16, 128], F32, "kout")
    vout = kb.tile([128, 16, 128], F32, "vout")
    vtok = kb.tile([128, 16, 128], BF16, "vtok")
    ks = kb.tile([S, 128], F32, "ks")
    vs = kb.tile([S, 128], F32, "vs")
    vsb = kb.tile([S, 128], BF16, "vsb")
    pe_ = [kb.tile([128, 512], BF16, "pe") for _ in range(3)]
    pm_ = [kb.tile([128, 512], BF16, "pm") for _ in range(3)]
    rl = kb.tile([128, 512], F32, "rl")
    ao = [kb.tile([128, T], BF16, "ao") for _ in range(2)]
    ck = kb.tile([128, 16, 128], F32, "ck")
    cv = kb.tile([128, 16, 128], F32, "cv")
    vcb = kb.tile([128, 16, 128], BF16, "vcb")
    kcT = kb.tile([128, 2048], BF16, "kcT")
    srr = [0]

    def sbank():
        b = srr[0] % 4
        srr[0] += 1
        return b

    for hd in range(ATT_HEADS):
        a_ = ao[hd % 2]
        kb.dma("sp", qf[:], self.Z[hd * 128:(hd + 1) * 128, :], writes=[qf])
        kb.dma("sp", kf[:], self.Z[2048 + hd * 128:2048 + (hd + 1) * 128, :], writes=[kf])
        kb.dma("sp", vf[:], self.Z[4096 + hd * 128:4096 + (hd + 1) * 128, :], writes=[vf])
        for hh in range(2):
            kb.dma("sp", ck[:, hh * 8:(hh + 1) * 8, :], ckv[:, hh * 8:(hh + 1) * 8, hd, :], writes=[ck])
            kb.dma("sp", cv[:, hh * 8:(hh + 1) * 8, :], cvv[:, hh * 8:(hh + 1) * 8, hd, :], writes=[cv])
        if 'early' in ATT_PARTS:
            for hh in range(2):
                kb.dma(ATT_Q, kpv[:, hh * 8:(hh + 1) * 8, hd, :], ck[:, hh * 8:(hh + 1) * 8, :], reads=[ck])
        if 'xload' in ATT_PARTS:
            for hh in range(2):
                kb.dma(ATT_Q, vcb[:, hh * 8:(hh + 1) * 8, :].bitcast(F32)[:, :, 0:64] if False else kout[:, hh * 8:(hh + 1) * 8, :], ckv[:, hh * 8:(hh + 1) * 8, hd, :], writes=[kout])
        for (src, dst) in (((qf, rq), (kf, rk)) if 'norm' in ATT_PARTS else []):
            kb.op("act", lambda e, src=src: e.activation(out=sqb[:], in_=src[:], func=AF.Square), reads=[src], writes=[sqb])
            for (c0, c1) in SUBS:
                b = sbank()
                kb.op("pe", lambda e, b=b, c0=c0, c1=c1: e.matmul(out=P[b][:, 0:c1 - c0], lhsT=self.ones_b[:], rhs=sqb[:, c0:c1], start=True, stop=True),
                      reads=[sqb, self.ones_b], writes=[PB[b]])
                kb.op("act", lambda e, b=b, c0=c0, c1=c1, dst=dst: e.activation(out=dst[:, c0:c1], in_=P[b][:, 0:c1 - c0], func=AF.Sqrt, bias=1e-6, scale=1.0 / 128),
                      reads=[PB[b]], writes=[dst])
            kb.op("dve", lambda e, dst=dst: e.reciprocal(out=dst[:], in_=dst[:]), reads=[dst], writes=[dst])
        kb.op("dve", lambda e: _stt(e, qb[:], qf[:], qns[:, 0:1], rq[:]), reads=[qf, qns, rq], writes=[qb])
        kb.op("dve", lambda e: _stt(e, knf[:], kf[:], kn[:, 0:1], rk[:]), reads=[kf, kn, rk], writes=[knf])
        kb.op("act", lambda e: e.copy(out=kbt[:], in_=knf[:]), reads=[knf], writes=[kbt])
        if 'tr' not in ATT_PARTS:
            continue
        for (src, dstf, dstb) in (((knf, kout, None), (vf, vout, vtok)) if 'trk' not in ATT_PARTS else ((knf, kout, None),)):
            for t4 in range(int(ATT_T4)):
                b = sbank()

                def tr(e, b=b, src=src, t4=t4):
                    for q in range(4):
                        tt = t4 * 4 + q
                        ins = e.transpose(out=P[b][:, q * 128:(q + 1) * 128], in_=src[:, tt * 128:(tt + 1) * 128], identity=self.ident)
                    return ins
                kb.op("pe", tr, reads=[src, self.cst], writes=[PB[b]])
                if 'trnoevac' in ATT_PARTS:
                    continue
                kb.op("dve", lambda e, b=b, dstf=dstf, t4=t4: e.tensor_copy(out=dstf[:, t4 * 4:(t4 + 1) * 4, :], in_=P[b][:].rearrange("p (a c) -> p a c", c=128)),
                      reads=[PB[b]], writes=[dstf])
                if dstb is not None:
                    kb.op("act", lambda e, dstf=dstf, dstb=dstb, t4=t4: e.copy(out=dstb[:, t4 * 4:(t4 + 1) * 4, :], in_=dstf[:, t4 * 4:(t4 + 1) * 4, :]),
                          reads=[dstf], writes=[dstb])
        if 'trdma' in ATT_PARTS:
            for hh in range(2):
                if 'nok' not in ATT_PARTS:
                    src_t = ck if 'srcck' in ATT_PARTS else kout
                    dst_ap = self.Z[0:128, hh * 1024:(hh + 1) * 1024].rearrange("p (a b) -> p a b", b=128) if 'dstz' in ATT_PARTS else kpv[:, hh * 8:(hh + 1) * 8, hd, :]
                    kb.dma(ATT_Q, dst_ap, src_t[:, hh * 8:(hh + 1) * 8, :], reads=[src_t])
                if 'nov' not in ATT_PARTS:
                    kb.dma("sp", vpv[:, hh * 8:(hh + 1) * 8, hd, :], vout[:, hh * 8:(hh + 1) * 8, :], reads=[vout])
        if 'trs' not in ATT_PARTS:
            continue
        b = sbank()

        def trs(e, b=b):
            e.transpose(out=P[b][0:S, 0:128], in_=knf[:, L:T], identity=self.ident)
            return e.transpose(out=P[b][0:S, 128:256], in_=vf[:, L:T], identity=self.ident)
        kb.op("pe", trs, reads=[knf, vf, self.cst], writes=[PB[b]])
        kb.op("dve", lambda e, b=b: e.tensor_copy(out=ks[:], in_=P[b][0:S, 0:128]), reads=[PB[b]], writes=[ks])
        kb.op("dve", lambda e, b=b: e.tensor_copy(out=vs[:], in_=P[b][0:S, 128:256]), reads=[PB[b]], writes=[vs])
        kb.op("act", lambda e: e.copy(out=vsb[:], in_=vs[:]), reads=[vs], writes=[vsb])
        kb.dma("sp", self.o["k_sample"][i][:, hd, :], ks[:], reads=[ks])
        kb.dma("sp", self.o["v_sample"][i][:, hd, :], vs[:], reads=[vs])
        cnt = 0
        for qs in (range(4) if 'prompt' in ATT_PARTS else []):
            ob, lb = (4, 5) if qs % 2 == 0 else (6, 7)
            nk = 4 * qs + 4
            for kt in range(nk):
                b = sbank()
                p1 = pe_[cnt % 3]
                p2 = pm_[cnt % 3]
                cnt += 1
                off = 384 + qs * 512 - kt * 128
                kb.op("pe", lambda e, b=b, kt=kt, qs=qs: e.matmul(out=P[b][:], lhsT=kbt[:, kt * 128:(kt + 1) * 128], rhs=qb[:, qs * 512:(qs + 1) * 512], start=True, stop=True),
                      reads=[kbt, qb], writes=[PB[b]])
                kb.op("act", lambda e, b=b, p1=p1: e.activation(out=p1[:], in_=P[b][:], func=AF.Exp, bias=EXP_SHIFT, scale=1.0), reads=[PB[b]], writes=[p1])
                eng = "dve" if cnt % 3 != 0 else "pool"
                kb.op(eng, lambda e, p1=p1, p2=p2, off=off: e.tensor_tensor(out=p2[:], in0=p1[:], in1=wm[:, off:off + 512], op=ALU.mult), reads=[p1, wm], writes=[p2])

                def pv(e, p2=p2, kt=kt, nk=nk, ob=ob, lb=lb):
                    e.matmul(out=P[ob][:], lhsT=vtok[:, kt, :], rhs=p2[:], start=(kt == 0), stop=(kt == nk - 1))
                    return e.matmul(out=P[lb][:], lhsT=self.ones_b[:], rhs=p2[:], start=(kt == 0), stop=(kt == nk - 1))
                kb.op("pe", pv, reads=[p2, vtok, self.ones_b], writes=[PB[ob], PB[lb]])
            kb.op("dve", lambda e, lb=lb: e.reciprocal(out=rl[:], in_=P[lb][:]), reads=[PB[lb]], writes=[rl])
            kb.op("dve", lambda e, ob=ob, qs=qs, a_=a_: e.tensor_tensor(out=a_[:, qs * 512:(qs + 1) * 512], in0=P[ob][:], in1=rl[:], op=ALU.mult), reads=[PB[ob], rl], writes=[a_])
        if 'sample' not in ATT_PARTS:
            continue
        kb.op("pool", lambda e: e.tensor_copy(out=vcb[:], in_=cv[:]), reads=[cv], writes=[vcb])
        for t4 in range(4):
            b = sbank()

            def trc(e, b=b, t4=t4):
                for q in range(4):
                    ins = e.transpose(out=P[b][:, q * 128:(q + 1) * 128], in_=ck[:, t4 * 4 + q, :], identity=self.ident)
                return ins
            kb.op("pe", trc, reads=[ck, self.cst], writes=[PB[b]])
            kb.op("act", lambda e, b=b, t4=t4: e.copy(out=kcT[:, t4 * 512:(t4 + 1) * 512], in_=P[b][:]), reads=[PB[b]], writes=[kcT])
        b = sbank()

        def sc(e, b=b):
            for kt in range(16):
                e.matmul(out=P[b][:, kt * S:(kt + 1) * S], lhsT=kcT[:, kt * 128:(kt + 1) * 128], rhs=qb[:, L:T], start=True, stop=True)
            return e.matmul(out=P[b][0:S, 128:128 + S], lhsT=kbt[:, L:T], rhs=qb[:, L:T], start=True, stop=True)
        kb.op("pe", sc, reads=[kcT, kbt, qb], writes=[PB[b]])
        p1 = pe_[cnt % 3]
        p2 = pm_[cnt % 3]
        cnt += 1
        kb.op("act", lambda e, b=b, p1=p1: e.activation(out=p1[:, 0:128], in_=P[b][:, 0:128], func=AF.Exp, bias=EXP_SHIFT, scale=1.0), reads=[PB[b]], writes=[p1])
        kb.op("act", lambda e, b=b, p1=p1: e.activation(out=p1[0:S, 128:136], in_=P[b][0:S, 128:136], func=AF.Exp, bias=EXP_SHIFT, scale=1.0), reads=[PB[b]], writes=[p1])
        kb.op("dve", lambda e, p1=p1, p2=p2: e.tensor_tensor(out=p2[:, 0:128], in0=p1[:, 0:128], in1=ws[:, 0:128], op=ALU.mult), reads=[p1, ws], writes=[p2])
        kb.op("dve", lambda e, p1=p1, p2=p2: e.tensor_tensor(out=p2[0:S, 128:136], in0=p1[0:S, 128:136], in1=ws[0:S, 128:136], op=ALU.mult), reads=[p1, ws], writes=[p2])
        ob, lb = 4, 5

        def pvs(e, p2=p2):
            for kt in range(16):
                e.matmul(out=P[ob][:, 0:S], lhsT=vcb[:, kt, :], rhs=p2[:, kt * S:(kt + 1) * S], start=(kt == 0), stop=False)
            e.matmul(out=P[ob][:, 0:S], lhsT=vsb[:], rhs=p2[0:S, 128:136], start=False, stop=True)
            for kt in range(16):
                e.matmul(out=P[lb][:, 0:S], lhsT=self.ones_b[:], rhs=p2[:, kt * S:(kt + 1) * S], start=(kt == 0), stop=False)
            return e.matmul(out=P[lb][:, 0:S], lhsT=self.ones_b[0:S, :], rhs=p2[0:S, 128:136], start=False, stop=True)
        kb.op("pe", pvs, reads=[p2, vcb, vsb, self.ones_b], writes=[PB[ob], PB[lb]])
        kb.op("dve", lambda e: e.reciprocal(out=rl[:, 0:S], in_=P[lb][:, 0:S]), reads=[PB[lb]], writes=[rl])
        kb.op("dve", lambda e, a_=a_: e.tensor_tensor(out=a_[:, L:T], in0=P[ob][:, 0:S], in1=rl[:, 0:S], op=ALU.mult), reads=[PB[ob], rl], writes=[a_])
        kb.dma("sp", self.MIX[hd * 128:(hd + 1) * 128, :], a_[:], reads=[a_])
    kb.bank_rr = 0
    kb.phase_end()


def phase_sgu(self, i):
    kb = self.kb
    P = self.psum
    PB = self.psb
    lg = self.load_cols(self.i["sgu_ln_g"][i], 16, "lg")
    lbv = self.load_cols(self.i["sgu_ln_b"][i], 16, "lb")
    tril = self.cst.t[:, C_TRIL:C_TRIL + 128]
    wst = kb.tile([128, 8, 128], BF16, "wst")
    wraw = [kb.tile([128, 128], F32, "wraw") for _ in range(2)]
    for g in range(8):
        w_ = wraw[g % 2]
        kb.dma("sp", w_[:], self.i["sgu_w"][i, g], writes=[w_])
        kb.op("dve", lambda e, w_=w_: e.tensor_tensor(out=w_[:], in0=w_[:], in1=tril, op=ALU.mult), reads=[w_, self.cst], writes=[w_])
        b = g % 4
        kb.op("pe", lambda e, b=b, w_=w_: e.transpose(out=P[b][:, 0:128], in_=w_[:], identity=self.ident), reads=[w_, self.cst], writes=[PB[b]])
        kb.op("act", lambda e, b=b, g=g: e.copy(out=wst[:, g, :], in_=P[b][:, 0:128]), reads=[PB[b]], writes=[wst])
    sbr = kb.tile([1, 1024], F32, "sbr")
    kb.dma("sp", sbr[:], self.i["sgu_b"][i].rearrange("g t -> (g t)").rearrange("(o n) -> o n", o=1), writes=[sbr])
    sbb = kb.tile([1, 1024], BF16, "sbb")
    kb.op("dve", lambda e: e.tensor_copy(out=sbb[:], in_=sbr[:]), reads=[sbr], writes=[sbb])
    GUG = kb.tile([128, 16, T], BF16, "GUG")
    VN = kb.tile([128, 16, T], BF16, "VN")
    gsf = kb.tile([128, 16, S], F32, "gsf")
    vns = kb.tile([128, 16, S], F32, "vns")
    sgu_mark = kb.sb_mark()
    gvf = [kb.tile([128, T], F32, "gvf") for _ in range(2)]
    sqf = [kb.tile([128, L], F32, "sqf") for _ in range(2)]
    guf = [kb.tile([128, T], F32, "guf") for _ in range(2)]
    for ft in range(16):
        g_ = gvf[ft % 2]
        s_ = sqf[ft % 2]
        u_ = guf[ft % 2]
        kb.dma("sp", g_[:], self.Z[8192 + ft * 128:8192 + (ft + 1) * 128, :], writes=[g_])
        kb.dma("sp", u_[:], self.Z[6144 + ft * 128:6144 + (ft + 1) * 128, :], writes=[u_])
        kb.op("act", lambda e, g_=g_: e.activation(out=g_[:], in_=g_[:], func=AF.Gelu), reads=[g_], writes=[g_])
        kb.op("dve", lambda e, g_=g_, ft=ft: e.tensor_copy(out=VN[:, ft, :], in_=g_[:]), reads=[g_], writes=[VN])
        kb.op("dve", lambda e, g_=g_, ft=ft: e.tensor_copy(out=gsf[:, ft, :], in_=g_[:, L:T]), reads=[g_], writes=[gsf])
        kb.op("pool", lambda e, g_=g_, s_=s_: e.tensor_tensor(out=s_[:], in0=g_[:, 0:L], in1=g_[:, 0:L], op=ALU.mult), reads=[g_], writes=[s_])

        def st(e, g_=g_, s_=s_, ft=ft):
            for si in range(4):
                c0, c1 = SUBS[si]
                e.matmul(out=P[si][:], lhsT=self.ones_f[:], rhs=g_[:, c0:c1], start=(ft == 0), stop=(ft == 15))
                ins = e.matmul(out=P[4 + si][:], lhsT=self.ones_f[:], rhs=s_[:, c0:c1], start=(ft == 0), stop=(ft == 15))
            return ins
        kb.op("pe", st, reads=[g_, s_, self.ones_f], writes=PB)
        kb.op("act", lambda e, u_=u_, ft=ft: e.activation(out=GUG[:, ft, :], in_=u_[:], func=AF.Gelu), reads=[u_], writes=[GUG])
    kb.barrier()
    kb.sb_reset(sgu_mark)
    mu = kb.tile([128, L], F32, "mu")
    rs = kb.tile([128, L], F32, "rs")
    nmr = kb.tile([128, L], F32, "nmr")
    for si in range(4):
        c0, c1 = SUBS[si]
        kb.op("act", lambda e, si=si, c0=c0, c1=c1: e.activation(out=mu[:, c0:c1], in_=P[si][:], func=AF.Copy, scale=1.0 / 2048), reads=[PB[si]], writes=[mu])
        kb.op("dve", lambda e, c0=c0, c1=c1: e.tensor_tensor(out=nmr[:, c0:c1], in0=mu[:, c0:c1], in1=mu[:, c0:c1], op=ALU.mult), reads=[mu], writes=[nmr])
        kb.op("dve", lambda e, si=si, c0=c0, c1=c1: _stt(e, rs[:, c0:c1], P[4 + si][:], 1.0 / 2048, nmr[:, c0:c1], ALU.mult, ALU.subtract), reads=[PB[4 + si], nmr], writes=[rs])
    kb.op("act", lambda e: e.activation(out=rs[:], in_=rs[:], func=AF.Sqrt, bias=1e-5, scale=1.0), reads=[rs], writes=[rs])
    kb.op("dve", lambda e: e.reciprocal(out=rs[:], in_=rs[:]), reads=[rs], writes=[rs])
    kb.op("dve", lambda e: _stt(e, nmr[:], mu[:], -1.0, rs[:]), reads=[mu, rs], writes=[nmr])
    tmp = [kb.tile([128, L], F32, "ntmp") for _ in range(2)]
    for ft in range(16):
        t_ = tmp[ft % 2]
        kb.op("dve", lambda e, t_=t_, ft=ft: e.tensor_tensor(out=t_[:], in0=VN[:, ft, 0:L], in1=rs[:], op=ALU.mult), reads=[VN, rs], writes=[t_])
        kb.op("pool", lambda e, t_=t_: e.tensor_tensor(out=t_[:], in0=t_[:], in1=nmr[:], op=ALU.add), reads=[t_, nmr], writes=[t_])
        kb.op("dve", lambda e, t_=t_, ft=ft: e.tensor_scalar(out=VN[:, ft, 0:L], in0=t_[:], scalar1=lg[:, ft:ft + 1], scalar2=lbv[:, ft:ft + 1], op0=ALU.mult, op1=ALU.add),
              reads=[t_, lg, lbv], writes=[VN])
    sqs = kb.tile([128, 16, S], F32, "sqs")
    kb.op("dve", lambda e: e.tensor_tensor(out=sqs[:], in0=gsf[:], in1=gsf[:], op=ALU.mult), reads=[gsf], writes=[sqs])

    def sts(e):
        for ft in range(16):
            e.matmul(out=P[0][:, 0:S], lhsT=self.ones_f[:], rhs=gsf[:, ft, :], start=(ft == 0), stop=(ft == 15))
        for ft in range(16):
            ins = e.matmul(out=P[1][:, 0:S], lhsT=self.ones_f[:], rhs=sqs[:, ft, :], start=(ft == 0), stop=(ft == 15))
        return ins
    kb.op("pe", sts, reads=[gsf, sqs, self.ones_f], writes=[PB[0], PB[1]])
    mus = kb.tile([128, S], F32, "mus")
    rss = kb.tile([128, S], F32, "rss")
    nms = kb.tile([128, S], F32, "nms")
    kb.op("act", lambda e: e.activation(out=mus[:], in_=P[0][:, 0:S], func=AF.Copy, scale=1.0 / 2048), reads=[PB[0]], writes=[mus])
    kb.op("dve", lambda e: e.tensor_tensor(out=nms[:], in0=mus[:], in1=mus[:], op=ALU.mult), reads=[mus], writes=[nms])
    kb.op("dve", lambda e: _stt(e, rss[:], P[1][:, 0:S], 1.0 / 2048, nms[:], ALU.mult, ALU.subtract), reads=[PB[1], nms], writes=[rss])
    kb.op("act", lambda e: e.activation(out=rss[:], in_=rss[:], func=AF.Sqrt, bias=1e-5, scale=1.0), reads=[rss], writes=[rss])
    kb.op("dve", lambda e: e.reciprocal(out=rss[:], in_=rss[:]), reads=[rss], writes=[rss])
    kb.op("dve", lambda e: _stt(e, nms[:], mus[:], -1.0, rss[:]), reads=[mus, rss], writes=[nms])
    for ft in range(16):
        kb.op("dve", lambda e, ft=ft: e.tensor_tensor(out=vns[:, ft, :], in0=gsf[:, ft, :], in1=rss[:], op=ALU.mult), reads=[gsf, rss], writes=[vns])
        kb.op("dve", lambda e, ft=ft: e.tensor_tensor(out=vns[:, ft, :], in0=vns[:, ft, :], in1=nms[:], op=ALU.add), reads=[vns, nms], writes=[vns])
        kb.op("dve", lambda e, ft=ft: e.tensor_scalar(out=vns[:, ft, :], in0=vns[:, ft, :], scalar1=lg[:, ft:ft + 1], scalar2=lbv[:, ft:ft + 1], op0=ALU.mult, op1=ALU.add),
              reads=[vns, lg, lbv], writes=[vns])
    kb.op("dve", lambda e: e.tensor_copy(out=VN[:, :, L:T], in_=vns[:]), reads=[vns], writes=[VN])
    ysv = kb.tile([S, 2048], F32, "ysv")
    for f4 in range(4):
        b = 2 + (f4 % 2)

        def trv(e, b=b, f4=f4):
            for q in range(4):
                ins = e.transpose(out=P[b][0:S, q * 128:(q + 1) * 128], in_=vns[:, f4 * 4 + q, :], identity=self.ident)
            return ins
        kb.op("pe", trv, reads=[vns, self.cst], writes=[PB[b]])
        kb.op("dve", lambda e, b=b, f4=f4: e.tensor_copy(out=ysv[:, f4 * 512:(f4 + 1) * 512], in_=P[b][0:S, :]), reads=[PB[b]], writes=[ysv])
    kb.dma("sp", self.o["sgu_v_sample"][i], ysv[:], reads=[ysv])
    kb.barrier()
    kb.sb_reset(sgu_mark)
    Mv = self.MIX.rearrange("(kt p) t -> p kt t", p=128)
    vtok = [kb.tile([128, 2048], BF16, "vtk") for _ in range(2)]
    mo = [kb.tile([128, 16, 128], BF16, "mo") for _ in range(2)]
    for ch in range(16):
        vt = vtok[ch % 2]
        m_ = mo[ch % 2]
        for h in range(2):
            b = h
            pbf = P[b][:].bitcast(BF16)

            def trn(e, pbf=pbf, h=h, ch=ch):
                for q in range(8):
                    ins = e.transpose(out=pbf[:, q * 128:(q + 1) * 128], in_=VN[:, h * 8 + q, ch * 128:(ch + 1) * 128], identity=self.identb[:])
                return ins
            kb.op("pe", trn, reads=[VN, self.identb], writes=[PB[b]])
            kb.op("act", lambda e, pbf=pbf, vt=vt, h=h: e.copy(out=vt[:, h * 1024:(h + 1) * 1024], in_=pbf), reads=[PB[b]], writes=[vt])
        for f4 in range(4):
            b = 2 + (f4 + 4 * ch) % 6

            def mx(e, b=b, f4=f4, vt=vt):
                for q in range(4):
                    ft = f4 * 4 + q
                    g = ft // 2
                    e.matmul(out=P[b][:, q * 128:(q + 1) * 128], lhsT=vt[:, ft * 128:(ft + 1) * 128], rhs=wst[:, g, :], start=True, stop=False)
                    ins = e.matmul(out=P[b][:, q * 128:(q + 1) * 128], lhsT=self.ones_b[0:1, :], rhs=sbb[0:1, g * 128:(g + 1) * 128], start=False, stop=True)
                return ins
            kb.op("pe", mx, reads=[vt, wst, sbb, self.ones_b], writes=[PB[b]])
            kb.op("dve", lambda e, b=b, f4=f4, m_=m_, ch=ch: e.tensor_tensor(out=m_[:, f4 * 4:(f4 + 1) * 4, :], in0=P[b][:].rearrange("p (a c) -> p a c", c=128),
                                                                  in1=GUG[:, f4 * 4:(f4 + 1) * 4, ch * 128:(ch + 1) * 128], op=ALU.mult),
                  reads=[PB[b], GUG], writes=[m_])
        kb.dma("sp", Mv[:, 16:32, ch * 128:(ch + 1) * 128], m_[:], reads=[m_])
    vts = kb.tile([S, 2048], BF16, "vts")
    for h in range(2):
        b = h
        pbf = P[b][:].bitcast(BF16)

        def trn2(e, pbf=pbf, h=h):
            for q in range(8):
                ins = e.transpose(out=pbf[0:S, q * 128:(q + 1) * 128], in_=VN[:, h * 8 + q, L:T], identity=self.identb[:])
            return ins
        kb.op("pe", trn2, reads=[VN, self.identb], writes=[PB[b]])
        kb.op("act", lambda e, pbf=pbf, h=h: e.copy(out=vts[:, h * 1024:(h + 1) * 1024], in_=pbf[0:S, :]), reads=[PB[b]], writes=[vts])
    mos = kb.tile([128, 16, S], BF16, "mos")
    b = 2

    def mxs(e):
        for ft in range(16):
            g = ft // 2
            e.matmul(out=P[b][:, ft * S:(ft + 1) * S], lhsT=vts[0:S, ft * 128:(ft + 1) * 128], rhs=wst[0:S, g, 0:S], start=True, stop=False)
            ins = e.matmul(out=P[b][:, ft * S:(ft + 1) * S], lhsT=self.ones_b[0:1, :], rhs=sbb[0:1, g * 128:g * 128 + S], start=False, stop=True)
        return ins
    kb.op("pe", mxs, reads=[vts, wst, sbb, self.ones_b], writes=[PB[b]])
    kb.op("dve", lambda e: e.tensor_tensor(out=mos[:], in0=P[b][:, 0:16 * S].rearrange("p (a c) -> p a c", c=S), in1=GUG[:, :, L:T], op=ALU.mult), reads=[PB[b], GUG], writes=[mos])
    kb.dma("sp", Mv[:, 16:32, L:T], mos[:], reads=[mos])
    kb.bank_rr = 0
    kb.phase_end()


Prog.phase_attn = phase_attn
Prog.phase_sgu = phase_sgu


NK = 11


def phase_s5(self, i):
    kb = self.kb
    P = self.psum
    PB = self.psb
    V = "dve"
    NP = 5 + 2 * NK
    PT = kb.tile([128, NP + NK + 1, 64], F32, "PT")
    prep_mark = kb.sb_mark()

    def t64(name):
        return kb.tile([64, 128], F32, name)
    lre, lim, h0r, h0i = t64("lre"), t64("lim"), t64("h0r"), t64("h0i")
    ldt = kb.tile([64, 2], F32, "ldt")
    kb.dma("sp", lre[:], self.i["s5_lambda_re"][i], writes=[lre])
    kb.dma("sp", lim[:], self.i["s5_lambda_im"][i], writes=[lim])
    kb.dma("sp", ldt[:], self.i["s5_log_dt"][i], writes=[ldt])
    kb.dma("sp", h0r[:], self.i["state_s5_re"][i], writes=[h0r])
    kb.dma("sp", h0i[:], self.i["state_s5_im"][i], writes=[h0i])
    dt = kb.tile([64, 2], F32, "dt")
    kb.op("act", lambda e: e.activation(out=dt[:], in_=ldt[:], func=AF.Exp), reads=[ldt], writes=[dt])
    are, aim, mag, thr, tmp, sn, sh, cs = [t64(n) for n in ("are", "aim", "mag", "thr", "tmp", "sn", "sh", "cs")]
    for gl in range(2):
        sl = slice(gl * 64, (gl + 1) * 64)
        kb.op(V, lambda e, sl=sl, gl=gl: e.tensor_scalar(out=are[:, sl], in0=lre[:, sl], scalar1=dt[:, gl:gl + 1], scalar2=None, op0=ALU.mult), reads=[lre, dt], writes=[are])
        kb.op(V, lambda e, sl=sl, gl=gl: e.tensor_scalar(out=aim[:, sl], in0=lim[:, sl], scalar1=dt[:, gl:gl + 1], scalar2=None, op0=ALU.mult), reads=[lim, dt], writes=[aim])
    kb.op("act", lambda e: e.activation(out=mag[:], in_=are[:], func=AF.Exp), reads=[are], writes=[mag])
    kb.op(V, lambda e: e.tensor_copy(out=thr[:], in_=aim[:]), reads=[aim], writes=[thr])
    for m in range(1, 9):
        kb.op(V, lambda e, m=m: e.tensor_scalar(out=tmp[:], in0=aim[:], scalar1=(2 * m - 1) * PI, scalar2=-2.0 * PI, op0=ALU.is_ge, op1=ALU.mult), reads=[aim], writes=[tmp])
        kb.op(V, lambda e: e.tensor_tensor(out=thr[:], in0=thr[:], in1=tmp[:], op=ALU.add), reads=[thr, tmp], writes=[thr])
    kb.op(V, lambda e: e.tensor_scalar(out=thr[:], in0=thr[:], scalar1=-PI, scalar2=PI, op0=ALU.max, op1=ALU.min), reads=[thr], writes=[thr])
    kb.op("act", lambda e: e.activation(out=sn[:], in_=thr[:], func=AF.Sin), reads=[thr], writes=[sn])
    kb.op("act", lambda e: e.activation(out=sh[:], in_=thr[:], func=AF.Sin, scale=0.5), reads=[thr], writes=[sh])
    kb.op(V, lambda e: e.tensor_tensor(out=tmp[:], in0=sh[:], in1=sh[:], op=ALU.mult), reads=[sh], writes=[tmp])
    kb.op(V, lambda e: e.tensor_scalar(out=cs[:], in0=tmp[:], scalar1=-2.0, scalar2=1.0, op0=ALU.mult, op1=ALU.add), reads=[tmp], writes=[cs])
    lbr, lbi, nr, dd, fr, fi, t2 = [t64(n) for n in ("lbr", "lbi", "nr", "dd", "fr", "fi", "t2")]

    def tt(out, a, b, op):
        kb.op(V, lambda e: e.tensor_tensor(out=out[:], in0=a[:], in1=b[:], op=op), reads=[a, b], writes=[out])
    tt(lbr, mag, cs, ALU.mult)
    tt(lbi, mag, sn, ALU.mult)
    kb.op(V, lambda e: e.tensor_scalar(out=nr[:], in0=lbr[:], scalar1=-1.0, scalar2=None, op0=ALU.add), reads=[lbr], writes=[nr])
    tt(dd, lre, lre, ALU.mult)
    tt(tmp, lim, lim, ALU.mult)
    tt(dd, dd, tmp, ALU.add)
    kb.op(V, lambda e: e.reciprocal(out=dd[:], in_=dd[:]), reads=[dd], writes=[dd])
    tt(fr, nr, lre, ALU.mult)
    tt(tmp, lbi, lim, ALU.mult)
    tt(fr, fr, tmp, ALU.add)
    tt(fr, fr, dd, ALU.mult)
    tt(fi, lbi, lre, ALU.mult)
    tt(tmp, nr, lim, ALU.mult)
    tt(fi, fi, tmp, ALU.subtract)
    tt(fi, fi, dd, ALU.mult)
    inr, ini = t64("inr"), t64("ini")
    tt(inr, cs, h0r, ALU.mult)
    tt(tmp, sn, h0i, ALU.mult)
    tt(inr, inr, tmp, ALU.subtract)
    tt(ini, sn, h0r, ALU.mult)
    tt(tmp, cs, h0i, ALU.mult)
    tt(ini, ini, tmp, ALU.add)
    cks = [cs] + [t64(f"ck{k}") for k in range(1, NK)]
    sks = [sn] + [t64(f"sk{k}") for k in range(1, NK)]
    for k in range(NK - 1):
        tt(cks[k + 1], cks[k], cks[k], ALU.mult)
        tt(tmp, sks[k], sks[k], ALU.mult)
        tt(cks[k + 1], cks[k + 1], tmp, ALU.subtract)
        tt(t2, cks[k], sks[k], ALU.mult)
        kb.op(V, lambda e, k=k: e.tensor_scalar(out=sks[k + 1][:], in0=t2[:], scalar1=2.0, scalar2=None, op0=ALU.mult), reads=[t2], writes=[sks[k + 1]])
    srcs = [mag, fr, fi, inr, ini] + cks + sks
    assert NP == len(srcs)
    I_MAG, I_FR, I_FI, I_INR, I_INI, I_CK, I_SK = 0, 1, 2, 3, 4, 5, 5 + NK
    I_NSK = NP
    I_NFI = NP + NK
    for a in range(0, NP, 8):
        b = self.bank()
        grp = srcs[a:a + 8]

        def trp(e, b=b, grp=grp):
            for q, sx in enumerate(grp):
                ins = e.transpose(out=P[b][:, q * 64:(q + 1) * 64], in_=sx[:], identity=self.ident[0:64, 0:64])
            return ins
        kb.op("pe", trp, reads=grp + [self.cst], writes=[PB[b]])
        kb.op(V, lambda e, b=b, a=a, n=len(grp): e.tensor_copy(out=PT[:, a:a + n, :], in_=P[b][:, 0:n * 64].rearrange("p (a c) -> p a c", c=64)), reads=[PB[b]], writes=[PT])
    kb.op(V, lambda e: e.tensor_scalar(out=PT[:, I_NSK:I_NSK + NK, :], in0=PT[:, I_SK:I_SK + NK, :], scalar1=-1.0, scalar2=None, op0=ALU.mult), reads=[PT], writes=[PT])
    kb.op(V, lambda e: e.tensor_scalar(out=PT[:, I_NFI, :], in0=PT[:, I_FI, :], scalar1=-1.0, scalar2=None, op0=ALU.mult), reads=[PT], writes=[PT])
    kb.barrier()
    kb.sb_reset(prep_mark)
    dcol = self.load_cols(self.i["s5_d"][i], 16, "dcol")
    Ball = [kb.tile([128, 64, 16], F32, "Ball") for _ in range(2)]
    Call = [kb.tile([128, 16, 64], F32, "Call") for _ in range(2)]
    kb.dma("sp", Ball[0][:], self.i["s5_b_re"][i].rearrange("j q h -> q j h"), writes=[Ball[0]])
    kb.dma("sp", Ball[1][:], self.i["s5_b_im"][i].rearrange("j q h -> q j h"), writes=[Ball[1]])
    kb.dma("sp", Call[0][:], self.i["s5_c_re"][i].rearrange("(kt r) p -> r kt p", r=128), writes=[Call[0]])
    kb.dma("sp", Call[1][:], self.i["s5_c_im"][i].rearrange("(kt r) p -> r kt p", r=128), writes=[Call[1]])
    rm = self.cst.t[:, C_RM:C_RM + 8]
    preB = [[kb.tile([128, 128], F32, "preB") for _ in range(2)] for _ in range(4)]
    for a in range(4):
        for c in range(2):
            kb.op(V, lambda e, a=a, c=c: e.memset(preB[a][c][:], 0.0), writes=[preB[a][c]])
    preC = [kb.tile([128, 128], F32, "preC") for _ in range(2)]
    tb16 = kb.tile([128, 16], F32, "tb16")
    LB = [kb.tile([128, 4, 2, 128], BF16, "LB") for _ in range(2)]
    LC = [kb.tile([128, 4, 2, 128], BF16, "LC") for _ in range(2)]
    Er = [kb.tile([128, T], F32, "Er") for _ in range(2)]
    Ei = [kb.tile([128, T], F32, "Ei") for _ in range(2)]
    et1 = kb.tile([128, 4096], F32, "et1")
    uf = kb.tile([128, T], F32, "uf")
    ub = kb.tile([128, T], BF16, "ub")
    qr = kb.tile([128, T], F32, "qr")
    qi = kb.tile([128, T], F32, "qi")
    magb = kb.tile([128, 512], F32, "magb")
    ones512 = kb.tile([128, 512], F32, "ones512")
    kb.op(V, lambda e: e.memset(ones512[:], 1.0), writes=[ones512])
    w1, w2, w3, w4, qri, qii, hrf, hif = [kb.tile([128, 512], F32, n) for n in ("w1", "w2", "w3", "w4", "qri", "qii", "hrf", "hif")]
    HR = [kb.tile([128, T], BF16, "HR") for _ in range(4)]
    HI = [kb.tile([128, T], BF16, "HI") for _ in range(4)]
    yf = [kb.tile([128, 512], F32, "yf") for _ in range(2)]
    ygt = [kb.tile([128, T], BF16, "ygt") for _ in range(2)]
    ST = kb.tile([128, 4, 64], F32, "ST")
    bcnt = [0]
    ccnt = [0]

    def sc(idx, j):
        return PT[:, idx, j:j + 1]

    for j in range(64):
        kt, jj = j // 4, j % 4
        lb_, lc_ = LB[kt % 2], LC[kt % 2]
        er, ei = Er[j % 2], Ei[j % 2]
        G = "pool"
        kb.op(G, lambda e, er=er: e.memset(er[:, 0:1], 1.0), writes=[er])
        kb.op(G, lambda e, ei=ei: e.memset(ei[:, 0:1], 0.0), writes=[ei])
        for k in range(NK):
            n = 1 << k
            kb.op(G, lambda e, er=er, n=n, k=k, j=j: e.tensor_scalar(out=et1[:, 0:n], in0=er[:, 0:n], scalar1=sc(I_CK + k, j), scalar2=None, op0=ALU.mult), reads=[er, PT], writes=[et1])
            kb.op(G, lambda e, ei=ei, n=n, k=k, j=j: e.tensor_scalar(out=et1[:, 1024:1024 + n], in0=ei[:, 0:n], scalar1=sc(I_SK + k, j), scalar2=None, op0=ALU.mult), reads=[ei, PT], writes=[et1])
            kb.op(G, lambda e, er=er, n=n, k=k, j=j: e.tensor_scalar(out=et1[:, 2048:2048 + n], in0=er[:, 0:n], scalar1=sc(I_SK + k, j), scalar2=None, op0=ALU.mult), reads=[er, PT], writes=[et1])
            kb.op(G, lambda e, ei=ei, n=n, k=k, j=j: e.tensor_scalar(out=et1[:, 3072:3072 + n], in0=ei[:, 0:n], scalar1=sc(I_CK + k, j), scalar2=None, op0=ALU.mult), reads=[ei, PT], writes=[et1])
            kb.op(G, lambda e, er=er, n=n: e.tensor_tensor(out=er[:, n:2 * n], in0=et1[:, 0:n], in1=et1[:, 1024:1024 + n], op=ALU.subtract), reads=[et1], writes=[er])
            kb.op(G, lambda e, ei=ei, n=n: e.tensor_tensor(out=ei[:, n:2 * n], in0=et1[:, 2048:2048 + n], in1=et1[:, 3072:3072 + n], op=ALU.add), reads=[et1], writes=[ei])
        kb.op(G, lambda e, er=er: e.tensor_copy(out=er[:, L:T], in_=er[:, 0:S]), reads=[er], writes=[er])
        kb.op(G, lambda e, ei=ei: e.tensor_copy(out=ei[:, L:T], in_=ei[:, 0:S]), reads=[ei], writes=[ei])
        pb_re, pb_im = preB[jj]
        c0b = ((2 * j) % 8) * 16
        for gl in range(2):
            ps_ = slice(gl * 64, (gl + 1) * 64)
            cc = slice(c0b + gl * 16, c0b + gl * 16 + 16)
            kb.op(V, lambda e, ps_=ps_, j=j: e.tensor_scalar(out=tb16[ps_, :], in0=Ball[0][ps_, j, :], scalar1=PT[ps_, I_FR, j:j + 1], scalar2=None, op0=ALU.mult), reads=[Ball[0], PT], writes=[tb16])
            kb.op(V, lambda e, ps_=ps_, cc=cc, j=j, pb_re=pb_re: _stt(e, pb_re[ps_, cc], Ball[1][ps_, j, :], PT[ps_, I_NFI, j:j + 1], tb16[ps_, :], ALU.mult, ALU.add), reads=[Ball[1], PT, tb16], writes=[pb_re])
            kb.op(V, lambda e, ps_=ps_, j=j: e.tensor_scalar(out=tb16[ps_, :], in0=Ball[1][ps_, j, :], scalar1=PT[ps_, I_FR, j:j + 1], scalar2=None, op0=ALU.mult), reads=[Ball[1], PT], writes=[tb16])
            kb.op(V, lambda e, ps_=ps_, cc=cc, j=j, pb_im=pb_im: _stt(e, pb_im[ps_, cc], Ball[0][ps_, j, :], PT[ps_, I_FI, j:j + 1], tb16[ps_, :], ALU.mult, ALU.add), reads=[Ball[0], PT, tb16], writes=[pb_im])
        for c in range(2):
            pc = preC[c]
            sgn = 1.0 if c == 0 else -1.0
            for gl in range(2):
                m = (2 * j + gl) % 8
                kb.op(V, lambda e, pc=pc, c=c, gl=gl, m=m, kt=kt, sgn=sgn: e.tensor_scalar(out=pc[:, gl * 64:(gl + 1) * 64], in0=Call[c][:, kt, :], scalar1=rm[:, m:m + 1], scalar2=sgn, op0=ALU.mult, op1=ALU.mult),
                      reads=[Call[c], self.cst], writes=[pc])
        b = 6 + (j % 2)

        def trl(e, b=b, pb_re=pb_re, pb_im=pb_im):
            e.transpose(out=P[b][:, 0:128], in_=pb_re[:], identity=self.ident)
            e.transpose(out=P[b][:, 128:256], in_=pb_im[:], identity=self.ident)
            e.transpose(out=P[b][:, 256:384], in_=preC[0][:], identity=self.ident)
            return e.transpose(out=P[b][:, 384:512], in_=preC[1][:], identity=self.ident)
        kb.op("pe", trl, reads=[pb_re, pb_im, preC[0], preC[1], self.cst], writes=[PB[b]])
        kb.op("act", lambda e, b=b, lb_=lb_, jj=jj: e.copy(out=lb_[:, jj, :, :], in_=P[b][:, 0:256].rearrange("p (a c) -> p a c", c=128)), reads=[PB[b]], writes=[lb_])
        kb.op("act", lambda e, b=b, lc_=lc_, jj=jj: e.copy(out=lc_[:, jj, :, :], in_=P[b][:, 256:512].rearrange("p (a c) -> p a c", c=128)), reads=[PB[b]], writes=[lc_])
        if jj == 0:
            kb.dma("sp", uf[:], self.Z[kt * 128:(kt + 1) * 128, :], writes=[uf])
            kb.op("act", lambda e: e.copy(out=ub[:], in_=uf[:]), reads=[uf], writes=[ub])
        kb.op(V, lambda e, j=j: e.tensor_scalar(out=magb[:], in0=ones512[:], scalar1=sc(I_MAG, j), scalar2=None, op0=ALU.mult), reads=[ones512, PT], writes=[magb])
        for si, (c0, c1) in enumerate(SUBS):
            n = c1 - c0
            br = 2 * (bcnt[0] % 3)
            bi = br + 1
            bcnt[0] += 1

            def bp(e, br=br, bi=bi, lb_=lb_, jj=jj, c0=c0, c1=c1, n=n):
                e.matmul(out=P[br][:, 0:n], lhsT=lb_[:, jj, 0, :], rhs=ub[:, c0:c1], start=True, stop=True)
                return e.matmul(out=P[bi][:, 0:n], lhsT=lb_[:, jj, 1, :], rhs=ub[:, c0:c1], start=True, stop=True)
            kb.op("pe", bp, reads=[lb_, ub], writes=[PB[br], PB[bi]])
            def m2(out, a, b_, op, rd, wr):
                kb.op(V, lambda e: e.tensor_tensor(out=out, in0=a, in1=b_, op=op), reads=rd, writes=wr)
            m2(w1[:, 0:n], P[br][:, 0:n], er[:, c0:c1], ALU.mult, [PB[br], er], [w1])
            m2(w2[:, 0:n], P[bi][:, 0:n], ei[:, c0:c1], ALU.mult, [PB[bi], ei], [w2])
            m2(qri[:, 0:n], w1[:, 0:n], w2[:, 0:n], ALU.add, [w1, w2], [qri])
            m2(w3[:, 0:n], P[bi][:, 0:n], er[:, c0:c1], ALU.mult, [PB[bi], er], [w3])
            m2(w4[:, 0:n], P[br][:, 0:n], ei[:, c0:c1], ALU.mult, [PB[br], ei], [w4])
            m2(qii[:, 0:n], w3[:, 0:n], w4[:, 0:n], ALU.subtract, [w3, w4], [qii])
            if si == 0:
                inr_, ini_ = 0.0, 0.0
            elif si == 4:
                inr_, ini_ = sc(I_INR, j), sc(I_INI, j)
            else:
                inr_, ini_ = qr[:, c0 - 1:c0], qi[:, c0 - 1:c0]
            kb.op(V, lambda e, c0=c0, c1=c1, n=n, inr_=inr_: e.tensor_tensor_scan(out=qr[:, c0:c1], data0=magb[:, 0:n], data1=qri[:, 0:n], initial=inr_, op0=ALU.mult, op1=ALU.add),
                  reads=[magb, qri, qr, PT], writes=[qr])
            kb.op(V, lambda e, c0=c0, c1=c1, n=n, ini_=ini_: e.tensor_tensor_scan(out=qi[:, c0:c1], data0=magb[:, 0:n], data1=qii[:, 0:n], initial=ini_, op0=ALU.mult, op1=ALU.add),
                  reads=[magb, qii, qi, PT], writes=[qi])
            m2(w1[:, 0:n], qr[:, c0:c1], er[:, c0:c1], ALU.mult, [qr, er], [w1])
            m2(w2[:, 0:n], qi[:, c0:c1], ei[:, c0:c1], ALU.mult, [qi, ei], [w2])
            m2(hrf[:, 0:n], w1[:, 0:n], w2[:, 0:n], ALU.subtract, [w1, w2], [hrf])
            m2(w3[:, 0:n], qr[:, c0:c1], ei[:, c0:c1], ALU.mult, [qr, ei], [w3])
            m2(w4[:, 0:n], qi[:, c0:c1], er[:, c0:c1], ALU.mult, [qi, er], [w4])
            m2(hif[:, 0:n], w3[:, 0:n], w4[:, 0:n], ALU.add, [w3, w4], [hif])
            kb.op("act", lambda e, jj=jj, c0=c0, c1=c1, n=n: e.copy(out=HR[jj][:, c0:c1], in_=hrf[:, 0:n]), reads=[hrf], writes=[HR[jj]])
            kb.op("act", lambda e, jj=jj, c0=c0, c1=c1, n=n: e.copy(out=HI[jj][:, c0:c1], in_=hif[:, 0:n]), reads=[hif], writes=[HI[jj]])
            if si in (3, 4):
                a = 0 if si == 3 else 2
                kb.op("act", lambda e, a=a, j=j, n=n: e.copy(out=ST[:, a, j:j + 1], in_=hrf[:, n - 1:n]), reads=[hrf], writes=[ST])
                kb.op("act", lambda e, a=a, j=j, n=n: e.copy(out=ST[:, a + 1, j:j + 1], in_=hif[:, n - 1:n]), reads=[hif], writes=[ST])
        if jj == 3:
            yg_ = ygt[kt % 2]
            for si, (c0, c1) in enumerate(SUBS):
                n = c1 - c0
                b = 6 + (ccnt[0] % 2)
                y_ = yf[ccnt[0] % 2]
                ccnt[0] += 1

                def cp(e, b=b, lc_=lc_, c0=c0, c1=c1, n=n):
                    for a in range(4):
                        e.matmul(out=P[b][:, 0:n], lhsT=lc_[:, a, 0, :], rhs=HR[a][:, c0:c1], start=(a == 0), stop=False)
                        ins = e.matmul(out=P[b][:, 0:n], lhsT=lc_[:, a, 1, :], rhs=HI[a][:, c0:c1], start=False, stop=(a == 3))
                    return ins
                kb.op("pe", cp, reads=[lc_] + HR + HI, writes=[PB[b]])
                kb.op(V, lambda e, b=b, y_=y_, c0=c0, c1=c1, n=n, kt=kt: _stt(e, y_[:, 0:n], uf[:, c0:c1], dcol[:, kt:kt + 1], P[b][:, 0:n], ALU.mult, ALU.add), reads=[uf, dcol, PB[b]], writes=[y_])
                kb.op("act", lambda e, y_=y_, yg_=yg_, c0=c0, c1=c1, n=n: e.activation(out=yg_[:, c0:c1], in_=y_[:, 0:n], func=AF.Gelu), reads=[y_], writes=[yg_])
            kb.dma("sp", self.YG[kt * 128:(kt + 1) * 128, :], yg_[:], reads=[yg_])
    b = 0

    def trs(e):
        for a in range(4):
            ins = e.transpose(out=P[b][0:64, a * 128:(a + 1) * 128], in_=ST[:, a, :], identity=self.ident)
        return ins
    kb.op("pe", trs, reads=[ST, self.cst], writes=[PB[b]])
    sto = kb.tile([64, 512], F32, "sto")
    kb.op(V, lambda e: e.tensor_copy(out=sto[:], in_=P[b][0:64, :]), reads=[PB[b]], writes=[sto])
    for a, nm in enumerate(("s5_re_prompt", "s5_im_prompt", "s5_re_sample", "s5_im_sample")):
        kb.dma("sp", self.o[nm][i], sto[:, a * 128:(a + 1) * 128], reads=[sto])
    kb.bank_rr = 0
    kb.phase_end()
    bg = self.load_cols(self.i["s5_b_glu"][i], 16, "bg")
    YGt = kb.tile([128, 16, T], BF16, "YGt")
    kb.dma("sp", YGt[:], self.YG.rearrange("(kt p) t -> p kt t", p=128), writes=[YGt])
    wv = self.i["s5_w_glu"][i].rearrange("(kt p) n -> p kt n", p=128)
    sg = [kb.tile([128, 512], F32, "sg") for _ in range(2)]
    mo = [kb.tile([128, T], BF16, "mo") for _ in range(2)]
    gc = [0]

    def epi(n, si, c0, c1, ps, pb):
        s_ = sg[gc[0] % 2]
        gc[0] += 1
        m_ = mo[n % 2]
        kb.op("act", lambda e: e.activation(out=s_[:, 0:c1 - c0], in_=ps, func=AF.Sigmoid, bias=bg[:, n:n + 1], scale=1.0), reads=[pb, bg], writes=[s_])
        kb.op(V, lambda e: e.tensor_tensor(out=m_[:, c0:c1], in0=s_[:, 0:c1 - c0], in1=YGt[:, n, c0:c1], op=ALU.mult), reads=[s_, YGt], writes=[m_])
        if si == len(SUBS) - 1:
            kb.dma("sp", self.MIX[n * 128:(n + 1) * 128, :], m_[:], reads=[m_])
    self.gemm(lambda n: wv[:, :, n * 128:(n + 1) * 128], 16, 16, lambda kt, c0, c1: YGt[:, kt, c0:c1], [YGt], epi)
    kb.phase_end()


def phase_pool(self, i):
    kb = self.kb
    P = self.psum
    PB = self.psb
    V = "dve"
    pscale = self.load_cols(self.i["pool_scale"][i], 16, "pscale")
    ZP = kb.tile([128, 16, T], BF16, "ZP")
    spin = kb.tile([15, 2048], F32, "spin")
    kb.dma("sp", spin[:], self.i["state_pool"][i], writes=[spin])
    kb.dma("sp", self.o["pool_sample"][i][0:7, :], self.i["state_pool"][i][8:15, :])
    pp = kb.tile([15, 2048], F32, "pp")
    psm = kb.tile([S, 2048], F32, "psm")
    EA = [kb.tile([128, 15 + L], F32, "EA") for _ in range(2)]
    EB = [kb.tile([128, 15 + L], F32, "EB") for _ in range(2)]
    ES = [kb.tile([128, 15 + S], F32, "ES") for _ in range(3)]
    zpf = kb.tile([128, L], F32, "zpf")
    t15 = kb.tile([128, 15], F32, "t15")
    for a in range(2):
        kb.op(V, lambda e, a=a: e.memset(EA[a][:, 0:15], 0.0), writes=[EA[a]])
    for ft in range(16):
        g = ft // 4
        w = (2, 4, 8, 16)[g]
        E0 = EA[ft % 2]
        kb.dma("sp", E0[:, 15:15 + L], self.Z[2048 + ft * 128:2048 + (ft + 1) * 128, 0:L], writes=[E0])
        Es0 = ES[0]
        kb.dma("sp", Es0[:, 15:15 + S], self.Z[2048 + ft * 128:2048 + (ft + 1) * 128, L:T], writes=[Es0])
        b = ft % 4
        kb.op("pe", lambda e, b=b, ft=ft: e.transpose(out=P[b][:, 0:15], in_=spin[:, ft * 128:(ft + 1) * 128], identity=self.ident[0:15, 0:15]), reads=[spin, self.cst], writes=[PB[b]])
        kb.op(V, lambda e, b=b: e.tensor_copy(out=Es0[:, 0:15], in_=P[b][:, 0:15]), reads=[PB[b]], writes=[Es0])
        A, Bt = E0, EB[0]
        As, Bs = Es0, ES[1]
        nb = 0
        step = 1
        while step < w:
            dst = EB[nb % 2]
            dsts = ES[1 + nb % 2]
            nb += 1
            kb.op(V, lambda e, A=A, dst=dst, step=step: e.tensor_tensor(out=dst[:, step:], in0=A[:, step:], in1=A[:, 0:15 + L - step], op=ALU.add), reads=[A], writes=[dst])
            kb.op(V, lambda e, As=As, dsts=dsts, step=step: e.tensor_tensor(out=dsts[:, step:], in0=As[:, step:], in1=As[:, 0:15 + S - step], op=ALU.add), reads=[As], writes=[dsts])
            A, As = dst, dsts
            step *= 2
        kb.op(V, lambda e, A=A, E0=E0, w=w: _stt(e, zpf[:], A[:, 15:15 + L], 1.0 / w, E0[:, 15:15 + L], ALU.mult, ALU.subtract), reads=[A, E0], writes=[zpf])
        kb.op(V, lambda e, A=A, g=g: e.tensor_tensor(out=t15[:], in0=A[:, 15:30], in1=self.cst.t[:, C_FIX + g * 15:C_FIX + (g + 1) * 15], op=ALU.mult), reads=[A, self.cst], writes=[t15])
        kb.op(V, lambda e: e.tensor_tensor(out=zpf[:, 0:15], in0=zpf[:, 0:15], in1=t15[:], op=ALU.add), reads=[zpf, t15], writes=[zpf])
        kb.op("act", lambda e, ft=ft: e.copy(out=ZP[:, ft, 0:L], in_=zpf[:]), reads=[zpf], writes=[ZP])
        kb.op(V, lambda e, As=As, Es0=Es0, w=w, ft=ft: _stt(e, ZP[:, ft, L:T], As[:, 15:15 + S], 1.0 / w, Es0[:, 15:15 + S], ALU.mult, ALU.subtract), reads=[As, Es0], writes=[ZP])
        b2 = 4 + ft % 4

        def tro(e, b2=b2, E0=E0, Es0=Es0):
            e.transpose(out=P[b2][0:15, 0:128], in_=E0[:, L:L + 15], identity=self.ident)
            return e.transpose(out=P[b2][0:S, 128:256], in_=Es0[:, 15:15 + S], identity=self.ident)
        kb.op("pe", tro, reads=[E0, Es0, self.cst], writes=[PB[b2]])
        kb.op("act", lambda e, b2=b2, ft=ft: e.copy(out=pp[:, ft * 128:(ft + 1) * 128], in_=P[b2][0:15, 0:128]), reads=[PB[b2]], writes=[pp])
        kb.op("act", lambda e, b2=b2, ft=ft: e.copy(out=psm[:, ft * 128:(ft + 1) * 128], in_=P[b2][0:S, 128:256]), reads=[PB[b2]], writes=[psm])
    kb.dma("sp", self.o["pool_prompt"][i], pp[:], reads=[pp])
    kb.dma("sp", self.o["pool_sample"][i][7:15, :], psm[:], reads=[psm])
    kb.barrier()
    kb.bank_rr = 0
    mo = [kb.tile([128, T], BF16, "pmo") for _ in range(2)]
    for g in range(4):
        wv = self.i["pool_w"][i, g].rearrange("(kt p) n -> p kt n", p=128)

        def epi(n, si, c0, c1, ps, pb, g=g):
            m_ = mo[n % 2]
            f = g * 4 + n
            kb.op(V, lambda e: e.tensor_scalar(out=m_[:, c0:c1], in0=ps, scalar1=pscale[:, f:f + 1], scalar2=None, op0=ALU.mult), reads=[pb, pscale], writes=[m_])
            if si == len(SUBS) - 1:
                kb.dma("sp", self.MIX[2048 + f * 128:2048 + (f + 1) * 128, :], m_[:], reads=[m_])
        self.gemm(lambda n, wv=wv: wv[:, :, n * 128:(n + 1) * 128], 4, 4, lambda kt, c0, c1, g=g: ZP[:, g * 4 + kt, c0:c1], [ZP], epi, nw=2, wname=f"pw{g}")
    kb.phase_end()


Prog.phase_s5 = phase_s5
Prog.phase_pool = phase_pool


def build_all(self):
    self.phase_in()
    for l in range(self.depth):
        i = l // 2
        if l % 2 == 0:
            self.phase_win(self.i["norm_mix"][l], self.i["ev_w_in"][i], 4096)
            self.phase_s5(i)
            self.phase_pool(i)
            self.phase_wout(self.i["ev_w_out"][i])
        else:
            self.phase_win(self.i["norm_mix"][l], self.i["od_w_in"][i], 10240)
            self.phase_attn(i)
            self.phase_sgu(i)
            self.phase_wout(self.i["od_w_out"][i])
        self.phase_ffn(l)
    self.phase_out()
    self.kb.emit()


Prog.build_all = build_all


_CONSTS = None


def core_inputs(inp, c):
    global _CONSTS
    if _CONSTS is None:
        _CONSTS = make_consts()
    bp = c % 4
    f = lambda a: np.ascontiguousarray(a, dtype=np.float32)
    m = dict(
        x_prompt=f(inp["x_prompt"][bp]), x_sample=f(inp["x_sample"][c]),
        state_s5_re=f(inp["state_s5_re"][:, c]).reshape(2, 64, 128), state_s5_im=f(inp["state_s5_im"][:, c]).reshape(2, 64, 128),
        state_pool=f(inp["state_pool"][:, c]),
        cache_k=f(inp["cache_k"][:, c]), cache_v=f(inp["cache_v"][:, c]),
        s5_lambda_re=f(inp["s5_lambda_re"]).reshape(2, 64, 128), s5_lambda_im=f(inp["s5_lambda_im"]).reshape(2, 64, 128),
        s5_log_dt=f(inp["s5_log_dt"]).reshape(2, 64, 2),
        s5_b_re=f(inp["s5_b_re"]).reshape(2, 64, 128, 16), s5_b_im=f(inp["s5_b_im"]).reshape(2, 64, 128, 16),
        s5_c_re=f(inp["s5_c_re"]).reshape(2, 2048, 64), s5_c_im=f(inp["s5_c_im"]).reshape(2, 2048, 64),
        consts=_CONSTS,
    )
    for k in ("norm_mix", "norm_ffn", "ev_w_in", "ev_w_out", "s5_d", "s5_w_glu", "s5_b_glu", "pool_w", "pool_scale",
              "od_w_in", "od_w_out", "q_norm", "k_norm", "sgu_ln_g", "sgu_ln_b", "sgu_w", "sgu_b", "ffn_w1", "ffn_w3", "ffn_w2"):
        m[k] = f(inp[k])
    return m


def kernel(**inputs):
    n = 8
    prog = Prog(depth=4)
    prog.build_all()
    in_maps = [core_inputs(inputs, c) for c in range(n)]
    res = run_bass_kernel_spmd(prog.nc, in_maps, core_ids=list(range(n)))
    r = res.results

    def stack(name, cores, shape=None):
        a = np.stack([np.asarray(r[c][name], dtype=np.float32) for c in cores], axis=1)
        return a if shape is None else a.reshape(shape)

    pc = list(range(4))
    sc_ = list(range(8))
    return (
        stack("y_prompt", pc)[0] if False else np.stack([r[c]["y_prompt"] for c in pc], axis=0).astype(np.float32),
        np.stack([r[c]["y_sample"] for c in sc_], axis=0).astype(np.float32),
        stack("s5_re_prompt", pc, (2, 4, 128, 64)),
        stack("s5_im_prompt", pc, (2, 4, 128, 64)),
        stack("pool_prompt", pc),
        stack("k_prompt", pc),
        stack("v_prompt", pc),
        stack("s5_re_sample", sc_, (2, 8, 128, 64)),
        stack("s5_im_sample", sc_, (2, 8, 128, 64)),
        stack("pool_sample", sc_),
        stack("k_sample", sc_),
        stack("v_sample", sc_),
        stack("sgu_v_sample", sc_),
    )
```
